# Optimizing a Trainium2 kernel written in Bass

```python
import math
import jax, jax.numpy as jnp
from jax import lax
import numpy as np

D_MODEL = 1024
BATCH = 4
SEQ = 8192
DEPTH = 4

N_MIXERS = 2
RMS_EPS = 1e-6
DN_HEAD_DIM = 128
DN_HEADS = D_MODEL // DN_HEAD_DIM
DN_WIDTH = DN_HEADS * DN_HEAD_DIM
DN_CONV = 4
DN_CHUNK = 64
DN_IN = 4 * DN_WIDTH + 2 * DN_HEADS
SWA_HEAD_DIM = 64
SWA_HEADS = D_MODEL // SWA_HEAD_DIM
SWA_KV_HEADS = SWA_HEADS // 8
SWA_GROUP = SWA_HEADS // SWA_KV_HEADS
SWA_QKV = (SWA_HEADS + 2 * SWA_KV_HEADS) * SWA_HEAD_DIM
WINDOW = 128
SWA_BLOCK = WINDOW
NUM_BUCKETS = 32
MAX_DISTANCE = 128
D_FF = 4 * D_MODEL
N_DN_LAYERS = (DEPTH + N_MIXERS - 1) // N_MIXERS
N_SWA_LAYERS = DEPTH // N_MIXERS

kernel_name = 'hybrid_gdn_swa_sink_t5_sqrelu'


def rms_norm(x, w):
    xf = x.astype(jnp.float32)
    y = xf * lax.rsqrt(jnp.mean(xf * xf, axis=-1, keepdims=True) + RMS_EPS)
    return (y * w.astype(jnp.float32)).astype(x.dtype)


def l2_normalize(t):
    return t * lax.rsqrt(jnp.sum(t * t, axis=-1, keepdims=True) + RMS_EPS)


def causal_depthwise_conv(x, w):
    k = w.shape[0]
    return lax.conv_general_dilated(
        x, w[:, None, :].astype(x.dtype), window_strides=(1,), padding=[(k - 1, 0)],
        dimension_numbers=('NWC', 'WIO', 'NWC'), feature_group_count=x.shape[-1])


def chunk_gated_delta_rule(q, k, v, beta, g):
    B, S, H, DK = q.shape
    DV = v.shape[-1]
    C = DN_CHUNK
    N = S // C

    def to_chunks(t):
        return t.reshape(B, N, C, H, -1).transpose(0, 3, 1, 2, 4)

    q = to_chunks(q) * (DK ** -0.5)
    k = to_chunks(k)
    v = to_chunks(v)
    beta = beta.reshape(B, N, C, H).transpose(0, 3, 1, 2)
    gc = jnp.cumsum(g.reshape(B, N, C, H).transpose(0, 3, 1, 2), axis=-1)
    causal = jnp.tril(jnp.ones((C, C), dtype=bool))
    strict = jnp.tril(jnp.ones((C, C), dtype=bool), -1)
    decay = jnp.exp(jnp.where(causal, gc[..., :, None] - gc[..., None, :], -jnp.inf))
    k_beta = k * beta[..., None]
    lower = jnp.where(strict, jnp.einsum('bhnid,bhnjd->bhnij', k_beta, k) * decay, 0.0)
    eye = jnp.eye(C, dtype=jnp.float32)
    rhs = jnp.concatenate([v * beta[..., None], k_beta * jnp.exp(gc)[..., None]], axis=-1)
    sol = lax.linalg.triangular_solve(lower + eye, rhs, left_side=True, lower=True, unit_diagonal=True)
    u, w = sol[..., :DV], sol[..., DV:]
    attn = jnp.einsum('bhnid,bhnjd->bhnij', q, k) * decay
    q_dec = q * jnp.exp(gc)[..., None]
    k_dec = k * jnp.exp(gc[..., -1:] - gc)[..., None]
    g_last = jnp.exp(gc[..., -1])
    xs = tuple(jnp.moveaxis(t, 2, 0) for t in (q_dec, k_dec, u, w, attn, g_last))

    def step(state, inp):
        q_c, k_c, u_c, w_c, a_c, gl = inp
        v_new = u_c - jnp.einsum('bhck,bhkv->bhcv', w_c, state)
        o_c = jnp.einsum('bhck,bhkv->bhcv', q_c, state) + jnp.einsum('bhij,bhjv->bhiv', a_c, v_new)
        state = state * gl[..., None, None] + jnp.einsum('bhck,bhcv->bhkv', k_c, v_new)
        return state, o_c

    s0 = jnp.zeros((B, H, DK, DV), jnp.float32)
    _, o = lax.scan(step, s0, xs)
    return o.transpose(1, 0, 3, 2, 4).reshape(B, S, H, DV)


def gated_deltanet(h, w_in, conv_w, a_log, dt_bias, norm_w, w_out):
    B, S, _ = h.shape
    proj = h @ w_in
    qkv = jax.nn.silu(causal_depthwise_conv(proj[..., :3 * DN_WIDTH], conv_w))
    z = proj[..., 3 * DN_WIDTH:4 * DN_WIDTH]
    b = proj[..., 4 * DN_WIDTH:4 * DN_WIDTH + DN_HEADS]
    a = proj[..., 4 * DN_WIDTH + DN_HEADS:]
    q, k, v = jnp.split(qkv.astype(jnp.float32), 3, axis=-1)
    q = l2_normalize(q.reshape(B, S, DN_HEADS, DN_HEAD_DIM))
    k = l2_normalize(k.reshape(B, S, DN_HEADS, DN_HEAD_DIM))
    v = v.reshape(B, S, DN_HEADS, DN_HEAD_DIM)
    beta = jax.nn.sigmoid(b.astype(jnp.float32))
    g = -jnp.exp(a_log.astype(jnp.float32)) * jax.nn.softplus(a.astype(jnp.float32) + dt_bias.astype(jnp.float32))
    o = chunk_gated_delta_rule(q, k, v, beta, g)
    zf = z.reshape(B, S, DN_HEADS, DN_HEAD_DIM).astype(jnp.float32)
    o = o * lax.rsqrt(jnp.mean(o * o, axis=-1, keepdims=True) + RMS_EPS) * norm_w.astype(jnp.float32) * jax.nn.silu(zf)
    return o.reshape(B, S, DN_WIDTH).astype(h.dtype) @ w_out


def t5_causal_bucket(dist):
    n = jnp.maximum(dist, 0)
    max_exact = NUM_BUCKETS // 2
    large = max_exact + (jnp.log(jnp.maximum(n, 1).astype(jnp.float32) / max_exact)
                         / math.log(MAX_DISTANCE / max_exact) * (NUM_BUCKETS - max_exact)).astype(jnp.int32)
    large = jnp.minimum(large, NUM_BUCKETS - 1)
    return jnp.where(n < max_exact, n, large)


def sliding_window_sink_attention(h, w_qkv, b_qkv, sinks, w_out, b_out, rel_bias):
    B, S, _ = h.shape
    NB = S // SWA_BLOCK
    qkv = h @ w_qkv + b_qkv
    q = qkv[..., :SWA_HEADS * SWA_HEAD_DIM].reshape(B, NB, SWA_BLOCK, SWA_KV_HEADS, SWA_GROUP, SWA_HEAD_DIM)
    k = qkv[..., SWA_HEADS * SWA_HEAD_DIM:(SWA_HEADS + SWA_KV_HEADS) * SWA_HEAD_DIM]
    v = qkv[..., (SWA_HEADS + SWA_KV_HEADS) * SWA_HEAD_DIM:]
    k = k.reshape(B, NB, SWA_BLOCK, SWA_KV_HEADS, SWA_HEAD_DIM)
    v = v.reshape(B, NB, SWA_BLOCK, SWA_KV_HEADS, SWA_HEAD_DIM)
    pad = ((0, 0), (1, 0), (0, 0), (0, 0), (0, 0))
    kk = jnp.concatenate([jnp.pad(k[:, :-1], pad), k], axis=2)
    vv = jnp.concatenate([jnp.pad(v[:, :-1], pad), v], axis=2)
    logits = jnp.einsum('bnqhgd,bnkhd->bnhgqk', q, kk).astype(jnp.float32) * (SWA_HEAD_DIM ** -0.5)
    qi = jnp.arange(SWA_BLOCK)[:, None]
    kj = jnp.arange(2 * SWA_BLOCK)[None, :]
    dist = qi + SWA_BLOCK - kj
    bias = rel_bias[t5_causal_bucket(dist)].astype(jnp.float32)
    bias = bias.transpose(2, 0, 1).reshape(SWA_KV_HEADS, SWA_GROUP, SWA_BLOCK, 2 * SWA_BLOCK)
    key_pos = jnp.arange(NB)[:, None] * SWA_BLOCK - SWA_BLOCK + kj
    valid = ((dist >= 0) & (dist < WINDOW))[None] & (key_pos >= 0)[:, None, :]
    logits = jnp.where(valid[None, :, None, None], logits + bias, -jnp.inf)
    sink = sinks.astype(jnp.float32).reshape(SWA_KV_HEADS, SWA_GROUP)[None, None, :, :, None, None]
    m = jnp.maximum(jnp.max(logits, axis=-1, keepdims=True), sink)
    p = jnp.exp(logits - m)
    p = p / (jnp.sum(p, axis=-1, keepdims=True) + jnp.exp(sink - m))
    out = jnp.einsum('bnhgqk,bnkhd->bnqhgd', p.astype(vv.dtype), vv)
    out = out.reshape(B, S, SWA_HEADS * SWA_HEAD_DIM)
    return out @ w_out + b_out


def squared_relu_mlp(h, w_up, w_down):
    return jnp.square(jax.nn.relu(h @ w_up)) @ w_down


def setup_inputs(seed: int = 0) -> dict:
    key = jax.random.key(seed)
    ks = jax.random.split(key, 18)
    f32 = jnp.float32

    def nrm(k, shape, scale):
        return jax.random.normal(k, shape, f32) * scale

    x = nrm(ks[0], (BATCH, SEQ, D_MODEL), 1.0)
    norm_mix = 1.0 + nrm(ks[1], (DEPTH, D_MODEL), 0.02)
    norm_mlp = 1.0 + nrm(ks[2], (DEPTH, D_MODEL), 0.02)
    norm_final = 1.0 + nrm(ks[3], (D_MODEL,), 0.02)
    dn_w_in = nrm(ks[4], (N_DN_LAYERS, D_MODEL, DN_IN), D_MODEL ** -0.5)
    dn_conv_w = nrm(ks[5], (N_DN_LAYERS, DN_CONV, 3 * DN_WIDTH), DN_CONV ** -0.5)
    dn_a_log = jnp.log(jax.random.uniform(ks[6], (N_DN_LAYERS, DN_HEADS), f32, 1.0, 16.0))
    dt = jnp.exp(jax.random.uniform(ks[7], (N_DN_LAYERS, DN_HEADS), f32, math.log(1e-3), math.log(1e-1)))
    dn_dt_bias = dt + jnp.log(-jnp.expm1(-dt))
    dn_norm_w = 1.0 + nrm(ks[8], (N_DN_LAYERS, DN_HEAD_DIM), 0.02)
    dn_w_out = nrm(ks[9], (N_DN_LAYERS, DN_WIDTH, D_MODEL), DN_WIDTH ** -0.5)
    swa_w_qkv = nrm(ks[10], (N_SWA_LAYERS, D_MODEL, SWA_QKV), D_MODEL ** -0.5)
    swa_b_qkv = nrm(ks[11], (N_SWA_LAYERS, SWA_QKV), 0.02)
    swa_sinks = nrm(ks[12], (N_SWA_LAYERS, SWA_HEADS), 0.5)
    swa_w_out = nrm(ks[13], (N_SWA_LAYERS, SWA_HEADS * SWA_HEAD_DIM, D_MODEL), (SWA_HEADS * SWA_HEAD_DIM) ** -0.5)
    swa_b_out = nrm(ks[14], (N_SWA_LAYERS, D_MODEL), 0.02)
    rel_bias = nrm(ks[15], (NUM_BUCKETS, SWA_HEADS), 0.5)
    mlp_w_up = nrm(ks[16], (DEPTH, D_MODEL, D_FF), D_MODEL ** -0.5)
    mlp_w_down = nrm(ks[17], (DEPTH, D_FF, D_MODEL), D_FF ** -0.5)
    return {'x': x, 'norm_mix': norm_mix, 'norm_mlp': norm_mlp, 'norm_final': norm_final,
            'dn_w_in': dn_w_in, 'dn_conv_w': dn_conv_w, 'dn_a_log': dn_a_log, 'dn_dt_bias': dn_dt_bias,
            'dn_norm_w': dn_norm_w, 'dn_w_out': dn_w_out,
            'swa_w_qkv': swa_w_qkv, 'swa_b_qkv': swa_b_qkv, 'swa_sinks': swa_sinks,
            'swa_w_out': swa_w_out, 'swa_b_out': swa_b_out, 'rel_bias': rel_bias,
            'mlp_w_up': mlp_w_up, 'mlp_w_down': mlp_w_down}


def reference(x, norm_mix, norm_mlp, norm_final, dn_w_in, dn_conv_w, dn_a_log, dn_dt_bias, dn_norm_w,
              dn_w_out, swa_w_qkv, swa_b_qkv, swa_sinks, swa_w_out, swa_b_out, rel_bias,
              mlp_w_up, mlp_w_down):
    for i in range(DEPTH):
        h = rms_norm(x, norm_mix[i])
        j = i // N_MIXERS
        if i % N_MIXERS == 0:
            y = gated_deltanet(h, dn_w_in[j], dn_conv_w[j], dn_a_log[j], dn_dt_bias[j], dn_norm_w[j], dn_w_out[j])
        else:
            y = sliding_window_sink_attention(h, swa_w_qkv[j], swa_b_qkv[j], swa_sinks[j], swa_w_out[j],
                                              swa_b_out[j], rel_bias)
        x = x + y
        x = x + squared_relu_mlp(rms_norm(x, norm_mlp[i]), mlp_w_up[i], mlp_w_down[i])
    return rms_norm(x, norm_final)
```

```python
import math
import numpy as np
from contextlib import ExitStack
import concourse.bass as bass
import concourse.mybir as mybir
from concourse.bass_utils import run_bass_kernel_spmd

F32 = mybir.dt.float32
F32R = mybir.dt.float32r
BF16 = mybir.dt.bfloat16
AF = mybir.ActivationFunctionType
ALU = mybir.AluOpType
AX = mybir.AxisListType

D = 1024
T = 512
NSLOT = 4
EPS = 1e-6
NEG = -1.0e30
SEM_LIM = 16000


class Sched:
    def __init__(self):
        self.ops = []

    def add(self, eng, fn, reads=(), writes=(), dma=None, reg=True):
        def isps(k):
            return isinstance(k, tuple) and len(k) > 1 and k[0] in ('pq', 'pb')
        banks = []
        for k in tuple(reads) + tuple(writes):
            if isps(k) and ('pb', k[1]) not in banks:
                banks.append(('pb', k[1]))
        reads = tuple(k for k in reads if not isps(k))
        writes = tuple(k for k in writes if not isps(k)) + tuple(banks)
        self.ops.append(dict(eng=eng, fn=fn, r=tuple(reads), w=tuple(writes), dma=dma, reg=reg))

    def barrier(self, keys):
        keys = tuple(keys)
        for eng in ('pe', 'act', 'dve', 'pool', 'sp'):
            self.add(eng, lambda e: None, reads=(), writes=keys, reg=False)

    def emit(self, nc, stack):
        ops = self.ops
        last_w = {}
        readers = {}
        deps = []
        for i, o in enumerate(ops):
            d = set()
            for k in o['r']:
                if k in last_w:
                    d.add(last_w[k])
            for k in o['w']:
                if k in last_w:
                    d.add(last_w[k])
                d.update(readers.get(k, ()))
            d.discard(i)
            dd = []
            rset = set(o['r'])
            for j in d:
                p = ops[j]
                if o['reg'] and p['dma'] is None and o['dma'] is None and p['eng'] == o['eng']:
                    if o['eng'] == 'pe':
                        continue
                dd.append(j)
            deps.append(dd)
            if o['reg']:
                for k in o['r']:
                    readers.setdefault(k, []).append(i)
                for k in o['w']:
                    last_w[k] = i
                    readers[k] = []
        signaled = set()
        for dd in deps:
            signaled.update(dd)
        sems = {}
        counts = {}
        ncnt = {}
        token = [None] * len(ops)
        for i, o in enumerate(ops):
            if o['dma'] is not None:
                nm = 'd_' + o['dma']
                counts[nm] = counts.get(nm, 0) + 16
                token[i] = (nm, counts[nm])
            elif i in signaled:
                base = 'e_' + o['eng']
                ncnt[base] = ncnt.get(base, 0) + 1
                gen = (ncnt[base] - 1) // SEM_LIM
                nm = '%s_%d' % (base, gen)
                counts[nm] = (ncnt[base] - 1) % SEM_LIM + 1
                token[i] = (nm, counts[nm])
        self.counts = counts
        per_eng = {}
        for i, o in enumerate(ops):
            per_eng.setdefault(o['eng'], []).append(i)
        for nm in counts:
            sems[nm] = stack.enter_context(nc.semaphore(nm))
        block = stack.enter_context(nc.Block())
        binder = dict(pe=block.tensor, act=block.scalar, dve=block.vector,
                      pool=block.gpsimd, sp=block.sync)
        nw = [0]

        def make(idxs):
            def body(e):
                seen = {}
                for i in idxs:
                    o = ops[i]
                    need = {}
                    for j in deps[i]:
                        nm, v = token[j]
                        if seen.get(nm, 0) >= v:
                            continue
                        if need.get(nm, 0) < v:
                            need[nm] = v
                    for nm, v in need.items():
                        e.wait_ge(sems[nm], v)
                        seen[nm] = v
                        nw[0] += 1
                    inst = o['fn'](e)
                    if token[i] is not None:
                        nm, v = token[i]
                        inst.then_inc(sems[nm], 16 if o['dma'] is not None else 1)
            return body

        for engname, idxs in per_eng.items():
            binder[engname](make(idxs))
        self.nwaits = nw[0]


def run_interleaved(gens):
    gens = list(gens)
    while gens:
        for g in list(gens):
            try:
                next(g)
            except StopIteration:
                gens.remove(g)


def t5_bucket(n):
    if n < 16:
        return n
    v = 16 + int(np.float32(np.log(np.float32(n) / np.float32(16.0))) / np.float32(math.log(8.0)) * np.float32(16.0))
    return min(v, 31)


def host_constants():
    c = {}
    c['c_ident'] = np.eye(128, dtype=np.float32)
    ii = np.arange(128)
    c['c_U'] = (ii[:, None] <= ii[None, :]).astype(np.float32)
    c['c_neglo'] = np.where(ii[:, None] > ii[None, :], 0.0, NEG).astype(np.float32)
    c['c_negup'] = np.where(ii[:, None] <= ii[None, :], 0.0, NEG).astype(np.float32)
    c['c_J'] = np.eye(128, dtype=np.float32)[::-1].copy()
    mk = np.zeros((4, 128, 128), np.float32)
    mk[0] = (ii[:, None] // 16 == ii[None, :] // 16)
    for l, s_ in enumerate((16, 32, 64)):
        mk[l + 1] = (ii[:, None] // (2 * s_) == ii[None, :] // (2 * s_)) & (ii[:, None] // s_ != ii[None, :] // s_)
    c['c_masks'] = np.ascontiguousarray(mk.transpose(1, 0, 2)).reshape(128, 512)
    oh = np.zeros((32, 384), np.float32)
    ng = np.full((16, 384), NEG, np.float32)
    for m in range(384):
        dist = 255 - m
        if 0 <= dist <= 127:
            oh[t5_bucket(dist), m] = 1.0
            ng[:, m] = 0.0
    c['c_onehot'] = oh
    c['c_negrow'] = ng
    return c


def build_program(S, n_layers=4, dn=True, swa=True):
    NT = S // T
    nc = bass.Bass("TRN2", target_bir_lowering=False)

    def din(name, shape):
        return nc.dram_tensor(name, list(shape), F32, kind="ExternalInput")

    x_d = din("x", [S, D])
    nmix_d = din("norm_mix", [4, D]); nmlp_d = din("norm_mlp", [4, D]); nfin_d = din("norm_final", [1, D])
    dnin_d = din("dn_w_in", [2, D, 4112]); dncw_d = din("dn_conv_w", [2, 4, 3072])
    alog_d = din("dn_a_log", [2, 8]); dtb_d = din("dn_dt_bias", [2, 8]); dnnw_d = din("dn_norm_w", [2, 128])
    dnout_d = din("dn_w_out", [2, D, D])
    swqkv_d = din("swa_w_qkv", [2, D, 1280]); swb_d = din("swa_b_qkv", [2, 1280]); sink_d = din("swa_sinks", [2, 16])
    swout_d = din("swa_w_out", [2, D, D]); swbo_d = din("swa_b_out", [2, D]); relb_d = din("rel_bias", [32, 16])
    up_d = din("mlp_w_up", [4, D, 4096]); down_d = din("mlp_w_down", [4, 4096, D])
    cid_d = din("c_ident", [128, 128]); cU_d = din("c_U", [128, 128]); cnl_d = din("c_neglo", [128, 128])
    cnu_d = din("c_negup", [128, 128]); cJ_d = din("c_J", [128, 128]); coh_d = din("c_onehot", [32, 384])
    cng_d = din("c_negrow", [16, 384])
    cmk_d = din("c_masks", [128, 512])
    out_d = nc.dram_tensor("out", [S, D], F32, kind="ExternalOutput")
    stS_i = din("st_S_in", [128, 2048]); stH_i = din("st_halo_in", [128, 192]); fmask_d = din("first_mask", [128, 128])
    stK_i = nc.dram_tensor("st_k_in", [128, 512], BF16, kind="ExternalInput")
    stV_i = nc.dram_tensor("st_vv_in", [128, 256], BF16, kind="ExternalInput")
    stS_o = nc.dram_tensor("st_S_out", [128, 2048], F32, kind="ExternalOutput")
    stH_o = nc.dram_tensor("st_halo_out", [128, 192], F32, kind="ExternalOutput")
    stK_o = nc.dram_tensor("st_k_out", [128, 512], BF16, kind="ExternalOutput")
    stV_o = nc.dram_tensor("st_vv_out", [128, 256], BF16, kind="ExternalOutput")

    units = []
    for L in range(n_layers):
        j = L // 2
        if dn and L % 2 == 0:
            for hd in range(8):
                units.append(('dnin', L, hd))
            for u in range(2):
                units.append(('dnout', L, u))
        if swa and L % 2 == 1:
            for u in range(2):
                units.append(('swq', L, u))
            units.append(('swkv', L, 0))
            for u in range(2):
                units.append(('swout', L, u))
        for g in range(8):
            units.append(('up', L, g))
            units.append(('down', L, g))
    NU = len(units)
    wsc_d = nc.dram_tensor("wsc", [NU, 128, 4096], BF16, kind="Internal")
    bm_d = nc.dram_tensor("bmscr", [16, 384], F32, kind="Internal")

    S_ = Sched()
    st = ExitStack()

    def sb(name, shape, dt=F32):
        return st.enter_context(nc.sbuf_tensor(name, list(shape), dt))

    identf = sb("identf", [128, 128]); identb = sb("identb", [128, 128], BF16)
    onesf = sb("onesf", [128, 128]); negonesf = sb("negonesf", [128, 128]); onesb = sb("onesb", [128, 128], BF16)
    fmask = sb("fmask", [128, 128]); cmask = sb("cmask", [128, 512]); Umask = sb("Umask", [128, 128]); neglo = sb("neglo", [128, 128]); negup = sb("negup", [128, 128])
    nmix = sb("nmix", [128, 32]); nmlp = sb("nmlp", [128, 32]); nfin = sb("nfin", [128, 8])
    cwt = sb("cwt", [128, 2, 96])
    nexpA = sb("nexpA", [128, 2, 8]); dtb = sb("dtb", [128, 2, 8]); dnnw = sb("dnnw", [128, 2])
    wba = sb("wba", [128, 2, 8, 16], BF16)
    bq8 = sb("bq8", [128, 2, 8]); bk2 = sb("bk2", [128, 2, 2]); bvb = sb("bvb", [128, 2, 128])
    sinkb = sb("sinkb", [128, 2, 16]); bo = sb("bo", [128, 2, 8])
    xT = sb("xT", [128, 4096]); hh = sb("hh", [128, 4096], BF16); mo = sb("mo", [128, 4096], BF16)
    ag = sb("ag", [128, 2, 2048], BF16)
    rlb = sb("rlb", [128, 2, 512])
    wr = sb("wr", [128, NSLOT, 4096], BF16)
    xs = sb("xs", [128, 2, 1024])
    Sf = sb("Sf", [128, 16, 128]); Sb = sb("Sb", [128, 16, 128], BF16)
    halo = sb("halo", [128, 48, 4])
    kbuf = sb("kbuf", [128, 2, 2, 640], BF16)
    vbuf = sb("vbuf", [128, 2, 5, 128], BF16)
    sqb = sb("sqb", [128, 2, 512], BF16); rt = sb("rt", [128, 512]); rstd = sb("rstd", [128, 512])
    rpool = sb("rpool", [128, 4, 12, 128], F32R)
    uni = sb("uni", [128, 16512])
    pb = [st.enter_context(nc.psum_tensor("pb%d" % b, [128, 512], F32)) for b in range(8)]

    def xTc(c):
        return xT[:, c * 512:(c + 1) * 512]

    def hc(c):
        return hh[:, c * 512:(c + 1) * 512]

    def moc(c):
        return mo[:, c * 512:(c + 1) * 512]

    def PQ(b):
        return [('pq', b, q) for q in range(4)]

    dcount = [0]

    def dma(eng, out, in_, reads, writes, key):
        writes = list(writes)
        if key.startswith('c'):
            writes.append('chain_' + key)
        S_.add(eng, lambda e: e.dma_start(out=out, in_=in_), reads=reads, writes=writes, dma=key)

    def DAP(t, off, dims):
        return bass.AP(t, off, [list(d) for d in dims])

    dma('pool', identf[:], cid_d.ap(), [], ['identf'], 'c0')
    dma('pool', Umask[:], cU_d.ap(), [], ['Umask'], 'c1')
    dma('pool', neglo[:], cnl_d.ap(), [], ['neglo'], 'c2')
    dma('pool', negup[:], cnu_d.ap(), [], ['negup'], 'c3')
    dma('pool', cmask[:], cmk_d.ap(), [], ['cmask'], 'c3')
    S_.add('pool', lambda e: e.tensor_copy(out=identb[:], in_=identf[:]), ['identf'], ['identb'])
    S_.add('pool', lambda e: e.memset(onesf[:], 1.0), [], ['onesf'])
    S_.add('pool', lambda e: e.memset(negonesf[:], -1.0), [], ['negonesf'])
    S_.add('pool', lambda e: e.memset(onesb[:], 1.0), [], ['onesb'])
    dma('pool', fmask[:], fmask_d.ap(), [], ['fmask'], 'c3')
    dma('pool', Sf[:].rearrange("p a b -> p (a b)"), stS_i.ap(), [], [('Sf', i) for i in range(16)], 'c11')
    S_.add('act', lambda e: e.activation(out=Sb[:].rearrange("p a b -> p (a b)"), in_=Sf[:].rearrange("p a b -> p (a b)"), func=AF.Copy),
           [('Sf', i) for i in range(16)], [('Sb', i) for i in range(16)])
    S_.add('pool', lambda e: e.memset(halo[:], 0.0), [], [('halo', i) for i in range(48)])
    dma('pool', halo[:, :, 0:3], stH_i.ap().rearrange("p (a b) -> p a b", b=4)[:, :, 0:3], [], [('halo', i) for i in range(48)], 'c11')
    S_.add('pool', lambda e: e.memset(kbuf[:], 0.0), [], [('kbuf', jj) for jj in range(2)])
    S_.add('pool', lambda e: e.memset(vbuf[:], 0.0), [], [('vbuf', jj) for jj in range(2)])
    for jj in range(2):
        dma('pool', kbuf[:, jj, :, 0:128], stK_i.ap().rearrange("p (j k c) -> p j k c", j=2, k=2)[:, jj, :, :], [], [('kbuf', jj)], 'c11')
        dma('pool', vbuf[:, jj, 0, :], stV_i.ap().rearrange("p (j c) -> p j c", j=2)[:, jj, :], [], [('vbuf', jj)], 'c11')
    tmpA = uni[:, 0:128]
    tmpB = uni[:, 128:256]

    def load_cols(dst_ap, src_t, nrows, key):
        dma('pool', tmpA[0:nrows, :], DAP(src_t, 0, [[128, nrows], [1, 128]]), [], [('u', 'tmpA')], 'c4')
        S_.add('pe', lambda e: e.transpose(out=pb[0][:, 0:nrows], in_=tmpA[0:nrows, :], identity=identf[0:nrows, 0:nrows]),
               [('u', 'tmpA'), 'identf'], PQ(0))
        S_.add('act', lambda e: e.activation(out=dst_ap, in_=pb[0][:, 0:nrows], func=AF.Copy), PQ(0), [key])

    load_cols(nmix[:], nmix_d, 32, 'nmix')
    load_cols(nmlp[:], nmlp_d, 32, 'nmlp')
    load_cols(nfin[:], nfin_d, 8, 'nfin')
    for jj in range(2):
        dma('pool', tmpA[0:96, :], DAP(dncw_d, jj * 4 * 3072, [[128, 96], [1, 128]]), [], [('u', 'tmpA')], 'c4')
        S_.add('pe', lambda e: e.transpose(out=pb[0][:, 0:96], in_=tmpA[0:96, :], identity=identf[0:96, 0:96]),
               [('u', 'tmpA'), 'identf'], PQ(0))
        S_.add('act', lambda e, jj=jj: e.activation(out=cwt[:, jj, :], in_=pb[0][:, 0:96], func=AF.Copy), PQ(0), ['cwt'])
    load_cols(dnnw[:], dnnw_d, 2, 'dnnw')
    load_cols(bo[:].rearrange("p a b -> p (a b)"), swbo_d, 16, 'bo')
    dma('pool', tmpA[0:20, :], DAP(swb_d, 0, [[128, 20], [1, 128]]), [], [('u', 'tmpA')], 'c4')
    S_.add('pe', lambda e: e.transpose(out=pb[0][:, 0:20], in_=tmpA[0:20, :], identity=identf[0:20, 0:20]),
           [('u', 'tmpA'), 'identf'], PQ(0))
    for jj in range(2):
        S_.add('act', lambda e, jj=jj: e.activation(out=bq8[:, jj, :], in_=pb[0][:, jj * 10:jj * 10 + 8], func=AF.Copy, scale=0.125),
               PQ(0), ['bq8'])
    for jj in range(2):
        for kv in range(2):
            for half in range(2):
                dma('pool', bk2[half * 64:(half + 1) * 64, jj, kv:kv + 1],
                    DAP(swb_d, jj * 1280 + 1024 + kv * 64, [[1, 64], [1, 1]]), [], ['bk2'], 'c5')
        dma('pool', bvb[:, jj, :], DAP(swb_d, jj * 1280 + 1152, [[0, 128], [1, 128]]), [], ['bvb'], 'c5')
        dma('pool', sinkb[:, jj, :], DAP(sink_d, jj * 16, [[0, 128], [1, 16]]), [], ['sinkb'], 'c5')
        dma('pool', dtb[:, jj, :], DAP(dtb_d, jj * 8, [[0, 128], [1, 8]]), [], ['dtb'], 'c5')
        dma('pool', nexpA[:, jj, :], DAP(alog_d, jj * 8, [[0, 128], [1, 8]]), [], ['nexpA0'], 'c5')
    S_.add('act', lambda e: e.activation(out=nexpA[:], in_=nexpA[:], func=AF.Exp), ['nexpA0'], ['nexpA1'])
    S_.add('dve', lambda e: e.tensor_scalar(out=nexpA[:], in0=nexpA[:], scalar1=-1.0, scalar2=None, op0=ALU.mult),
           ['nexpA1'], ['nexpA'])
    for jj in range(2):
        dma('pool', tmpB[:, 0:128].rearrange("p (k c) -> p k c", c=16),
            DAP(dnin_d, jj * D * 4112 + 4096, [[4112, 128], [128 * 4112, 8], [1, 16]]), [], [('u', 'tmpB')], 'c6')
        S_.add('dve', lambda e, jj=jj: e.tensor_copy(out=wba[:, jj, :, :], in_=tmpB[:, 0:128].rearrange("p (k c) -> p k c", c=16)),
               [('u', 'tmpB')], ['wba'])

    cast_rr = [0]

    def prepass():
        hu = 0
        for uid, (kind, L, idx) in enumerate(units):
            j = L // 2
            for hf in range(2):
                s = hu % 2
                hu += 1
                stg = xT[:, s * 2048:(s + 1) * 2048]
                bst = hh[:, s * 2048:(s + 1) * 2048]
                skeys = [('xT', 4 * s + i) for i in range(4)]
                bkeys = [('h', 4 * s + i) for i in range(4)]
                if kind == 'dnin':
                    for f2 in range(2):
                        f = 2 * hf + f2
                        dma('sp', stg[:, f2 * 1024:(f2 + 1) * 1024].rearrange("p (k c) -> p k c", c=128),
                            DAP(dnin_d, j * D * 4112 + f * 1024 + idx * 128, [[4112, 128], [128 * 4112, 8], [1, 128]]),
                            [], skeys, 'pp%d' % s)
                elif kind in ('dnout', 'swq', 'swout', 'up'):
                    src, rs, base = {'dnout': (dnout_d, D, j * D * D), 'swq': (swqkv_d, 1280, j * D * 1280),
                                     'swout': (swout_d, D, j * D * D), 'up': (up_d, 4096, L * D * 4096)}[kind]
                    dma('sp', stg.rearrange("p (k c) -> p k c", c=512),
                        DAP(src, base + (4 * hf) * 128 * rs + idx * 512, [[rs, 128], [128 * rs, 4], [1, 512]]),
                        [], skeys, 'pp%d' % s)
                elif kind == 'down':
                    dma('sp', stg.rearrange("p (k c) -> p k c", c=1024),
                        DAP(down_d, L * 4096 * D + ((idx * 4 + 2 * hf) * 128) * D, [[D, 128], [128 * D, 2], [1, D]]),
                        [], skeys, 'pp%d' % s)
                elif kind == 'swkv':
                    s3 = stg.rearrange("p (k c) -> p k c", c=512)
                    for (c0, w, col) in ((0, 64, 1024), (64, 64, 1024), (128, 64, 1088), (192, 64, 1088), (256, 128, 1152)):
                        dma('sp', s3[:, :, c0:c0 + w],
                            DAP(swqkv_d, j * D * 1280 + (4 * hf) * 128 * 1280 + col, [[1280, 128], [128 * 1280, 4], [1, w]]),
                            [], skeys, 'pp%d' % s)
                    S_.add('pool', lambda e, s3=s3: e.memset(s3[:, :, 384:512], 0.0), [], skeys)
                eng = ('act', 'dve', 'pool')[cast_rr[0] % 3]
                cast_rr[0] += 1
                if eng == 'act':
                    S_.add('act', lambda e, bst=bst, stg=stg: e.activation(out=bst, in_=stg, func=AF.Copy), skeys, bkeys)
                else:
                    S_.add(eng, lambda e, bst=bst, stg=stg: e.tensor_copy(out=bst, in_=stg), skeys, bkeys)
                dma('sp', DAP(wsc_d, uid * 128 * 4096 + hf * 2048, [[4096, 128], [1, 2048]]), bst,
                    bkeys, [('wsc', uid)], 'ps%d' % s)

    prepass()

    total_units = NU * NT
    ring = dict(loaded=0, used=0)

    def ring_load():
        gu = ring['loaded']
        if gu >= total_units:
            return
        slot = gu % NSLOT
        uid = gu % NU
        dma('sp', wr[:, slot, :], DAP(wsc_d, uid * 128 * 4096, [[4096, 128], [1, 4096]]),
            [('wsc', uid)], [('wr', slot)], 'w%d' % slot)
        ring['loaded'] += 1

    def ring_next(expect):
        gu = ring['used']
        assert units[gu % NU][0] == expect, (units[gu % NU], expect)
        ring['used'] += 1
        slot = gu % NSLOT
        return wr[:, slot, :], ('wr', slot)

    def ring_release():
        ring_load()

    for _ in range(NSLOT):
        ring_load()

    rot = dict(big=0, q=0)

    def big(banks=(0, 1)):
        b = banks[rot['big'] % len(banks)]
        rot['big'] += 1
        return pb[b], PQ(b)

    def rms_stats():
        ps, pk = big()
        for c in range(8):
            sq = sqb[:, c % 2, :]
            S_.add('act', lambda e, c=c, sq=sq: e.activation(out=sq, in_=xTc(c), func=AF.Square), [('xT', c)], [('sqb', c % 2)])
            S_.add('pe', lambda e, c=c, sq=sq, ps=ps: e.matmul(ps[:], lhsT=onesb[:], rhs=sq, start=(c == 0), stop=(c == 7)),
                   [('sqb', c % 2), 'onesb'], pk)
        S_.add('act', lambda e, ps=ps: e.activation(out=rt[:], in_=ps[:], func=AF.Ln, bias=EPS, scale=1.0 / D), pk, ['rt'])
        S_.add('act', lambda e: e.activation(out=rstd[:], in_=rt[:], func=AF.Exp, scale=-0.5), ['rt'], ['rstd'])

    def rms_norm_to_h(wtile, wkey, col0):
        rms_stats()
        for c in range(8):
            S_.add('dve', lambda e, c=c: e.scalar_tensor_tensor(out=hc(c), in0=xTc(c), scalar=wtile[:, col0 + c:col0 + c + 1],
                                                              op0=ALU.mult, in1=rstd[:], op1=ALU.mult),
                   [('xT', c), wkey, 'rstd'], [('h', c)])

    def load_x_tile(t):
        for s4 in range(4):
            sl = s4 % 2
            r0 = t * T + s4 * 128
            dma('pool', xs[:, sl, :], DAP(x_d, r0 * D, [[D, 128], [1, D]]), [], [('xs', sl)], 'xs%d' % sl)
            for hf in range(2):
                ps, pk = big()
                for cl in range(4):
                    c = hf * 4 + cl
                    S_.add('pe', lambda e, ps=ps, cl=cl, c=c, sl=sl: e.transpose(out=ps[:, cl * 128:(cl + 1) * 128],
                                                                               in_=xs[:, sl, c * 128:(c + 1) * 128], identity=identf[:]),
                           [('xs', sl), 'identf'], pk)
                dst = xT[:, hf * 2048:(hf + 1) * 2048].rearrange("p (c t) -> p c t", t=512)[:, :, s4 * 128:(s4 + 1) * 128]
                S_.add('act', lambda e, ps=ps, dst=dst: e.activation(out=dst, in_=ps[:].rearrange("p (c t) -> p c t", t=128), func=AF.Copy),
                       pk, [('xT', hf * 4 + i) for i in range(4)])

    def store_out_tile(t):
        rms_stats()
        for c in range(8):
            S_.add('dve', lambda e, c=c: e.scalar_tensor_tensor(out=xTc(c), in0=xTc(c), scalar=nfin[:, c:c + 1],
                                                              op0=ALU.mult, in1=rstd[:], op1=ALU.mult),
                   [('xT', c), 'nfin', 'rstd'], [('xT', c)])
        for s4 in range(4):
            sl = s4 % 2
            for hf in range(2):
                ps, pk = big()
                for cl in range(4):
                    c = hf * 4 + cl
                    S_.add('pe', lambda e, ps=ps, cl=cl, c=c, s4=s4: e.transpose(out=ps[:, cl * 128:(cl + 1) * 128],
                                                                               in_=xTc(c)[:, s4 * 128:(s4 + 1) * 128], identity=identf[:]),
                           [('xT', c), 'identf'], pk)
                S_.add('act', lambda e, ps=ps, sl=sl, hf=hf: e.activation(out=xs[:, sl, hf * 512:(hf + 1) * 512], in_=ps[:], func=AF.Copy),
                       pk, [('xs', sl)])
            r0 = t * T + s4 * 128
            dma('pool', DAP(out_d, r0 * D, [[D, 128], [1, D]]), xs[:, sl, :], [('xs', sl)], [('out', sl)], 'os%d' % sl)

    def mlp(L):
        rms_norm_to_h(nmlp, 'nmlp', L * 8)
        for g in range(8):
            wu, wuk = ring_next('up')
            wu3 = wu.rearrange("p (k c) -> p k c", c=512)
            a = ag[:, g % 2, :]
            for fl in range(4):
                ps, pk = big((0, 1, 2, 3))
                for k in range(8):
                    S_.add('pe', lambda e, ps=ps, wu3=wu3, k=k, fl=fl: e.matmul(ps[:], lhsT=wu3[:, k, fl * 128:(fl + 1) * 128], rhs=hc(k),
                                                                              start=(k == 0), stop=(k == 7)),
                           [wuk, ('h', k)], pk)
                asl = a[:, fl * 512:(fl + 1) * 512]
                akey = ('ag', g % 2, fl)
                rl = rlb[:, fl % 2, :]
                S_.add('act', lambda e, ps=ps, rl=rl: e.activation(out=rl, in_=ps[:], func=AF.Relu), pk, [('rlb', fl % 2)])
                S_.add('pool', lambda e, asl=asl, rl=rl: e.tensor_tensor(out=asl, in0=rl, in1=rl, op=ALU.mult), [('rlb', fl % 2)], [akey])
            ring_release()
            wd, wdk = ring_next('down')
            wd3 = wd.rearrange("p (f c) -> p f c", c=1024)
            for dc in range(8):
                ps, pk = big((4, 5, 6, 7))
                for fl in range(4):
                    S_.add('pe', lambda e, ps=ps, wd3=wd3, fl=fl, dc=dc, a=a: e.matmul(ps[:], lhsT=wd3[:, fl, dc * 128:(dc + 1) * 128],
                                                                                   rhs=a[:, fl * 512:(fl + 1) * 512],
                                                                                   start=(fl == 0), stop=(fl == 3)),
                           [wdk, ('ag', g % 2, fl)], pk)
                S_.add('dve', lambda e, ps=ps, dc=dc: e.tensor_tensor(out=xTc(dc), in0=ps[:], in1=xTc(dc), op=ALU.add),
                       pk + [('xT', dc)], [('xT', dc)])
            ring_release()

    ukeys = set([('u', 'tmpA'), ('u', 'tmpB')])

    def uk(*k):
        key = ('u',) + k
        ukeys.add(key)
        return key

    def uf(off, n):
        return uni[:, off:off + n]

    def ubf(off, n):
        return uni[:, off:off + n // 2].bitcast(BF16)

    def op(eng, method, reads, writes, **kw):
        S_.add(eng, lambda e: getattr(e, method)(**kw), reads, writes)

    def pq(b, q):
        return pb[b][:, q * 128:(q + 1) * 128], [('pq', b, q)]

    def phase_barrier():
        S_.barrier(sorted(ukeys, key=str))

    bmfull_d = nc.dram_tensor("bmfull", [128, 4096], F32, kind="Internal")
    if swa:
        Jm = uf(8192, 128); oh = uf(8320, 384); ngr = uf(8704, 384); relb = uf(9088, 16); rvec = uf(9104, 384)
        BMrev = uf(4096, 4096); BMsb = uf(0, 4096)
        dma('pool', Jm, cJ_d.ap(), [], [uk('Jm')], 'c7')
        dma('pool', oh[0:32, :], coh_d.ap(), [], [uk('oh')], 'c7')
        dma('pool', ngr[0:16, :], cng_d.ap(), [], [uk('ngr')], 'c7')
        dma('pool', relb[0:32, :], relb_d.ap(), [], [uk('relb')], 'c7')
        op('pe', 'matmul', [uk('relb'), uk('oh')], PQ(0), out=pb[0][0:16, 0:384], lhsT=relb[0:32, :], rhs=oh[0:32, :], start=True, stop=True)
        op('dve', 'tensor_tensor', PQ(0) + [uk('ngr')], [uk('rvec')], out=rvec[0:16, :], in0=pb[0][0:16, 0:384], in1=ngr[0:16, :], op=ALU.add)
        dma('pool', bm_d.ap(), rvec[0:16, :], [uk('rvec')], ['bm_d'], 'c8')
        for hh2 in range(2):
            dma('sp', BMrev[:, hh2 * 2048:(hh2 + 1) * 2048].rearrange("p (h k) -> p h k", k=256),
                DAP(bm_d, hh2 * 8 * 384, [[1, 128], [384, 8], [1, 256]]), ['bm_d'], [uk('BMrev', hh2)], 'c9')
        for g8 in range(8):
            ps, pk = big()
            op('pe', 'matmul', [uk('Jm'), uk('BMrev', g8 // 4)], pk, out=ps[:], lhsT=Jm, rhs=BMrev[:, g8 * 512:(g8 + 1) * 512], start=True, stop=True)
            op('act', 'activation', pk, [uk('BMsb', g8)], out=BMsb[:, g8 * 512:(g8 + 1) * 512], in_=ps[:], func=AF.Copy)
        dma('sp', bmfull_d.ap(), BMsb, [uk('BMsb', g8) for g8 in range(8)], ['bmfull'], 'c10')

    pre3 = uf(0, 1548).rearrange("p (f n) -> p f n", n=516)
    cv3 = uf(1548, 1536).rearrange("p (f n) -> p f n", n=512)
    sl3 = uf(3084, 1024).rearrange("p (f n) -> p f n", n=512)
    sg = uf(4108, 512); rt2 = uf(4620, 512); rs2 = uf(7692, 512)
    sq2 = ubf(5132, 512)
    vbb = [ubf(5388 + i * 256, 512) for i in range(2)]
    zsb = [ubf(5900 + i * 256, 512) for i in range(3)]
    qnb = [ubf(6668 + i * 256, 512) for i in range(2)]
    knb = [ubf(7180 + i * 256, 512) for i in range(2)]
    LNAMES = ['GU', 't1', 't2', 'egb']
    RNAMES = ['Q0', 'P0', 'Q1', 'P1', 'Q2', 'P2', 'R0', 'R1', 'T0', 'T1', 'kbg', 'bv']
    lset = [{n: uf(8204 + c * 512 + i * 128, 128) for i, n in enumerate(LNAMES)} for c in range(4)]
    for c in range(4):
        for i, n in enumerate(RNAMES):
            lset[c][n] = rpool[:, c, i, :].bitcast(F32)

    def mk_hout(base):
        d = {'u': uf(base, 128)}
        for i, n in enumerate(['attnT', 'kdec', 'qdec', 'wT']):
            d[n] = ubf(base + 128 + i * 64, 128)
        d['cols'] = uf(base + 384, 8)
        return d

    hout = [[mk_hout(10252 + (p_ * 4 + c) * 400) for c in range(4)] for p_ in range(2)]
    vnewb = [ubf(13452 + i * 64, 128) for i in range(2)]
    junk = ubf(13580, 128)
    onb = [ubf(13644 + i * 64, 128) for i in range(2)]
    scol = uf(13772, 8)
    beta3 = uf(13780, 32).rearrange("p (s h) -> p s h", h=8)
    g3 = uf(13812, 32).rearrange("p (s h) -> p s h", h=8)
    tg3 = uf(13844, 32).rearrange("p (s h) -> p s h", h=8)
    ee3 = uf(13876, 32).rearrange("p (s h) -> p s h", h=8)

    def dn_layer(L, t):
        j = L // 2
        phase_barrier()
        rms_norm_to_h(nmix, 'nmix', L * 8)
        psq, pk = pq(2, 0)
        for s4 in range(4):
            for k in range(8):
                op('pe', 'matmul', [('h', k), 'wba'], pk, out=psq[:, s4 * 16:(s4 + 1) * 16], lhsT=hc(k)[:, s4 * 128:(s4 + 1) * 128],
                   rhs=wba[:, j, k, :], start=(k == 0), stop=(k == 7))
        pba3 = psq[:, 0:64].rearrange("p (s c) -> p s c", c=16)
        op('act', 'activation', pk, [uk('ee')], out=ee3, in_=pba3[:, :, 0:8], func=AF.Exp, scale=-1.0)
        op('act', 'activation', [uk('ee')], [uk('ee')], out=ee3, in_=ee3, func=AF.Ln, bias=1.0)
        op('act', 'activation', [uk('ee')], [uk('beta')], out=beta3, in_=ee3, func=AF.Exp, scale=-1.0)
        op('dve', 'tensor_tensor', pk + ['dtb'], [uk('tg')], out=tg3, in0=pba3[:, :, 8:16],
           in1=bass.AP(dtb, j * 8, [[16, 128], [0, 4], [1, 8]]), op=ALU.add)
        op('act', 'activation', [uk('tg')], [uk('tg')], out=tg3, in_=tg3, func=AF.Exp)
        op('act', 'activation', [uk('tg')], [uk('tg')], out=tg3, in_=tg3, func=AF.Ln, bias=1.0)
        op('dve', 'tensor_tensor', [uk('tg'), 'nexpA'], [uk('g')], out=g3, in0=tg3,
           in1=bass.AP(nexpA, j * 8, [[16, 128], [0, 4], [1, 8]]), op=ALU.mult)

        def projconv(hd):
            par = hd % 2
            wv, wk = ring_next('dnin')
            w4 = wv.rearrange("p (f k c) -> p f k c", f=4, k=8)
            for fi in range(3):
                ps, pk = big()
                for k in range(8):
                    op('pe', 'matmul', [wk, ('h', k)], pk, out=ps[:], lhsT=w4[:, fi, k, :], rhs=hc(k), start=(k == 0), stop=(k == 7))
                hidx = (j * 8 + hd) * 3 + fi
                op('pool', 'tensor_copy', [('halo', hidx)], [uk('pre', fi)], out=pre3[:, fi, 0:3], in_=halo[:, hidx, 0:3])
                op('act', 'activation', pk, [uk('pre', fi)], out=pre3[:, fi, 3:515], in_=ps[:], func=AF.Copy)
                op('pool', 'tensor_copy', [uk('pre', fi)], [('halo', hidx)], out=halo[:, hidx, 0:3], in_=pre3[:, fi, 512:515])
                yield
                ch = fi * 8 + hd
                for tap in range(4):
                    wcol = cwt[:, j, tap * 24 + ch:tap * 24 + ch + 1]
                    if tap == 0:
                        op('dve', 'tensor_scalar', [uk('pre', fi), 'cwt'], [uk('cv', fi)], out=cv3[:, fi, :], in0=pre3[:, fi, 0:512],
                           scalar1=wcol, scalar2=None, op0=ALU.mult)
                    else:
                        op('dve', 'scalar_tensor_tensor', [uk('pre', fi), 'cwt', uk('cv', fi)], [uk('cv', fi)], out=cv3[:, fi, :],
                           in0=pre3[:, fi, tap:tap + 512], scalar=wcol, op0=ALU.mult, in1=cv3[:, fi, :], op1=ALU.add)
                yield
                op('act', 'activation', [uk('cv', fi)], [uk('sg')], out=sg, in_=cv3[:, fi, :], func=AF.Exp, scale=-1.0)
                op('act', 'activation', [uk('sg')], [uk('sg')], out=sg, in_=sg, func=AF.Ln, bias=1.0)
                op('act', 'activation', [uk('sg')], [uk('sg')], out=sg, in_=sg, func=AF.Exp, scale=-1.0)
                if fi < 2:
                    op('pool', 'tensor_tensor', [uk('cv', fi), uk('sg')], [uk('sl', fi)], out=sl3[:, fi, :], in0=cv3[:, fi, :], in1=sg, op=ALU.mult)
                else:
                    op('pool', 'tensor_tensor', [uk('cv', 2), uk('sg')], [uk('vb', par)], out=vbb[par], in0=cv3[:, 2, :], in1=sg, op=ALU.mult)
                yield
            ps, pk = big()
            for k in range(8):
                op('pe', 'matmul', [wk, ('h', k)], pk, out=ps[:], lhsT=w4[:, 3, k, :], rhs=hc(k), start=(k == 0), stop=(k == 7))
            ring_release()
            op('act', 'activation', pk, [uk('cv', 0)], out=cv3[:, 0, :], in_=ps[:], func=AF.Copy)
            op('act', 'activation', pk, [uk('sg')], out=sg, in_=ps[:], func=AF.Exp, scale=-1.0)
            op('act', 'activation', [uk('sg')], [uk('sg')], out=sg, in_=sg, func=AF.Ln, bias=1.0)
            op('act', 'activation', [uk('sg')], [uk('sg')], out=sg, in_=sg, func=AF.Exp, scale=-1.0)
            op('pool', 'tensor_tensor', [uk('cv', 0), uk('sg')], [uk('zs', hd % 3)], out=zsb[hd % 3], in0=cv3[:, 0, :], in1=sg, op=ALU.mult)
            yield
            for fi in range(2):
                op('act', 'activation', [uk('sl', fi)], [uk('sq2')], out=sq2, in_=sl3[:, fi, :], func=AF.Square)
                ps, pk = big()
                op('pe', 'matmul', [uk('sq2'), 'onesb'], pk, out=ps[:], lhsT=onesb[:], rhs=sq2, start=True, stop=True)
                op('act', 'activation', pk, [uk('rt2')], out=rt2, in_=ps[:], func=AF.Ln, bias=EPS)
                bias = -0.5 * math.log(128.0) if fi == 0 else 0.0
                op('act', 'activation', [uk('rt2')], [uk('rs2')], out=rs2, in_=rt2, func=AF.Exp, scale=-0.5, bias=bias)
                dst = qnb[par] if fi == 0 else knb[par]
                op('pool', 'tensor_tensor', [uk('sl', fi), uk('rs2')], [uk('qn' if fi == 0 else 'kn', par)], out=dst, in0=sl3[:, fi, :], in1=rs2, op=ALU.mult)
                yield

        def local(hd, c):
            par = hd % 2
            cs = slice(c * 128, (c + 1) * 128)
            ls = lset[c]
            ho = hout[par][c]
            lk = lambda n: uk('l', c, n)
            hk = lambda n: uk('ho', par, c, n)
            bcol = beta3[:, c, hd:hd + 1]
            gcol = g3[:, c, hd:hd + 1]
            qn_, kn_, vb_ = qnb[par], knb[par], vbb[par]
            lb = 2 + c
            tpf, tk = pq(lb, 0)
            tp = tpf.bitcast(BF16)
            op('pe', 'transpose', [uk('kn', par), 'identb'], tk, out=tp[:, 0:128], in_=kn_[:, cs], identity=identb[:])
            op('pe', 'transpose', [uk('vb', par), 'identb'], tk, out=tp[:, 128:256], in_=vb_[:, cs], identity=identb[:])
            op('pool', 'tensor_scalar', ['Umask', uk('g')], [lk('GU')], out=ls['GU'], in0=Umask[:], scalar1=gcol, scalar2=None, op0=ALU.mult)
            Ep, ek = pq(lb, 1)
            op('pe', 'matmul', [lk('GU'), 'onesf'], ek, out=Ep, lhsT=ls['GU'], rhs=onesf[:], start=True, stop=False)
            op('pe', 'matmul', [lk('GU'), 'negonesf'], ek, out=Ep, lhsT=negonesf[:], rhs=ls['GU'], start=False, stop=True)
            Bp, bk_ = pq(lb, 2)
            op('pe', 'matmul', [lk('GU'), 'onesf'], bk_, out=Bp, lhsT=onesf[:], rhs=ls['GU'], start=True, stop=True)
            cols = ho['cols']
            op('act', 'activation', bk_, [hk('gl')], out=cols[:, 0:1], in_=Bp[:, 127:128], func=AF.Copy)
            op('act', 'activation', bk_, [hk('egl')], out=cols[:, 1:2], in_=Bp[:, 127:128], func=AF.Exp)
            op('act', 'activation', bk_, [lk('egb')], out=ls['egb'], in_=Bp, func=AF.Exp)
            op('act', 'activation', ek, [hk('edl')], out=cols[:, 2:3], in_=Ep[:, 127:128], func=AF.Exp, scale=-1.0)
            op('act', 'activation', ek + [hk('gl')], [hk('eg')], out=cols[:, 3:4], in_=Ep[:, 127:128], func=AF.Exp, bias=cols[:, 0:1], scale=1.0)
            op('dve', 'tensor_tensor', ek + ['neglo'], [lk('t1')], out=ls['t1'], in0=Ep, in1=neglo[:], op=ALU.add)
            op('dve', 'scalar_tensor_tensor', ek + ['negup'], [lk('t2')], out=ls['t2'], in0=Ep, scalar=-1.0, op0=ALU.mult, in1=negup[:], op1=ALU.add)
            op('act', 'activation', [lk('t1')], [lk('t1')], out=ls['t1'], in_=ls['t1'], func=AF.Exp)
            op('act', 'activation', [lk('t2')], [lk('t2')], out=ls['t2'], in_=ls['t2'], func=AF.Exp)
            op('dve', 'tensor_scalar', tk + [uk('beta'), hk('eg')], [lk('kbg')], out=ls['kbg'].bitcast(F32R), in0=tp[:, 0:128],
               scalar1=bcol, scalar2=cols[:, 3:4], op0=ALU.mult, op1=ALU.mult)
            op('dve', 'tensor_scalar', tk + [hk('edl')], [hk('kdec')], out=ho['kdec'], in0=tp[:, 0:128], scalar1=cols[:, 2:3], scalar2=None, op0=ALU.mult)
            op('dve', 'tensor_scalar', tk + [uk('beta')], [lk('bv')], out=ls['bv'].bitcast(F32R), in0=tp[:, 128:256], scalar1=bcol, scalar2=None, op0=ALU.mult)
            yield
            KKp, kkk = pq(lb, 0)
            op('pe', 'matmul', [uk('kn', par)], kkk, out=KKp, lhsT=kn_[:, cs], rhs=kn_[:, cs], start=True, stop=True)
            KQp, kqk = pq(lb, 1)
            op('pe', 'matmul', [uk('kn', par), uk('qn', par)], kqk, out=KQp, lhsT=kn_[:, cs], rhs=qn_[:, cs], start=True, stop=True)
            op('dve', 'scalar_tensor_tensor', kkk + [uk('beta'), lk('t1')], [lk('Q0')], out=ls['Q0'].bitcast(F32R), in0=KKp, scalar=bcol,
               op0=ALU.mult, in1=ls['t1'], op1=ALU.mult)
            op('dve', 'tensor_tensor', kqk + [lk('t2')], [hk('attnT')], out=ho['attnT'], in0=KQp, in1=ls['t2'], op=ALU.mult)
            op('pool', 'tensor_tensor', [uk('qn', par), lk('egb')], [hk('qdec')], out=ho['qdec'], in0=qn_[:, cs], in1=ls['egb'], op=ALU.mult)
            yield
            Btp, btk = pq(lb, 2)
            op('pe', 'transpose', [lk('Q0'), 'identf'], btk, out=Btp, in_=ls['Q0'], identity=identf[:])
            op('act', 'activation', btk, [lk('P0')], out=ls['P0'].bitcast(F32R), in_=Btp, func=AF.Copy)
            NM = dict(Qa='Q1', Pa='P1', Qb='Q2', Pb='P2', Ra='R0', Rb='R1', Ta='T0', Tb='T1')
            tl = lambda n: ls[NM[n]]
            tkk = lambda n: lk(NM[n])
            r32 = lambda n: tl(n).bitcast(F32R)
            M = lambda l: cmask[:, l * 128:(l + 1) * 128]
            op('dve', 'tensor_tensor', [lk('Q0'), 'cmask'], [tkk('Qa')], out=r32('Qa'), in0=ls['Q0'], in1=M(0), op=ALU.mult)
            op('dve', 'tensor_tensor', [lk('P0'), 'cmask'], [tkk('Pa')], out=r32('Pa'), in0=ls['P0'], in1=M(0), op=ALU.mult)
            op('dve', 'tensor_tensor', [tkk('Qa'), 'identf'], [tkk('Ta')], out=r32('Ta'), in0=identf[:], in1=tl('Qa'), op=ALU.subtract)
            op('dve', 'tensor_tensor', [tkk('Pa'), 'identf'], [tkk('Ra')], out=r32('Ra'), in0=identf[:], in1=tl('Pa'), op=ALU.subtract)
            yield
            Qc, Pc, Qn_, Pn_, Rc, Rn_, Tc, Tn_ = 'Qa', 'Pa', 'Qb', 'Pb', 'Ra', 'Rb', 'Ta', 'Tb'
            for lev in range(3):
                Pps, ppk = pq(lb, 0)
                op('pe', 'matmul', [tkk(Qc), tkk(Pc)], ppk, out=Pps, lhsT=r32(Qc), rhs=r32(Pc), start=True, stop=True)
                Qps, qpk = pq(lb, 1)
                op('pe', 'matmul', [tkk(Qc), tkk(Pc)], qpk, out=Qps, lhsT=r32(Pc), rhs=r32(Qc), start=True, stop=True)
                op('act', 'activation', qpk, [tkk(Qn_)], out=r32(Qn_), in_=Qps, func=AF.Copy)
                op('dve', 'tensor_copy', ppk, [tkk(Pn_)], out=r32(Pn_), in_=Pps)
                yield
                Rps, rpk = pq(lb, 2)
                op('pe', 'matmul', [tkk(Qn_), tkk(Rc)], rpk, out=Rps, lhsT=r32(Qn_), rhs=r32(Rc), start=True, stop=True)
                Tps, tpk = pq(lb, 3)
                op('pe', 'matmul', [tkk(Pn_), tkk(Tc)], tpk, out=Tps, lhsT=r32(Pn_), rhs=r32(Tc), start=True, stop=True)
                op('dve', 'tensor_tensor', rpk + [tkk(Rc)], [tkk(Rn_)], out=r32(Rn_), in0=Rps, in1=tl(Rc), op=ALU.add)
                op('dve', 'tensor_tensor', tpk + [tkk(Tc)], [tkk(Tn_)], out=r32(Tn_), in0=Tps, in1=tl(Tc), op=ALU.add)
                yield
                Qc, Qn_ = Qn_, Qc
                Pc, Pn_ = Pn_, Pc
                Rc, Rn_ = Rn_, Rc
                Tc, Tn_ = Tn_, Tc
            Xs, X2s = Qn_, Pn_
            for lev in range(1, 4):
                last = (lev == 3)
                Xp, xk = pq(lb, 0)
                op('pe', 'matmul', [lk('Q0'), tkk(Rc)], xk, out=Xp, lhsT=ls['Q0'].bitcast(F32R), rhs=r32(Rc), start=True, stop=True)
                op('dve', 'tensor_tensor', xk + ['cmask'], [tkk(Xs)], out=r32(Xs), in0=Xp, in1=M(lev), op=ALU.mult)
                if not last:
                    X2p, x2k = pq(lb, 1)
                    op('pe', 'matmul', [lk('P0'), tkk(Tc)], x2k, out=X2p, lhsT=ls['P0'].bitcast(F32R), rhs=r32(Tc), start=True, stop=True)
                    op('dve', 'tensor_tensor', x2k + ['cmask'], [tkk(X2s)], out=r32(X2s), in0=X2p, in1=M(lev), op=ALU.mult)
                yield
                Yrp, yrk = pq(lb, 2)
                op('pe', 'matmul', [tkk(Tc), tkk(Xs)], yrk, out=Yrp, lhsT=r32(Tc), rhs=r32(Xs), start=True, stop=True)
                if not last:
                    Ytp, ytk = pq(lb, 3)
                    op('pe', 'matmul', [tkk(Rc), tkk(X2s)], ytk, out=Ytp, lhsT=r32(Rc), rhs=r32(X2s), start=True, stop=True)
                op('dve', 'tensor_tensor', yrk + [tkk(Rc)], [tkk(Rn_)], out=r32(Rn_), in0=tl(Rc), in1=Yrp, op=ALU.subtract)
                if not last:
                    op('dve', 'tensor_tensor', ytk + [tkk(Tc)], [tkk(Tn_)], out=r32(Tn_), in0=tl(Tc), in1=Ytp, op=ALU.subtract)
                yield
                Rc, Rn_ = Rn_, Rc
                Tc, Tn_ = Tn_, Tc
            Rf = NM[Rc]
            ups, upk = pq(lb, 0)
            op('pe', 'matmul', [lk(Rf), lk('bv')], upk, out=ups, lhsT=ls[Rf].bitcast(F32R), rhs=ls['bv'].bitcast(F32R), start=True, stop=True)
            wps, wpk = pq(lb, 1)
            op('pe', 'matmul', [lk(Rf), lk('kbg')], wpk, out=wps, lhsT=ls['kbg'].bitcast(F32R), rhs=ls[Rf].bitcast(F32R), start=True, stop=True)
            op('act', 'activation', upk, [hk('u')], out=ho['u'], in_=ups, func=AF.Copy)
            op('act', 'activation', wpk, [hk('wT')], out=ho['wT'], in_=wps, func=AF.Copy)
            yield

        def seq(hd):
            par = hd % 2
            si = j * 8 + hd
            for c in range(4):
                cs = slice(c * 128, (c + 1) * 128)
                ho = hout[par][c]
                hk = lambda n, c=c: uk('ho', par, c, n)
                cols = ho['cols']
                vi = c % 2
                wsp, wsk = pq(6, 0)
                op('pe', 'matmul', [hk('wT'), ('Sb', si)], wsk, out=wsp, lhsT=ho['wT'], rhs=Sb[:, si, :], start=True, stop=True)
                op('dve', 'tensor_tensor', wsk + [hk('u')], [uk('vnew', vi)], out=vnewb[vi], in0=ho['u'], in1=wsp, op=ALU.subtract)
                yield
                ops_, opk = pq(7, 0)
                op('pe', 'matmul', [hk('qdec'), ('Sb', si)], opk, out=ops_, lhsT=ho['qdec'], rhs=Sb[:, si, :], start=True, stop=False)
                op('pe', 'matmul', [hk('attnT'), uk('vnew', vi)], opk, out=ops_, lhsT=ho['attnT'], rhs=vnewb[vi], start=False, stop=True)
                sup, suk = pq(6, 1)
                op('pe', 'matmul', [hk('kdec'), uk('vnew', vi)], suk, out=sup, lhsT=ho['kdec'], rhs=vnewb[vi], start=True, stop=True)
                op('dve', 'scalar_tensor_tensor', suk + [('Sf', si), hk('egl')], [('Sf', si)], out=Sf[:, si, :], in0=Sf[:, si, :], scalar=cols[:, 1:2],
                   op0=ALU.mult, in1=sup, op1=ALU.add)
                op('act', 'activation', [('Sf', si)], [('Sb', si)], out=Sb[:, si, :], in_=Sf[:, si, :], func=AF.Copy)
                sc = scol[:, vi * 4:vi * 4 + 4]
                op('act', 'activation', opk, [uk('junk'), uk('ssq', vi)], out=junk, in_=ops_, func=AF.Square, accum_out=sc[:, 0:1])
                op('act', 'activation', [uk('ssq', vi)], [uk('sln', vi)], out=sc[:, 1:2], in_=sc[:, 0:1], func=AF.Ln, bias=EPS, scale=1.0 / 128)
                op('act', 'activation', [uk('sln', vi)], [uk('rso', vi)], out=sc[:, 2:3], in_=sc[:, 1:2], func=AF.Exp, scale=-0.5)
                op('act', 'activation', opk + [uk('rso', vi)], [uk('on', vi)], out=onb[vi], in_=ops_, func=AF.Copy, scale=sc[:, 2:3])
                yield
                otf, otk = pq(7, 1)
                otp = otf.bitcast(BF16)
                op('pe', 'transpose', [uk('on', vi), 'identb'], otk, out=otp[:, 0:128], in_=onb[vi], identity=identb[:])
                op('dve', 'scalar_tensor_tensor', otk + ['dnnw', uk('zs', hd % 3)], [('mo', hd)], out=moc(hd)[:, cs], in0=otp[:, 0:128],
                   scalar=dnnw[:, j:j + 1], op0=ALU.mult, in1=zsb[hd % 3][:, cs], op1=ALU.mult)
                yield

        run_interleaved([projconv(0)])
        for hd in range(8):
            gens = [local(hd, c) for c in range(4)]
            if hd > 0:
                gens.append(seq(hd - 1))
            if hd < 7:
                gens.append(projconv(hd + 1))
            run_interleaved(gens)
        run_interleaved([seq(7)])
        for u in range(2):
            wv, wk = ring_next('dnout')
            w3 = wv.rearrange("p (k c) -> p k c", c=512)
            for dcl in range(4):
                dc = u * 4 + dcl
                ps, pk = big()
                for kk in range(8):
                    op('pe', 'matmul', [wk, ('mo', kk)], pk, out=ps[:], lhsT=w3[:, kk, dcl * 128:(dcl + 1) * 128], rhs=moc(kk), start=(kk == 0), stop=(kk == 7))
                op('dve', 'tensor_tensor', pk + [('xT', dc)], [('xT', dc)], out=xTc(dc), in0=ps[:], in1=xTc(dc), op=ALU.add)
            ring_release()

    BM = uf(0, 4096)
    BM4 = BM.rearrange("p (a two k) -> p a two k", two=2, k=256)
    qT3 = ubf(8192, 4096).rearrange("p (c t) -> p c t", t=512)
    NQS = 2

    def mk_qs(base):
        return dict(s=uf(base, 1024).rearrange("p (s k) -> p s k", k=256),
                    p=ubf(base + 1024, 1024).rearrange("p (s k) -> p s k", k=256),
                    pT=ubf(base + 1536, 1024),
                    cols=uf(base + 2048, 32))

    qsets = [mk_qs(10240 + i * 2080) for i in range(NQS)]
    otok = [ubf(14400 + n * 512, 1024) for n in range(4)]
    sink4 = sinkb[:].rearrange("p j (a two) -> p j a two", two=2)

    def swa_layer(L, t):
        j = L // 2
        phase_barrier()
        dma('pool', BM, bmfull_d.ap(), ['bmfull'], [uk('BM')], 'bm')
        rms_norm_to_h(nmix, 'nmix', L * 8)
        for u in range(2):
            wv, wk = ring_next('swq')
            w3 = wv.rearrange("p (k c) -> p k c", c=512)
            for ccl in range(4):
                cc = u * 4 + ccl
                ps, pk = big()
                for k in range(8):
                    op('pe', 'matmul', [wk, ('h', k)], pk, out=ps[:], lhsT=w3[:, k, ccl * 128:(ccl + 1) * 128], rhs=hc(k), start=(k == 0), stop=(k == 7))
                op('act', 'activation', pk + ['bq8'], [uk('qT', cc)], out=qT3[:, cc, :], in_=ps[:], func=AF.Identity, bias=bq8[:, j, cc:cc + 1], scale=0.125)
            ring_release()
        wv, wk = ring_next('swkv')
        w3 = wv.rearrange("p (k c) -> p k c", c=512)
        for kv in range(2):
            ps, pk = big()
            for k in range(8):
                op('pe', 'matmul', [wk, ('h', k)], pk, out=ps[:], lhsT=w3[:, k, kv * 128:(kv + 1) * 128], rhs=hc(k), start=(k == 0), stop=(k == 7))
            op('act', 'activation', pk + ['bk2'], [('kbuf', j)], out=kbuf[:, j, kv, 128:640], in_=ps[:], func=AF.Identity, bias=bk2[:, j, kv:kv + 1], scale=1.0)
        for s4 in range(4):
            psb, pk = big()
            psq = psb[:, 0:128]
            for k in range(8):
                op('pe', 'matmul', [wk, ('h', k)], pk, out=psq, lhsT=hc(k)[:, s4 * 128:(s4 + 1) * 128], rhs=w3[:, k, 256:384], start=(k == 0), stop=(k == 7))
            op('dve', 'tensor_tensor', pk + ['bvb'], [('vbuf', j)], out=vbuf[:, j, 1 + s4, :], in0=psq, in1=bvb[:, j, :], op=ALU.add)
        ring_release()

        qcount = [0]

        def quad(n, qd):
            qs = qsets[qcount[0] % NQS]
            qi_ = qcount[0] % NQS
            qcount[0] += 1
            first = False
            seq_start_blk = (t == 0 and n == 0)
            W0 = 128 if first else 0
            KW = 256 - W0
            kvh = qd // 2
            qk = lambda *nm: uk('qs', qi_, *nm)
            s_, p_, pT_, cols = qs['s'], qs['p'], qs['pT'], qs['cols']
            banks = [(0, 1), (2, 3)][qi_]
            for two in range(2):
                b = banks[two]
                for a in range(2):
                    cc = 2 * qd + a
                    op('pe', 'matmul', [uk('qT', cc), ('kbuf', j)], PQ(b), out=pb[b][:, a * 256 + W0:(a + 1) * 256],
                       lhsT=qT3[two * 64:(two + 1) * 64, cc, n * 128:(n + 1) * 128],
                       rhs=kbuf[two * 64:(two + 1) * 64, j, kvh, n * 128 + W0:n * 128 + 256], start=True, stop=True)
                op('dve', 'tensor_tensor', PQ(b) + [uk('BM')], [qk('s')], out=s_[:, 2 * two:2 * two + 2, W0:256],
                   in0=pb[b][:].rearrange("p (a k) -> p a k", k=256)[:, :, W0:256], in1=BM4[:, 2 * qd:2 * qd + 2, two, W0:256], op=ALU.add)
                if seq_start_blk:
                    op('dve', 'tensor_tensor', [qk('s'), 'fmask'], [qk('s')], out=s_[:, 2 * two:2 * two + 2, 0:128], in0=s_[:, 2 * two:2 * two + 2, 0:128],
                       in1=bass.AP(fmask, 0, [[128, 128], [0, 2], [1, 128]]), op=ALU.add)
            yield
            rmax = cols[:, 0:4]; mcol = cols[:, 4:8]; negm = cols[:, 8:12]; rsum = cols[:, 12:16]
            tmp4 = cols[:, 16:20]; esk = cols[:, 20:24]; den = cols[:, 24:28]; rinv = cols[:, 28:32]
            sk4 = sink4[:, j, 2 * qd:2 * qd + 2, :].rearrange("p a two -> p two a")
            op('dve', 'tensor_reduce', [qk('s')], [qk('rmax')], out=rmax, in_=s_[:, :, W0:256], axis=AX.X, op=ALU.max)
            op('dve', 'tensor_tensor', [qk('rmax'), 'sinkb'], [qk('m')], out=mcol.rearrange("p (two a) -> p two a", a=2),
               in0=rmax.rearrange("p (two a) -> p two a", a=2), in1=sk4, op=ALU.max)
            op('dve', 'tensor_scalar', [qk('m')], [qk('negm')], out=negm, in0=mcol, scalar1=-1.0, scalar2=None, op0=ALU.mult)
            for sl_ in range(4):
                op('act', 'activation', [qk('s'), qk('negm')], [qk('p', sl_), qk('rsum', sl_)], out=p_[:, sl_, W0:256], in_=s_[:, sl_, W0:256],
                   func=AF.Exp, bias=negm[:, sl_:sl_ + 1], scale=1.0, accum_out=rsum[:, sl_:sl_ + 1])
            op('dve', 'tensor_tensor', [qk('negm'), 'sinkb'], [qk('tmp4')], out=tmp4.rearrange("p (two a) -> p two a", a=2),
               in0=negm.rearrange("p (two a) -> p two a", a=2), in1=sk4, op=ALU.add)
            op('act', 'activation', [qk('tmp4')], [qk('esk')], out=esk, in_=tmp4, func=AF.Exp)
            op('dve', 'tensor_tensor', [qk('esk')] + [qk('rsum', i) for i in range(4)], [qk('den')], out=den, in0=rsum, in1=esk, op=ALU.add)
            op('dve', 'reciprocal', [qk('den')], [qk('rinv')], out=rinv, in_=den)
            yield
            tb = 4 + qi_
            ptp = pb[tb][:].bitcast(BF16)
            halves = [1] if first else [0, 1]
            for sl_ in range(4):
                for hf in halves:
                    op('pe', 'transpose', [qk('p', sl_), 'identb'], PQ(tb), out=ptp[:, (sl_ * 2 + hf) * 128:(sl_ * 2 + hf + 1) * 128],
                       in_=p_[:, sl_, hf * 128:(hf + 1) * 128], identity=identb[:])
            if first:
                for sl_ in range(4):
                    op('act', 'activation', PQ(tb), [qk('pT')], out=pT_[:, (sl_ * 2 + 1) * 128:(sl_ * 2 + 2) * 128],
                       in_=ptp[:, (sl_ * 2 + 1) * 128:(sl_ * 2 + 2) * 128], func=AF.Copy)
            else:
                op('act', 'activation', PQ(tb), [qk('pT')], out=pT_, in_=ptp, func=AF.Copy)
            yield
            pvk = PQ(6 + qi_)
            pv = pb[6 + qi_][:, 0:256]
            for sl_ in range(4):
                for hf in halves:
                    op('pe', 'matmul', [qk('pT'), ('vbuf', j)], pvk, out=pv[:, sl_ * 64:(sl_ + 1) * 64],
                       lhsT=pT_[:, (sl_ * 2 + hf) * 128:(sl_ * 2 + hf + 1) * 128], rhs=vbuf[:, j, n + hf, kvh * 64:(kvh + 1) * 64],
                       start=(hf == halves[0]), stop=(hf == 1))
            o4 = otok[n].rearrange("p (a two d) -> p two a d", two=2, d=64)[:, :, 2 * qd:2 * qd + 2, :]
            rinv_b = bass.AP(uni, rinv.offset, [[uni.shape[1], 128], [2, 2], [1, 2], [0, 64]])
            op('dve', 'tensor_tensor', pvk + [qk('rinv')], [uk('otok', n, qd)], out=o4, in0=pv.rearrange("p (two a d) -> p two a d", two=2, d=64),
               in1=rinv_b, op=ALU.mult)
            yield

        for n in range(4):
            gl = [quad(n, qd) for qd in range(4)]
            run_interleaved(gl[0:2])
            run_interleaved(gl[2:4])
            otp = pb[4][:].bitcast(BF16)
            for cc in range(8):
                op('pe', 'transpose', [uk('otok', n, cc // 2), 'identb'], PQ(4), out=otp[:, cc * 128:(cc + 1) * 128],
                   in_=otok[n][:, cc * 128:(cc + 1) * 128], identity=identb[:])
            op('act', 'activation', PQ(4), [('mo', cc) for cc in range(8)], out=mo[:].rearrange("p (c t) -> p c t", t=512)[:, :, n * 128:(n + 1) * 128],
               in_=otp.rearrange("p (c t) -> p c t", t=128), func=AF.Copy)
        op('pool', 'tensor_copy', [('kbuf', j)], [('kbuf', j)], out=kbuf[:, j, :, 0:128], in_=kbuf[:, j, :, 512:640])
        op('pool', 'tensor_copy', [('vbuf', j)], [('vbuf', j)], out=vbuf[:, j, 0, :], in_=vbuf[:, j, 4, :])
        for u in range(2):
            wv, wk = ring_next('swout')
            w3 = wv.rearrange("p (k c) -> p k c", c=512)
            for dcl in range(4):
                dc = u * 4 + dcl
                ps, pk = big()
                for kk in range(8):
                    op('pe', 'matmul', [wk, ('mo', kk)], pk, out=ps[:], lhsT=w3[:, kk, dcl * 128:(dcl + 1) * 128], rhs=moc(kk), start=(kk == 0), stop=(kk == 7))
                op('dve', 'scalar_tensor_tensor', pk + [('xT', dc), 'bo'], [('xT', dc)], out=xTc(dc), in0=ps[:], scalar=bo[:, j, dc:dc + 1],
                   op0=ALU.add, in1=xTc(dc), op1=ALU.add)
            ring_release()

    phase_barrier()
    for t in range(NT):
        load_x_tile(t)
        for L in range(n_layers):
            if dn and L % 2 == 0:
                dn_layer(L, t)
            if swa and L % 2 == 1:
                swa_layer(L, t)
            mlp(L)
        store_out_tile(t)
    dma('pool', stS_o.ap(), Sf[:].rearrange("p a b -> p (a b)"), [('Sf', i) for i in range(16)], ['so0'], 'so0')
    dma('pool', stH_o.ap().rearrange("p (a b) -> p a b", b=4), halo[:], [('halo', i) for i in range(48)], ['so1'], 'so1')
    for jj in range(2):
        dma('pool', stK_o.ap().rearrange("p (j k c) -> p j k c", j=2, k=2)[:, jj, :, :], kbuf[:, jj, :, 0:128], [('kbuf', jj)], ['so2%d' % jj], 'so2%d' % jj)
        dma('pool', stV_o.ap().rearrange("p (j c) -> p j c", j=2)[:, jj, :], vbuf[:, jj, 0, :], [('vbuf', jj)], ['so3%d' % jj], 'so3%d' % jj)
    S_.add('pool', lambda e: None, [('out', 0), ('out', 1), 'so0', 'so1', 'so20', 'so21', 'so30', 'so31'], [], reg=False)
    assert ring['used'] == total_units, (ring['used'], total_units)
    S_.emit(nc, st)
    st.close()
    return nc, S_


_CACHE = {}
SL = 8192


def kernel(**inputs):
    import ml_dtypes
    x = np.asarray(inputs['x'], np.float32)
    B, S, _ = x.shape
    sl = min(SL, S)
    if sl not in _CACHE:
        _CACHE[sl] = build_program(sl)
    nc, _ = _CACHE[sl]
    shared = {}
    for k, v in inputs.items():
        if k == 'x':
            continue
        a = np.ascontiguousarray(np.asarray(v, np.float32))
        if k == 'norm_final':
            a = a.reshape(1, D)
        shared[k] = a
    shared.update(host_constants())
    ncore = B
    state = [dict(st_S_in=np.zeros((128, 2048), np.float32), st_halo_in=np.zeros((128, 192), np.float32),
                  st_k_in=np.zeros((128, 512), ml_dtypes.bfloat16), st_vv_in=np.zeros((128, 256), ml_dtypes.bfloat16))
             for _ in range(ncore)]
    out = np.empty((B, S, D), np.float32)
    for li in range(S // sl):
        in_maps = []
        for c in range(ncore):
            m = dict(shared)
            m['x'] = np.ascontiguousarray(x[c, li * sl:(li + 1) * sl])
            m['first_mask'] = np.full((128, 128), NEG if li == 0 else 0.0, np.float32)
            m.update(state[c])
            in_maps.append(m)
        res = run_bass_kernel_spmd(nc, in_maps, core_ids=list(range(ncore)))
        for c in range(ncore):
            r = res.results[c]
            out[c, li * sl:(li + 1) * sl] = np.asarray(r['out'], np.float32)
            state[c] = dict(st_S_in=np.asarray(r['st_S_out']), st_halo_in=np.asarray(r['st_halo_out']),
                            st_k_in=np.asarray(r['st_k_out']), st_vv_in=np.asarray(r['st_vv_out']))
    return out
```

```python
import math
import numpy as np
from contextlib import ExitStack
import concourse.bass as bass
import concourse.mybir as mybir
from concourse.bass_utils import run_bass_kernel_spmd

F32 = mybir.dt.float32
F32R = mybir.dt.float32r
BF16 = mybir.dt.bfloat16
AF = mybir.ActivationFunctionType
ALU = mybir.AluOpType
AX = mybir.AxisListType

D = 1024
T = 512
NSLOT = 4
EPS = 1e-6
NEG = -1.0e30
SEM_LIM = 16000


class Sched:
    def __init__(self):
        self.ops = []

    def add(self, eng, fn, reads=(), writes=(), dma=None, reg=True):
        def isps(k):
            return isinstance(k, tuple) and len(k) > 1 and k[0] in ('pq', 'pb')
        banks = []
        for k in tuple(reads) + tuple(writes):
            if isps(k) and ('pb', k[1]) not in banks:
                banks.append(('pb', k[1]))
        reads = tuple(k for k in reads if not isps(k))
        writes = tuple(k for k in writes if not isps(k)) + tuple(banks)
        self.ops.append(dict(eng=eng, fn=fn, r=tuple(reads), w=tuple(writes), dma=dma, reg=reg))

    def barrier(self, keys):
        keys = tuple(keys)
        for eng in ('pe', 'act', 'dve', 'pool', 'sp'):
            self.add(eng, lambda e: None, reads=(), writes=keys, reg=False)

    def emit(self, nc, stack):
        ops = self.ops
        last_w = {}
        readers = {}
        deps = []
        for i, o in enumerate(ops):
            d = set()
            for k in o['r']:
                if k in last_w:
                    d.add(last_w[k])
            for k in o['w']:
                if k in last_w:
                    d.add(last_w[k])
                d.update(readers.get(k, ()))
            d.discard(i)
            dd = []
            rset = set(o['r'])
            for j in d:
                p = ops[j]
                if o['reg'] and p['dma'] is None and o['dma'] is None and p['eng'] == o['eng']:
                    if o['eng'] == 'pe':
                        continue
                dd.append(j)
            deps.append(dd)
            if o['reg']:
                for k in o['r']:
                    readers.setdefault(k, []).append(i)
                for k in o['w']:
                    last_w[k] = i
                    readers[k] = []
        signaled = set()
        for dd in deps:
            signaled.update(dd)
        sems = {}
        counts = {}
        ncnt = {}
        token = [None] * len(ops)
        for i, o in enumerate(ops):
            if o['dma'] is not None:
                nm = 'd_' + o['dma']
                counts[nm] = counts.get(nm, 0) + 16
                token[i] = (nm, counts[nm])
            elif i in signaled:
                base = 'e_' + o['eng']
                ncnt[base] = ncnt.get(base, 0) + 1
                gen = (ncnt[base] - 1) // SEM_LIM
                nm = '%s_%d' % (base, gen)
                counts[nm] = (ncnt[base] - 1) % SEM_LIM + 1
                token[i] = (nm, counts[nm])
        self.counts = counts
        per_eng = {}
        for i, o in enumerate(ops):
            per_eng.setdefault(o['eng'], []).append(i)
        for nm in counts:
            sems[nm] = stack.enter_context(nc.semaphore(nm))
        block = stack.enter_context(nc.Block())
        binder = dict(pe=block.tensor, act=block.scalar, dve=block.vector,
                      pool=block.gpsimd, sp=block.sync)
        nw = [0]

        def make(idxs):
            def body(e):
                seen = {}
                for i in idxs:
                    o = ops[i]
                    need = {}
                    for j in deps[i]:
                        nm, v = token[j]
                        if seen.get(nm, 0) >= v:
                            continue
                        if need.get(nm, 0) < v:
                            need[nm] = v
                    for nm, v in need.items():
                        e.wait_ge(sems[nm], v)
                        seen[nm] = v
                        nw[0] += 1
                    inst = o['fn'](e)
                    if token[i] is not None:
                        nm, v = token[i]
                        inst.then_inc(sems[nm], 16 if o['dma'] is not None else 1)
            return body

        for engname, idxs in per_eng.items():
            binder[engname](make(idxs))
        self.nwaits = nw[0]


def run_interleaved(gens):
    gens = list(gens)
    while gens:
        for g in list(gens):
            try:
                next(g)
            except StopIteration:
                gens.remove(g)


def t5_bucket(n):
    if n < 16:
        return n
    v = 16 + int(np.float32(np.log(np.float32(n) / np.float32(16.0))) / np.float32(math.log(8.0)) * np.float32(16.0))
    return min(v, 31)


def host_constants():
    c = {}
    c['c_ident'] = np.eye(128, dtype=np.float32)
    ii = np.arange(128)
    c['c_U'] = (ii[:, None] <= ii[None, :]).astype(np.float32)
    c['c_neglo'] = np.where(ii[:, None] > ii[None, :], 0.0, NEG).astype(np.float32)
    c['c_negup'] = np.where(ii[:, None] <= ii[None, :], 0.0, NEG).astype(np.float32)
    c['c_J'] = np.eye(128, dtype=np.float32)[::-1].copy()
    mk = np.zeros((4, 128, 128), np.float32)
    mk[0] = (ii[:, None] // 16 == ii[None, :] // 16)
    for l, s_ in enumerate((16, 32, 64)):
        mk[l + 1] = (ii[:, None] // (2 * s_) == ii[None, :] // (2 * s_)) & (ii[:, None] // s_ != ii[None, :] // s_)
    c['c_masks'] = np.ascontiguousarray(mk.transpose(1, 0, 2)).reshape(128, 512)
    oh = np.zeros((32, 384), np.float32)
    ng = np.full((16, 384), NEG, np.float32)
    for m in range(384):
        dist = 255 - m
        if 0 <= dist <= 127:
            oh[t5_bucket(dist), m] = 1.0
            ng[:, m] = 0.0
    c['c_onehot'] = oh
    c['c_negrow'] = ng
    return c


def build_program(S, n_layers=4, dn=True, swa=True):
    NT = S // T
    nc = bass.Bass("TRN2", target_bir_lowering=False)

    def din(name, shape):
        return nc.dram_tensor(name, list(shape), F32, kind="ExternalInput")

    x_d = din("x", [S, D])
    nmix_d = din("norm_mix", [4, D]); nmlp_d = din("norm_mlp", [4, D]); nfin_d = din("norm_final", [1, D])
    dnin_d = din("dn_w_in", [2, D, 4112]); dncw_d = din("dn_conv_w", [2, 4, 3072])
    alog_d = din("dn_a_log", [2, 8]); dtb_d = din("dn_dt_bias", [2, 8]); dnnw_d = din("dn_norm_w", [2, 128])
    dnout_d = din("dn_w_out", [2, D, D])
    swqkv_d = din("swa_w_qkv", [2, D, 1280]); swb_d = din("swa_b_qkv", [2, 1280]); sink_d = din("swa_sinks", [2, 16])
    swout_d = din("swa_w_out", [2, D, D]); swbo_d = din("swa_b_out", [2, D]); relb_d = din("rel_bias", [32, 16])
    up_d = din("mlp_w_up", [4, D, 4096]); down_d = din("mlp_w_down", [4, 4096, D])
    cid_d = din("c_ident", [128, 128]); cU_d = din("c_U", [128, 128]); cnl_d = din("c_neglo", [128, 128])
    cnu_d = din("c_negup", [128, 128]); cJ_d = din("c_J", [128, 128]); coh_d = din("c_onehot", [32, 384])
    cng_d = din("c_negrow", [16, 384])
    cmk_d = din("c_masks", [128, 512])
    out_d = nc.dram_tensor("out", [S, D], F32, kind="ExternalOutput")
    stS_i = din("st_S_in", [128, 2048]); stH_i = din("st_halo_in", [128, 192]); fmask_d = din("first_mask", [128, 128])
    stK_i = nc.dram_tensor("st_k_in", [128, 512], BF16, kind="ExternalInput")
    stV_i = nc.dram_tensor("st_vv_in", [128, 256], BF16, kind="ExternalInput")
    stS_o = nc.dram_tensor("st_S_out", [128, 2048], F32, kind="ExternalOutput")
    stH_o = nc.dram_tensor("st_halo_out", [128, 192], F32, kind="ExternalOutput")
    stK_o = nc.dram_tensor("st_k_out", [128, 512], BF16, kind="ExternalOutput")
    stV_o = nc.dram_tensor("st_vv_out", [128, 256], BF16, kind="ExternalOutput")

    units = []
    for L in range(n_layers):
        j = L // 2
        if dn and L % 2 == 0:
            for hd in range(8):
                units.append(('dnin', L, hd))
            for u in range(2):
                units.append(('dnout', L, u))
        if swa and L % 2 == 1:
            for u in range(2):
                units.append(('swq', L, u))
            units.append(('swkv', L, 0))
            for u in range(2):
                units.append(('swout', L, u))
        units.append(('up', L, 0))
        for g in range(8):
            if g < 7:
                units.append(('up', L, g + 1))
            units.append(('down', L, g))
    NU = len(units)
    wsc_d = nc.dram_tensor("wsc", [NU, 128, 4096], BF16, kind="Internal")
    bm_d = nc.dram_tensor("bmscr", [16, 384], F32, kind="Internal")

    S_ = Sched()
    st = ExitStack()

    def sb(name, shape, dt=F32):
        return st.enter_context(nc.sbuf_tensor(name, list(shape), dt))

    identf = sb("identf", [128, 128]); identb = sb("identb", [128, 128], BF16)
    onesf = sb("onesf", [128, 128]); negonesf = sb("negonesf", [128, 128]); onesb = sb("onesb", [128, 128], BF16)
    fmask = sb("fmask", [128, 128]); cmask = sb("cmask", [128, 512]); Umask = sb("Umask", [128, 128]); neglo = sb("neglo", [128, 128]); negup = sb("negup", [128, 128])
    nmix = sb("nmix", [128, 32]); nmlp = sb("nmlp", [128, 32]); nfin = sb("nfin", [128, 8])
    cwt = sb("cwt", [128, 2, 96])
    nexpA = sb("nexpA", [128, 2, 8]); dtb = sb("dtb", [128, 2, 8]); dnnw = sb("dnnw", [128, 2])
    wba = sb("wba", [128, 2, 8, 16], BF16)
    bq8 = sb("bq8", [128, 2, 8]); bk2 = sb("bk2", [128, 2, 2]); bvb = sb("bvb", [128, 2, 128])
    sinkb = sb("sinkb", [128, 2, 16]); bo = sb("bo", [128, 2, 8])
    xT = sb("xT", [128, 4096]); hh = sb("hh", [128, 4096], BF16); mo = sb("mo", [128, 4096], BF16)
    ag = sb("ag", [128, 2, 2048], BF16)
    rlb = sb("rlb", [128, 2, 512])
    wr = sb("wr", [128, NSLOT, 4096], BF16)
    xs = sb("xs", [128, 2, 1024])
    Sf = sb("Sf", [128, 16, 128]); Sb = sb("Sb", [128, 16, 128], BF16)
    halo = sb("halo", [128, 48, 4])
    kbuf = sb("kbuf", [128, 2, 2, 640], BF16)
    vbuf = sb("vbuf", [128, 2, 5, 128], BF16)
    sqb = sb("sqb", [128, 2, 512], BF16); rt = sb("rt", [128, 512]); rstd = sb("rstd", [128, 512])
    rpool = sb("rpool", [128, 4, 12, 128], F32R)
    uni = sb("uni", [128, 16512])
    pb = [st.enter_context(nc.psum_tensor("pb%d" % b, [128, 512], F32)) for b in range(8)]

    def xTc(c):
        return xT[:, c * 512:(c + 1) * 512]

    def hc(c):
        return hh[:, c * 512:(c + 1) * 512]

    def moc(c):
        return mo[:, c * 512:(c + 1) * 512]

    def PQ(b):
        return [('pq', b, q) for q in range(4)]

    dcount = [0]

    def dma(eng, out, in_, reads, writes, key):
        writes = list(writes)
        if key.startswith('c'):
            writes.append('chain_' + key)
        S_.add(eng, lambda e: e.dma_start(out=out, in_=in_), reads=reads, writes=writes, dma=key)

    def DAP(t, off, dims):
        return bass.AP(t, off, [list(d) for d in dims])

    dma('pool', identf[:], cid_d.ap(), [], ['identf'], 'c0')
    dma('pool', Umask[:], cU_d.ap(), [], ['Umask'], 'c1')
    dma('pool', neglo[:], cnl_d.ap(), [], ['neglo'], 'c2')
    dma('pool', negup[:], cnu_d.ap(), [], ['negup'], 'c3')
    dma('pool', cmask[:], cmk_d.ap(), [], ['cmask'], 'c3')
    S_.add('pool', lambda e: e.tensor_copy(out=identb[:], in_=identf[:]), ['identf'], ['identb'])
    S_.add('pool', lambda e: e.memset(onesf[:], 1.0), [], ['onesf'])
    S_.add('pool', lambda e: e.memset(negonesf[:], -1.0), [], ['negonesf'])
    S_.add('pool', lambda e: e.memset(onesb[:], 1.0), [], ['onesb'])
    dma('pool', fmask[:], fmask_d.ap(), [], ['fmask'], 'c3')
    dma('pool', Sf[:].rearrange("p a b -> p (a b)"), stS_i.ap(), [], [('Sf', i) for i in range(16)], 'c11')
    S_.add('act', lambda e: e.activation(out=Sb[:].rearrange("p a b -> p (a b)"), in_=Sf[:].rearrange("p a b -> p (a b)"), func=AF.Copy),
           [('Sf', i) for i in range(16)], [('Sb', i) for i in range(16)])
    S_.add('pool', lambda e: e.memset(halo[:], 0.0), [], [('halo', i) for i in range(48)])
    dma('pool', halo[:, :, 0:3], stH_i.ap().rearrange("p (a b) -> p a b", b=4)[:, :, 0:3], [], [('halo', i) for i in range(48)], 'c11')
    S_.add('pool', lambda e: e.memset(kbuf[:], 0.0), [], [('kbuf', jj) for jj in range(2)])
    S_.add('pool', lambda e: e.memset(vbuf[:], 0.0), [], [('vbuf', jj) for jj in range(2)])
    for jj in range(2):
        dma('pool', kbuf[:, jj, :, 0:128], stK_i.ap().rearrange("p (j k c) -> p j k c", j=2, k=2)[:, jj, :, :], [], [('kbuf', jj)], 'c11')
        dma('pool', vbuf[:, jj, 0, :], stV_i.ap().rearrange("p (j c) -> p j c", j=2)[:, jj, :], [], [('vbuf', jj)], 'c11')
    tmpA = uni[:, 0:128]
    tmpB = uni[:, 128:256]

    def load_cols(dst_ap, src_t, nrows, key):
        dma('pool', tmpA[0:nrows, :], DAP(src_t, 0, [[128, nrows], [1, 128]]), [], [('u', 'tmpA')], 'c4')
        S_.add('pe', lambda e: e.transpose(out=pb[0][:, 0:nrows], in_=tmpA[0:nrows, :], identity=identf[0:nrows, 0:nrows]),
               [('u', 'tmpA'), 'identf'], PQ(0))
        S_.add('act', lambda e: e.activation(out=dst_ap, in_=pb[0][:, 0:nrows], func=AF.Copy), PQ(0), [key])

    load_cols(nmix[:], nmix_d, 32, 'nmix')
    load_cols(nmlp[:], nmlp_d, 32, 'nmlp')
    load_cols(nfin[:], nfin_d, 8, 'nfin')
    for jj in range(2):
        dma('pool', tmpA[0:96, :], DAP(dncw_d, jj * 4 * 3072, [[128, 96], [1, 128]]), [], [('u', 'tmpA')], 'c4')
        S_.add('pe', lambda e: e.transpose(out=pb[0][:, 0:96], in_=tmpA[0:96, :], identity=identf[0:96, 0:96]),
               [('u', 'tmpA'), 'identf'], PQ(0))
        S_.add('act', lambda e, jj=jj: e.activation(out=cwt[:, jj, :], in_=pb[0][:, 0:96], func=AF.Copy), PQ(0), ['cwt'])
    load_cols(dnnw[:], dnnw_d, 2, 'dnnw')
    load_cols(bo[:].rearrange("p a b -> p (a b)"), swbo_d, 16, 'bo')
    dma('pool', tmpA[0:20, :], DAP(swb_d, 0, [[128, 20], [1, 128]]), [], [('u', 'tmpA')], 'c4')
    S_.add('pe', lambda e: e.transpose(out=pb[0][:, 0:20], in_=tmpA[0:20, :], identity=identf[0:20, 0:20]),
           [('u', 'tmpA'), 'identf'], PQ(0))
    for jj in range(2):
        S_.add('act', lambda e, jj=jj: e.activation(out=bq8[:, jj, :], in_=pb[0][:, jj * 10:jj * 10 + 8], func=AF.Copy, scale=0.125),
               PQ(0), ['bq8'])
    for jj in range(2):
        for kv in range(2):
            for half in range(2):
                dma('pool', bk2[half * 64:(half + 1) * 64, jj, kv:kv + 1],
                    DAP(swb_d, jj * 1280 + 1024 + kv * 64, [[1, 64], [1, 1]]), [], ['bk2'], 'c5')
        dma('pool', bvb[:, jj, :], DAP(swb_d, jj * 1280 + 1152, [[0, 128], [1, 128]]), [], ['bvb'], 'c5')
        dma('pool', sinkb[:, jj, :], DAP(sink_d, jj * 16, [[0, 128], [1, 16]]), [], ['sinkb'], 'c5')
        dma('pool', dtb[:, jj, :], DAP(dtb_d, jj * 8, [[0, 128], [1, 8]]), [], ['dtb'], 'c5')
        dma('pool', nexpA[:, jj, :], DAP(alog_d, jj * 8, [[0, 128], [1, 8]]), [], ['nexpA0'], 'c5')
    S_.add('act', lambda e: e.activation(out=nexpA[:], in_=nexpA[:], func=AF.Exp), ['nexpA0'], ['nexpA1'])
    S_.add('dve', lambda e: e.tensor_scalar(out=nexpA[:], in0=nexpA[:], scalar1=-1.0, scalar2=None, op0=ALU.mult),
           ['nexpA1'], ['nexpA'])
    for jj in range(2):
        dma('pool', tmpB[:, 0:128].rearrange("p (k c) -> p k c", c=16),
            DAP(dnin_d, jj * D * 4112 + 4096, [[4112, 128], [128 * 4112, 8], [1, 16]]), [], [('u', 'tmpB')], 'c6')
        S_.add('dve', lambda e, jj=jj: e.tensor_copy(out=wba[:, jj, :, :], in_=tmpB[:, 0:128].rearrange("p (k c) -> p k c", c=16)),
               [('u', 'tmpB')], ['wba'])

    cast_rr = [0]

    def prepass():
        hu = 0
        for uid, (kind, L, idx) in enumerate(units):
            j = L // 2
            for hf in range(2):
                s = hu % 2
                hu += 1
                stg = xT[:, s * 2048:(s + 1) * 2048]
                bst = hh[:, s * 2048:(s + 1) * 2048]
                skeys = [('xT', 4 * s + i) for i in range(4)]
                bkeys = [('h', 4 * s + i) for i in range(4)]
                if kind == 'dnin':
                    for f2 in range(2):
                        f = 2 * hf + f2
                        dma('sp', stg[:, f2 * 1024:(f2 + 1) * 1024].rearrange("p (k c) -> p k c", c=128),
                            DAP(dnin_d, j * D * 4112 + f * 1024 + idx * 128, [[4112, 128], [128 * 4112, 8], [1, 128]]),
                            [], skeys, 'pp%d' % s)
                elif kind in ('dnout', 'swq', 'swout', 'up'):
                    src, rs, base = {'dnout': (dnout_d, D, j * D * D), 'swq': (swqkv_d, 1280, j * D * 1280),
                                     'swout': (swout_d, D, j * D * D), 'up': (up_d, 4096, L * D * 4096)}[kind]
                    dma('sp', stg.rearrange("p (k c) -> p k c", c=512),
                        DAP(src, base + (4 * hf) * 128 * rs + idx * 512, [[rs, 128], [128 * rs, 4], [1, 512]]),
                        [], skeys, 'pp%d' % s)
                elif kind == 'down':
                    dma('sp', stg.rearrange("p (k c) -> p k c", c=1024),
                        DAP(down_d, L * 4096 * D + ((idx * 4 + 2 * hf) * 128) * D, [[D, 128], [128 * D, 2], [1, D]]),
                        [], skeys, 'pp%d' % s)
                elif kind == 'swkv':
                    s3 = stg.rearrange("p (k c) -> p k c", c=512)
                    for (c0, w, col) in ((0, 64, 1024), (64, 64, 1024), (128, 64, 1088), (192, 64, 1088), (256, 128, 1152)):
                        dma('sp', s3[:, :, c0:c0 + w],
                            DAP(swqkv_d, j * D * 1280 + (4 * hf) * 128 * 1280 + col, [[1280, 128], [128 * 1280, 4], [1, w]]),
                            [], skeys, 'pp%d' % s)
                    S_.add('pool', lambda e, s3=s3: e.memset(s3[:, :, 384:512], 0.0), [], skeys)
                eng = ('act', 'dve', 'pool')[cast_rr[0] % 3]
                cast_rr[0] += 1
                if eng == 'act':
                    S_.add('act', lambda e, bst=bst, stg=stg: e.activation(out=bst, in_=stg, func=AF.Copy), skeys, bkeys)
                else:
                    S_.add(eng, lambda e, bst=bst, stg=stg: e.tensor_copy(out=bst, in_=stg), skeys, bkeys)
                dma('sp', DAP(wsc_d, uid * 128 * 4096 + hf * 2048, [[4096, 128], [1, 2048]]), bst,
                    bkeys, [('wsc', uid)], 'ps%d' % s)

    prepass()

    total_units = NU * NT
    ring = dict(loaded=0, used=0)

    def ring_load():
        gu = ring['loaded']
        if gu >= total_units:
            return
        slot = gu % NSLOT
        uid = gu % NU
        dma('sp', wr[:, slot, :], DAP(wsc_d, uid * 128 * 4096, [[4096, 128], [1, 4096]]),
            [('wsc', uid)], [('wr', slot)], 'w%d' % slot)
        ring['loaded'] += 1

    def ring_next(expect):
        gu = ring['used']
        assert units[gu % NU][0] == expect, (units[gu % NU], expect)
        ring['used'] += 1
        slot = gu % NSLOT
        return wr[:, slot, :], ('wr', slot)

    def ring_release():
        ring_load()

    for _ in range(NSLOT):
        ring_load()

    rot = dict(big=0, q=0)

    def big(banks=(0, 1)):
        b = banks[rot['big'] % len(banks)]
        rot['big'] += 1
        return pb[b], PQ(b)

    def rms_stats():
        ps, pk = big()
        for c in range(8):
            sq = sqb[:, c % 2, :]
            S_.add('act', lambda e, c=c, sq=sq: e.activation(out=sq, in_=xTc(c), func=AF.Square), [('xT', c)], [('sqb', c % 2)])
            S_.add('pe', lambda e, c=c, sq=sq, ps=ps: e.matmul(ps[:], lhsT=onesb[:], rhs=sq, start=(c == 0), stop=(c == 7)),
                   [('sqb', c % 2), 'onesb'], pk)
        S_.add('act', lambda e, ps=ps: e.activation(out=rt[:], in_=ps[:], func=AF.Ln, bias=EPS, scale=1.0 / D), pk, ['rt'])
        S_.add('act', lambda e: e.activation(out=rstd[:], in_=rt[:], func=AF.Exp, scale=-0.5), ['rt'], ['rstd'])

    def rms_norm_to_h(wtile, wkey, col0):
        rms_stats()
        for c in range(8):
            S_.add('dve', lambda e, c=c: e.scalar_tensor_tensor(out=hc(c), in0=xTc(c), scalar=wtile[:, col0 + c:col0 + c + 1],
                                                              op0=ALU.mult, in1=rstd[:], op1=ALU.mult),
                   [('xT', c), wkey, 'rstd'], [('h', c)])

    def load_x_tile(t):
        for s4 in range(4):
            sl = s4 % 2
            r0 = t * T + s4 * 128
            dma('pool', xs[:, sl, :], DAP(x_d, r0 * D, [[D, 128], [1, D]]), [], [('xs', sl)], 'xs%d' % sl)
            for hf in range(2):
                ps, pk = big()
                for cl in range(4):
                    c = hf * 4 + cl
                    S_.add('pe', lambda e, ps=ps, cl=cl, c=c, sl=sl: e.transpose(out=ps[:, cl * 128:(cl + 1) * 128],
                                                                               in_=xs[:, sl, c * 128:(c + 1) * 128], identity=identf[:]),
                           [('xs', sl), 'identf'], pk)
                dst = xT[:, hf * 2048:(hf + 1) * 2048].rearrange("p (c t) -> p c t", t=512)[:, :, s4 * 128:(s4 + 1) * 128]
                S_.add('act', lambda e, ps=ps, dst=dst: e.activation(out=dst, in_=ps[:].rearrange("p (c t) -> p c t", t=128), func=AF.Copy),
                       pk, [('xT', hf * 4 + i) for i in range(4)])

    def store_out_tile(t):
        rms_stats()
        for c in range(8):
            S_.add('dve', lambda e, c=c: e.scalar_tensor_tensor(out=xTc(c), in0=xTc(c), scalar=nfin[:, c:c + 1],
                                                              op0=ALU.mult, in1=rstd[:], op1=ALU.mult),
                   [('xT', c), 'nfin', 'rstd'], [('xT', c)])
        for s4 in range(4):
            sl = s4 % 2
            for hf in range(2):
                ps, pk = big()
                for cl in range(4):
                    c = hf * 4 + cl
                    S_.add('pe', lambda e, ps=ps, cl=cl, c=c, s4=s4: e.transpose(out=ps[:, cl * 128:(cl + 1) * 128],
                                                                               in_=xTc(c)[:, s4 * 128:(s4 + 1) * 128], identity=identf[:]),
                           [('xT', c), 'identf'], pk)
                S_.add('act', lambda e, ps=ps, sl=sl, hf=hf: e.activation(out=xs[:, sl, hf * 512:(hf + 1) * 512], in_=ps[:], func=AF.Copy),
                       pk, [('xs', sl)])
            r0 = t * T + s4 * 128
            dma('pool', DAP(out_d, r0 * D, [[D, 128], [1, D]]), xs[:, sl, :], [('xs', sl)], [('out', sl)], 'os%d' % sl)

    def mlp(L):
        rms_norm_to_h(nmlp, 'nmlp', L * 8)

        def do_up(g):
            wu, wuk = ring_next('up')
            assert units[(ring['used'] - 1) % NU][2] == g
            wu3 = wu.rearrange("p (k c) -> p k c", c=512)
            a = ag[:, g % 2, :]
            for fl in range(4):
                ps, pk = big((0, 1, 2, 3))
                for k in range(8):
                    S_.add('pe', lambda e, ps=ps, wu3=wu3, k=k, fl=fl: e.matmul(ps[:], lhsT=wu3[:, k, fl * 128:(fl + 1) * 128], rhs=hc(k),
                                                                              start=(k == 0), stop=(k == 7)),
                           [wuk, ('h', k)], pk)
                asl = a[:, fl * 512:(fl + 1) * 512]
                akey = ('ag', g % 2, fl)
                rl = rlb[:, fl % 2, :]
                S_.add('act', lambda e, ps=ps, rl=rl: e.activation(out=rl, in_=ps[:], func=AF.Relu), pk, [('rlb', fl % 2)])
                S_.add('pool', lambda e, asl=asl, rl=rl: e.tensor_tensor(out=asl, in0=rl, in1=rl, op=ALU.mult), [('rlb', fl % 2)], [akey])
            ring_release()

        def do_down(g):
            wd, wdk = ring_next('down')
            assert units[(ring['used'] - 1) % NU][2] == g
            wd3 = wd.rearrange("p (f c) -> p f c", c=1024)
            a = ag[:, g % 2, :]
            for dc in range(8):
                ps, pk = big((4, 5, 6, 7))
                for fl in range(4):
                    S_.add('pe', lambda e, ps=ps, wd3=wd3, fl=fl, dc=dc, a=a: e.matmul(ps[:], lhsT=wd3[:, fl, dc * 128:(dc + 1) * 128],
                                                                                   rhs=a[:, fl * 512:(fl + 1) * 512],
                                                                                   start=(fl == 0), stop=(fl == 3)),
                           [wdk, ('ag', g % 2, fl)], pk)
                S_.add('dve', lambda e, ps=ps, dc=dc: e.tensor_tensor(out=xTc(dc), in0=ps[:], in1=xTc(dc), op=ALU.add),
                       pk + [('xT', dc)], [('xT', dc)])
            ring_release()

        do_up(0)
        for g in range(8):
            if g < 7:
                do_up(g + 1)
            do_down(g)

    ukeys = set([('u', 'tmpA'), ('u', 'tmpB')])

    def uk(*k):
        key = ('u',) + k
        ukeys.add(key)
        return key

    def uf(off, n):
        return uni[:, off:off + n]

    def ubf(off, n):
        return uni[:, off:off + n // 2].bitcast(BF16)

    def op(eng, method, reads, writes, **kw):
        S_.add(eng, lambda e: getattr(e, method)(**kw), reads, writes)

    def pq(b, q):
        return pb[b][:, q * 128:(q + 1) * 128], [('pq', b, q)]

    def phase_barrier():
        S_.barrier(sorted(ukeys, key=str))

    bmfull_d = nc.dram_tensor("bmfull", [128, 4096], F32, kind="Internal")
    if swa:
        Jm = uf(8192, 128); oh = uf(8320, 384); ngr = uf(8704, 384); relb = uf(9088, 16); rvec = uf(9104, 384)
        BMrev = uf(4096, 4096); BMsb = uf(0, 4096)
        dma('pool', Jm, cJ_d.ap(), [], [uk('Jm')], 'c7')
        dma('pool', oh[0:32, :], coh_d.ap(), [], [uk('oh')], 'c7')
        dma('pool', ngr[0:16, :], cng_d.ap(), [], [uk('ngr')], 'c7')
        dma('pool', relb[0:32, :], relb_d.ap(), [], [uk('relb')], 'c7')
        op('pe', 'matmul', [uk('relb'), uk('oh')], PQ(0), out=pb[0][0:16, 0:384], lhsT=relb[0:32, :], rhs=oh[0:32, :], start=True, stop=True)
        op('dve', 'tensor_tensor', PQ(0) + [uk('ngr')], [uk('rvec')], out=rvec[0:16, :], in0=pb[0][0:16, 0:384], in1=ngr[0:16, :], op=ALU.add)
        dma('pool', bm_d.ap(), rvec[0:16, :], [uk('rvec')], ['bm_d'], 'c8')
        for hh2 in range(2):
            dma('sp', BMrev[:, hh2 * 2048:(hh2 + 1) * 2048].rearrange("p (h k) -> p h k", k=256),
                DAP(bm_d, hh2 * 8 * 384, [[1, 128], [384, 8], [1, 256]]), ['bm_d'], [uk('BMrev', hh2)], 'c9')
        for g8 in range(8):
            ps, pk = big()
            op('pe', 'matmul', [uk('Jm'), uk('BMrev', g8 // 4)], pk, out=ps[:], lhsT=Jm, rhs=BMrev[:, g8 * 512:(g8 + 1) * 512], start=True, stop=True)
            op('act', 'activation', pk, [uk('BMsb', g8)], out=BMsb[:, g8 * 512:(g8 + 1) * 512], in_=ps[:], func=AF.Copy)
        dma('sp', bmfull_d.ap(), BMsb, [uk('BMsb', g8) for g8 in range(8)], ['bmfull'], 'c10')

    pre3 = uf(0, 1548).rearrange("p (f n) -> p f n", n=516)
    cv3 = uf(1548, 1536).rearrange("p (f n) -> p f n", n=512)
    sl3 = uf(3084, 1024).rearrange("p (f n) -> p f n", n=512)
    sg = uf(4108, 512); rt2 = uf(4620, 512); rs2 = uf(7692, 512)
    sq2 = ubf(5132, 512)
    vbb = [ubf(5388 + i * 256, 512) for i in range(2)]
    zsb = [ubf(5900 + i * 256, 512) for i in range(3)]
    qnb = [ubf(6668 + i * 256, 512) for i in range(2)]
    knb = [ubf(7180 + i * 256, 512) for i in range(2)]
    LNAMES = ['GU', 't1', 't2', 'egb']
    RNAMES = ['Q0', 'P0', 'Q1', 'P1', 'Q2', 'P2', 'R0', 'R1', 'T0', 'T1', 'kbg', 'bv']
    lset = [{n: uf(8204 + c * 512 + i * 128, 128) for i, n in enumerate(LNAMES)} for c in range(4)]
    for c in range(4):
        for i, n in enumerate(RNAMES):
            lset[c][n] = rpool[:, c, i, :].bitcast(F32)

    def mk_hout(base):
        d = {'u': uf(base, 128)}
        for i, n in enumerate(['attnT', 'kdec', 'qdec', 'wT']):
            d[n] = ubf(base + 128 + i * 64, 128)
        d['cols'] = uf(base + 384, 8)
        return d

    hout = [[mk_hout(10252 + (p_ * 4 + c) * 400) for c in range(4)] for p_ in range(2)]
    vnewb = [ubf(13452 + i * 64, 128) for i in range(2)]
    junk = ubf(13580, 128)
    onb = [ubf(13644 + i * 64, 128) for i in range(2)]
    scol = uf(13772, 8)
    beta3 = uf(13780, 32).rearrange("p (s h) -> p s h", h=8)
    g3 = uf(13812, 32).rearrange("p (s h) -> p s h", h=8)
    tg3 = uf(13844, 32).rearrange("p (s h) -> p s h", h=8)
    ee3 = uf(13876, 32).rearrange("p (s h) -> p s h", h=8)

    def dn_layer(L, t):
        j = L // 2
        phase_barrier()
        rms_norm_to_h(nmix, 'nmix', L * 8)
        psq, pk = pq(2, 0)
        for s4 in range(4):
            for k in range(8):
                op('pe', 'matmul', [('h', k), 'wba'], pk, out=psq[:, s4 * 16:(s4 + 1) * 16], lhsT=hc(k)[:, s4 * 128:(s4 + 1) * 128],
                   rhs=wba[:, j, k, :], start=(k == 0), stop=(k == 7))
        pba3 = psq[:, 0:64].rearrange("p (s c) -> p s c", c=16)
        op('act', 'activation', pk, [uk('ee')], out=ee3, in_=pba3[:, :, 0:8], func=AF.Exp, scale=-1.0)
        op('act', 'activation', [uk('ee')], [uk('ee')], out=ee3, in_=ee3, func=AF.Ln, bias=1.0)
        op('act', 'activation', [uk('ee')], [uk('beta')], out=beta3, in_=ee3, func=AF.Exp, scale=-1.0)
        op('dve', 'tensor_tensor', pk + ['dtb'], [uk('tg')], out=tg3, in0=pba3[:, :, 8:16],
           in1=bass.AP(dtb, j * 8, [[16, 128], [0, 4], [1, 8]]), op=ALU.add)
        op('act', 'activation', [uk('tg')], [uk('tg')], out=tg3, in_=tg3, func=AF.Exp)
        op('act', 'activation', [uk('tg')], [uk('tg')], out=tg3, in_=tg3, func=AF.Ln, bias=1.0)
        op('dve', 'tensor_tensor', [uk('tg'), 'nexpA'], [uk('g')], out=g3, in0=tg3,
           in1=bass.AP(nexpA, j * 8, [[16, 128], [0, 4], [1, 8]]), op=ALU.mult)

        def projconv(hd):
            par = hd % 2
            wv, wk = ring_next('dnin')
            w4 = wv.rearrange("p (f k c) -> p f k c", f=4, k=8)
            for fi in range(3):
                ps, pk = big()
                for k in range(8):
                    op('pe', 'matmul', [wk, ('h', k)], pk, out=ps[:], lhsT=w4[:, fi, k, :], rhs=hc(k), start=(k == 0), stop=(k == 7))
                hidx = (j * 8 + hd) * 3 + fi
                op('pool', 'tensor_copy', [('halo', hidx)], [uk('pre', fi)], out=pre3[:, fi, 0:3], in_=halo[:, hidx, 0:3])
                op('act', 'activation', pk, [uk('pre', fi)], out=pre3[:, fi, 3:515], in_=ps[:], func=AF.Copy)
                op('pool', 'tensor_copy', [uk('pre', fi)], [('halo', hidx)], out=halo[:, hidx, 0:3], in_=pre3[:, fi, 512:515])
                yield
                ch = fi * 8 + hd
                for tap in range(4):
                    wcol = cwt[:, j, tap * 24 + ch:tap * 24 + ch + 1]
                    if tap == 0:
                        op('dve', 'tensor_scalar', [uk('pre', fi), 'cwt'], [uk('cv', fi)], out=cv3[:, fi, :], in0=pre3[:, fi, 0:512],
                           scalar1=wcol, scalar2=None, op0=ALU.mult)
                    else:
                        op('dve', 'scalar_tensor_tensor', [uk('pre', fi), 'cwt', uk('cv', fi)], [uk('cv', fi)], out=cv3[:, fi, :],
                           in0=pre3[:, fi, tap:tap + 512], scalar=wcol, op0=ALU.mult, in1=cv3[:, fi, :], op1=ALU.add)
                yield
                op('act', 'activation', [uk('cv', fi)], [uk('sg')], out=sg, in_=cv3[:, fi, :], func=AF.Exp, scale=-1.0)
                op('act', 'activation', [uk('sg')], [uk('sg')], out=sg, in_=sg, func=AF.Ln, bias=1.0)
                op('act', 'activation', [uk('sg')], [uk('sg')], out=sg, in_=sg, func=AF.Exp, scale=-1.0)
                if fi < 2:
                    op('pool', 'tensor_tensor', [uk('cv', fi), uk('sg')], [uk('sl', fi)], out=sl3[:, fi, :], in0=cv3[:, fi, :], in1=sg, op=ALU.mult)
                else:
                    op('pool', 'tensor_tensor', [uk('cv', 2), uk('sg')], [uk('vb', par)], out=vbb[par], in0=cv3[:, 2, :], in1=sg, op=ALU.mult)
                yield
            ps, pk = big()
            for k in range(8):
                op('pe', 'matmul', [wk, ('h', k)], pk, out=ps[:], lhsT=w4[:, 3, k, :], rhs=hc(k), start=(k == 0), stop=(k == 7))
            ring_release()
            op('act', 'activation', pk, [uk('cv', 0)], out=cv3[:, 0, :], in_=ps[:], func=AF.Copy)
            op('act', 'activation', pk, [uk('sg')], out=sg, in_=ps[:], func=AF.Exp, scale=-1.0)
            op('act', 'activation', [uk('sg')], [uk('sg')], out=sg, in_=sg, func=AF.Ln, bias=1.0)
            op('act', 'activation', [uk('sg')], [uk('sg')], out=sg, in_=sg, func=AF.Exp, scale=-1.0)
            op('pool', 'tensor_tensor', [uk('cv', 0), uk('sg')], [uk('zs', hd % 3)], out=zsb[hd % 3], in0=cv3[:, 0, :], in1=sg, op=ALU.mult)
            yield
            for fi in range(2):
                op('act', 'activation', [uk('sl', fi)], [uk('sq2')], out=sq2, in_=sl3[:, fi, :], func=AF.Square)
                ps, pk = big()
                op('pe', 'matmul', [uk('sq2'), 'onesb'], pk, out=ps[:], lhsT=onesb[:], rhs=sq2, start=True, stop=True)
                op('act', 'activation', pk, [uk('rt2')], out=rt2, in_=ps[:], func=AF.Ln, bias=EPS)
                bias = -0.5 * math.log(128.0) if fi == 0 else 0.0
                op('act', 'activation', [uk('rt2')], [uk('rs2')], out=rs2, in_=rt2, func=AF.Exp, scale=-0.5, bias=bias)
                dst = qnb[par] if fi == 0 else knb[par]
                op('pool', 'tensor_tensor', [uk('sl', fi), uk('rs2')], [uk('qn' if fi == 0 else 'kn', par)], out=dst, in0=sl3[:, fi, :], in1=rs2, op=ALU.mult)
                yield

        def local(hd, c):
            par = hd % 2
            cs = slice(c * 128, (c + 1) * 128)
            ls = lset[c]
            ho = hout[par][c]
            lk = lambda n: uk('l', c, n)
            hk = lambda n: uk('ho', par, c, n)
            bcol = beta3[:, c, hd:hd + 1]
            gcol = g3[:, c, hd:hd + 1]
            qn_, kn_, vb_ = qnb[par], knb[par], vbb[par]
            lb = 2 + c
            tpf, tk = pq(lb, 0)
            tp = tpf.bitcast(BF16)
            op('pe', 'transpose', [uk('kn', par), 'identb'], tk, out=tp[:, 0:128], in_=kn_[:, cs], identity=identb[:])
            op('pe', 'transpose', [uk('vb', par), 'identb'], tk, out=tp[:, 128:256], in_=vb_[:, cs], identity=identb[:])
            op('pool', 'tensor_scalar', ['Umask', uk('g')], [lk('GU')], out=ls['GU'], in0=Umask[:], scalar1=gcol, scalar2=None, op0=ALU.mult)
            Ep, ek = pq(lb, 1)
            op('pe', 'matmul', [lk('GU'), 'onesf'], ek, out=Ep, lhsT=ls['GU'], rhs=onesf[:], start=True, stop=False)
            op('pe', 'matmul', [lk('GU'), 'negonesf'], ek, out=Ep, lhsT=negonesf[:], rhs=ls['GU'], start=False, stop=True)
            Bp, bk_ = pq(lb, 2)
            op('pe', 'matmul', [lk('GU'), 'onesf'], bk_, out=Bp, lhsT=onesf[:], rhs=ls['GU'], start=True, stop=True)
            cols = ho['cols']
            op('act', 'activation', bk_, [hk('gl')], out=cols[:, 0:1], in_=Bp[:, 127:128], func=AF.Copy)
            op('act', 'activation', bk_, [hk('egl')], out=cols[:, 1:2], in_=Bp[:, 127:128], func=AF.Exp)
            op('act', 'activation', bk_, [lk('egb')], out=ls['egb'], in_=Bp, func=AF.Exp)
            op('act', 'activation', ek, [hk('edl')], out=cols[:, 2:3], in_=Ep[:, 127:128], func=AF.Exp, scale=-1.0)
            op('act', 'activation', ek + [hk('gl')], [hk('eg')], out=cols[:, 3:4], in_=Ep[:, 127:128], func=AF.Exp, bias=cols[:, 0:1], scale=1.0)
            op('dve', 'tensor_tensor', ek + ['neglo'], [lk('t1')], out=ls['t1'], in0=Ep, in1=neglo[:], op=ALU.add)
            op('dve', 'scalar_tensor_tensor', ek + ['negup'], [lk('t2')], out=ls['t2'], in0=Ep, scalar=-1.0, op0=ALU.mult, in1=negup[:], op1=ALU.add)
            op('act', 'activation', [lk('t1')], [lk('t1')], out=ls['t1'], in_=ls['t1'], func=AF.Exp)
            op('act', 'activation', [lk('t2')], [lk('t2')], out=ls['t2'], in_=ls['t2'], func=AF.Exp)
            op('dve', 'tensor_scalar', tk + [uk('beta'), hk('eg')], [lk('kbg')], out=ls['kbg'].bitcast(F32R), in0=tp[:, 0:128],
               scalar1=bcol, scalar2=cols[:, 3:4], op0=ALU.mult, op1=ALU.mult)
            op('dve', 'tensor_scalar', tk + [hk('edl')], [hk('kdec')], out=ho['kdec'], in0=tp[:, 0:128], scalar1=cols[:, 2:3], scalar2=None, op0=ALU.mult)
            op('dve', 'tensor_scalar', tk + [uk('beta')], [lk('bv')], out=ls['bv'].bitcast(F32R), in0=tp[:, 128:256], scalar1=bcol, scalar2=None, op0=ALU.mult)
            yield
            KKp, kkk = pq(lb, 0)
            op('pe', 'matmul', [uk('kn', par)], kkk, out=KKp, lhsT=kn_[:, cs], rhs=kn_[:, cs], start=True, stop=True)
            KQp, kqk = pq(lb, 1)
            op('pe', 'matmul', [uk('kn', par), uk('qn', par)], kqk, out=KQp, lhsT=kn_[:, cs], rhs=qn_[:, cs], start=True, stop=True)
            op('dve', 'scalar_tensor_tensor', kkk + [uk('beta'), lk('t1')], [lk('Q0')], out=ls['Q0'].bitcast(F32R), in0=KKp, scalar=bcol,
               op0=ALU.mult, in1=ls['t1'], op1=ALU.mult)
            op('dve', 'tensor_tensor', kqk + [lk('t2')], [hk('attnT')], out=ho['attnT'], in0=KQp, in1=ls['t2'], op=ALU.mult)
            op('pool', 'tensor_tensor', [uk('qn', par), lk('egb')], [hk('qdec')], out=ho['qdec'], in0=qn_[:, cs], in1=ls['egb'], op=ALU.mult)
            yield
            Btp, btk = pq(lb, 2)
            op('pe', 'transpose', [lk('Q0'), 'identf'], btk, out=Btp, in_=ls['Q0'], identity=identf[:])
            op('act', 'activation', btk, [lk('P0')], out=ls['P0'].bitcast(F32R), in_=Btp, func=AF.Copy)
            NM = dict(Qa='Q1', Pa='P1', Qb='Q2', Pb='P2', Ra='R0', Rb='R1', Ta='T0', Tb='T1')
            tl = lambda n: ls[NM[n]]
            tkk = lambda n: lk(NM[n])
            r32 = lambda n: tl(n).bitcast(F32R)
            M = lambda l: cmask[:, l * 128:(l + 1) * 128]
            op('dve', 'tensor_tensor', [lk('Q0'), 'cmask'], [tkk('Qa')], out=r32('Qa'), in0=ls['Q0'], in1=M(0), op=ALU.mult)
            op('dve', 'tensor_tensor', [lk('P0'), 'cmask'], [tkk('Pa')], out=r32('Pa'), in0=ls['P0'], in1=M(0), op=ALU.mult)
            op('dve', 'tensor_tensor', [tkk('Qa'), 'identf'], [tkk('Ta')], out=r32('Ta'), in0=identf[:], in1=tl('Qa'), op=ALU.subtract)
            op('dve', 'tensor_tensor', [tkk('Pa'), 'identf'], [tkk('Ra')], out=r32('Ra'), in0=identf[:], in1=tl('Pa'), op=ALU.subtract)
            yield
            Qc, Pc, Qn_, Pn_, Rc, Rn_, Tc, Tn_ = 'Qa', 'Pa', 'Qb', 'Pb', 'Ra', 'Rb', 'Ta', 'Tb'
            for lev in range(3):
                Pps, ppk = pq(lb, 0)
                op('pe', 'matmul', [tkk(Qc), tkk(Pc)], ppk, out=Pps, lhsT=r32(Qc), rhs=r32(Pc), start=True, stop=True)
                Qps, qpk = pq(lb, 1)
                op('pe', 'matmul', [tkk(Qc), tkk(Pc)], qpk, out=Qps, lhsT=r32(Pc), rhs=r32(Qc), start=True, stop=True)
                op('act', 'activation', qpk, [tkk(Qn_)], out=r32(Qn_), in_=Qps, func=AF.Copy)
                op('act', 'activation', ppk, [tkk(Pn_)], out=r32(Pn_), in_=Pps, func=AF.Copy)
                yield
                Rps, rpk = pq(lb, 2)
                op('pe', 'matmul', [tkk(Qn_), tkk(Rc)], rpk, out=Rps, lhsT=r32(Qn_), rhs=r32(Rc), start=True, stop=True)
                Tps, tpk = pq(lb, 3)
                op('pe', 'matmul', [tkk(Pn_), tkk(Tc)], tpk, out=Tps, lhsT=r32(Pn_), rhs=r32(Tc), start=True, stop=True)
                op('dve', 'tensor_tensor', rpk + [tkk(Rc)], [tkk(Rn_)], out=r32(Rn_), in0=Rps, in1=tl(Rc), op=ALU.add)
                op('dve', 'tensor_tensor', tpk + [tkk(Tc)], [tkk(Tn_)], out=r32(Tn_), in0=Tps, in1=tl(Tc), op=ALU.add)
                yield
                Qc, Qn_ = Qn_, Qc
                Pc, Pn_ = Pn_, Pc
                Rc, Rn_ = Rn_, Rc
                Tc, Tn_ = Tn_, Tc
            Xs, X2s = Qn_, Pn_
            for lev in range(1, 4):
                last = (lev == 3)
                Xp, xk = pq(lb, 0)
                op('pe', 'matmul', [lk('Q0'), tkk(Rc)], xk, out=Xp, lhsT=ls['Q0'].bitcast(F32R), rhs=r32(Rc), start=True, stop=True)
                op('dve', 'tensor_tensor', xk + ['cmask'], [tkk(Xs)], out=r32(Xs), in0=Xp, in1=M(lev), op=ALU.mult)
                if not last:
                    X2p, x2k = pq(lb, 1)
                    op('pe', 'matmul', [lk('P0'), tkk(Tc)], x2k, out=X2p, lhsT=ls['P0'].bitcast(F32R), rhs=r32(Tc), start=True, stop=True)
                    op('dve', 'tensor_tensor', x2k + ['cmask'], [tkk(X2s)], out=r32(X2s), in0=X2p, in1=M(lev), op=ALU.mult)
                yield
                Yrp, yrk = pq(lb, 2)
                op('pe', 'matmul', [tkk(Tc), tkk(Xs)], yrk, out=Yrp, lhsT=r32(Tc), rhs=r32(Xs), start=True, stop=True)
                if not last:
                    Ytp, ytk = pq(lb, 3)
                    op('pe', 'matmul', [tkk(Rc), tkk(X2s)], ytk, out=Ytp, lhsT=r32(Rc), rhs=r32(X2s), start=True, stop=True)
                op('dve', 'tensor_tensor', yrk + [tkk(Rc)], [tkk(Rn_)], out=r32(Rn_), in0=tl(Rc), in1=Yrp, op=ALU.subtract)
                if not last:
                    op('dve', 'tensor_tensor', ytk + [tkk(Tc)], [tkk(Tn_)], out=r32(Tn_), in0=tl(Tc), in1=Ytp, op=ALU.subtract)
                yield
                Rc, Rn_ = Rn_, Rc
                Tc, Tn_ = Tn_, Tc
            Rf = NM[Rc]
            ups, upk = pq(lb, 0)
            op('pe', 'matmul', [lk(Rf), lk('bv')], upk, out=ups, lhsT=ls[Rf].bitcast(F32R), rhs=ls['bv'].bitcast(F32R), start=True, stop=True)
            wps, wpk = pq(lb, 1)
            op('pe', 'matmul', [lk(Rf), lk('kbg')], wpk, out=wps, lhsT=ls['kbg'].bitcast(F32R), rhs=ls[Rf].bitcast(F32R), start=True, stop=True)
            op('act', 'activation', upk, [hk('u')], out=ho['u'], in_=ups, func=AF.Copy)
            op('act', 'activation', wpk, [hk('wT')], out=ho['wT'], in_=wps, func=AF.Copy)
            yield

        def seq(hd):
            par = hd % 2
            si = j * 8 + hd
            for c in range(4):
                cs = slice(c * 128, (c + 1) * 128)
                ho = hout[par][c]
                hk = lambda n, c=c: uk('ho', par, c, n)
                cols = ho['cols']
                vi = c % 2
                wsp, wsk = pq(6, 0)
                op('pe', 'matmul', [hk('wT'), ('Sb', si)], wsk, out=wsp, lhsT=ho['wT'], rhs=Sb[:, si, :], start=True, stop=True)
                op('dve', 'tensor_tensor', wsk + [hk('u')], [uk('vnew', vi)], out=vnewb[vi], in0=ho['u'], in1=wsp, op=ALU.subtract)
                yield
                ops_, opk = pq(7, 0)
                op('pe', 'matmul', [hk('qdec'), ('Sb', si)], opk, out=ops_, lhsT=ho['qdec'], rhs=Sb[:, si, :], start=True, stop=False)
                op('pe', 'matmul', [hk('attnT'), uk('vnew', vi)], opk, out=ops_, lhsT=ho['attnT'], rhs=vnewb[vi], start=False, stop=True)
                sup, suk = pq(6, 1)
                op('pe', 'matmul', [hk('kdec'), uk('vnew', vi)], suk, out=sup, lhsT=ho['kdec'], rhs=vnewb[vi], start=True, stop=True)
                op('dve', 'scalar_tensor_tensor', suk + [('Sf', si), hk('egl')], [('Sf', si)], out=Sf[:, si, :], in0=Sf[:, si, :], scalar=cols[:, 1:2],
                   op0=ALU.mult, in1=sup, op1=ALU.add)
                op('act', 'activation', [('Sf', si)], [('Sb', si)], out=Sb[:, si, :], in_=Sf[:, si, :], func=AF.Copy)
                sc = scol[:, vi * 4:vi * 4 + 4]
                op('act', 'activation', opk, [uk('junk'), uk('ssq', vi)], out=junk, in_=ops_, func=AF.Square, accum_out=sc[:, 0:1])
                op('act', 'activation', [uk('ssq', vi)], [uk('sln', vi)], out=sc[:, 1:2], in_=sc[:, 0:1], func=AF.Ln, bias=EPS, scale=1.0 / 128)
                op('act', 'activation', [uk('sln', vi)], [uk('rso', vi)], out=sc[:, 2:3], in_=sc[:, 1:2], func=AF.Exp, scale=-0.5)
                op('act', 'activation', opk + [uk('rso', vi)], [uk('on', vi)], out=onb[vi], in_=ops_, func=AF.Copy, scale=sc[:, 2:3])
                yield
                otf, otk = pq(7, 1)
                otp = otf.bitcast(BF16)
                op('pe', 'transpose', [uk('on', vi), 'identb'], otk, out=otp[:, 0:128], in_=onb[vi], identity=identb[:])
                op('dve', 'scalar_tensor_tensor', otk + ['dnnw', uk('zs', hd % 3)], [('mo', hd)], out=moc(hd)[:, cs], in0=otp[:, 0:128],
                   scalar=dnnw[:, j:j + 1], op0=ALU.mult, in1=zsb[hd % 3][:, cs], op1=ALU.mult)
                yield

        run_interleaved([projconv(0)])
        for hd in range(8):
            gens = [local(hd, c) for c in range(4)]
            if hd > 0:
                gens.append(seq(hd - 1))
            if hd < 7:
                gens.append(projconv(hd + 1))
            run_interleaved(gens)
        run_interleaved([seq(7)])
        for u in range(2):
            wv, wk = ring_next('dnout')
            w3 = wv.rearrange("p (k c) -> p k c", c=512)
            for dcl in range(4):
                dc = u * 4 + dcl
                ps, pk = big()
                for kk in range(8):
                    op('pe', 'matmul', [wk, ('mo', kk)], pk, out=ps[:], lhsT=w3[:, kk, dcl * 128:(dcl + 1) * 128], rhs=moc(kk), start=(kk == 0), stop=(kk == 7))
                op('dve', 'tensor_tensor', pk + [('xT', dc)], [('xT', dc)], out=xTc(dc), in0=ps[:], in1=xTc(dc), op=ALU.add)
            ring_release()

    BM = uf(0, 4096)
    BM4 = BM.rearrange("p (a two k) -> p a two k", two=2, k=256)
    qT3 = ubf(8192, 4096).rearrange("p (c t) -> p c t", t=512)
    NQS = 2

    def mk_qs(base):
        return dict(s=uf(base, 1024).rearrange("p (s k) -> p s k", k=256),
                    p=ubf(base + 1024, 1024).rearrange("p (s k) -> p s k", k=256),
                    pT=ubf(base + 1536, 1024),
                    cols=uf(base + 2048, 32))

    qsets = [mk_qs(10240 + i * 2080) for i in range(NQS)]
    otok = [ubf(14400 + n * 512, 1024) for n in range(4)]
    sink4 = sinkb[:].rearrange("p j (a two) -> p j a two", two=2)

    def swa_layer(L, t):
        j = L // 2
        phase_barrier()
        dma('pool', BM, bmfull_d.ap(), ['bmfull'], [uk('BM')], 'bm')
        rms_norm_to_h(nmix, 'nmix', L * 8)
        for u in range(2):
            wv, wk = ring_next('swq')
            w3 = wv.rearrange("p (k c) -> p k c", c=512)
            for ccl in range(4):
                cc = u * 4 + ccl
                ps, pk = big()
                for k in range(8):
                    op('pe', 'matmul', [wk, ('h', k)], pk, out=ps[:], lhsT=w3[:, k, ccl * 128:(ccl + 1) * 128], rhs=hc(k), start=(k == 0), stop=(k == 7))
                op('act', 'activation', pk + ['bq8'], [uk('qT', cc)], out=qT3[:, cc, :], in_=ps[:], func=AF.Identity, bias=bq8[:, j, cc:cc + 1], scale=0.125)
            ring_release()
        wv, wk = ring_next('swkv')
        w3 = wv.rearrange("p (k c) -> p k c", c=512)
        for kv in range(2):
            ps, pk = big()
            for k in range(8):
                op('pe', 'matmul', [wk, ('h', k)], pk, out=ps[:], lhsT=w3[:, k, kv * 128:(kv + 1) * 128], rhs=hc(k), start=(k == 0), stop=(k == 7))
            op('act', 'activation', pk + ['bk2'], [('kbuf', j)], out=kbuf[:, j, kv, 128:640], in_=ps[:], func=AF.Identity, bias=bk2[:, j, kv:kv + 1], scale=1.0)
        for s4 in range(4):
            psb, pk = big()
            psq = psb[:, 0:128]
            for k in range(8):
                op('pe', 'matmul', [wk, ('h', k)], pk, out=psq, lhsT=hc(k)[:, s4 * 128:(s4 + 1) * 128], rhs=w3[:, k, 256:384], start=(k == 0), stop=(k == 7))
            op('dve', 'tensor_tensor', pk + ['bvb'], [('vbuf', j)], out=vbuf[:, j, 1 + s4, :], in0=psq, in1=bvb[:, j, :], op=ALU.add)
        ring_release()

        qcount = [0]

        def quad(n, qd):
            qs = qsets[qcount[0] % NQS]
            qi_ = qcount[0] % NQS
            qcount[0] += 1
            first = False
            seq_start_blk = (t == 0 and n == 0)
            W0 = 128 if first else 0
            KW = 256 - W0
            kvh = qd // 2
            qk = lambda *nm: uk('qs', qi_, *nm)
            s_, p_, pT_, cols = qs['s'], qs['p'], qs['pT'], qs['cols']
            banks = [(0, 1), (2, 3)][qi_]
            for two in range(2):
                b = banks[two]
                for a in range(2):
                    cc = 2 * qd + a
                    op('pe', 'matmul', [uk('qT', cc), ('kbuf', j)], PQ(b), out=pb[b][:, a * 256 + W0:(a + 1) * 256],
                       lhsT=qT3[two * 64:(two + 1) * 64, cc, n * 128:(n + 1) * 128],
                       rhs=kbuf[two * 64:(two + 1) * 64, j, kvh, n * 128 + W0:n * 128 + 256], start=True, stop=True)
                op('dve', 'tensor_tensor', PQ(b) + [uk('BM')], [qk('s')], out=s_[:, 2 * two:2 * two + 2, W0:256],
                   in0=pb[b][:].rearrange("p (a k) -> p a k", k=256)[:, :, W0:256], in1=BM4[:, 2 * qd:2 * qd + 2, two, W0:256], op=ALU.add)
                if seq_start_blk:
                    op('dve', 'tensor_tensor', [qk('s'), 'fmask'], [qk('s')], out=s_[:, 2 * two:2 * two + 2, 0:128], in0=s_[:, 2 * two:2 * two + 2, 0:128],
                       in1=bass.AP(fmask, 0, [[128, 128], [0, 2], [1, 128]]), op=ALU.add)
            yield
            rmax = cols[:, 0:4]; mcol = cols[:, 4:8]; negm = cols[:, 8:12]; rsum = cols[:, 12:16]
            tmp4 = cols[:, 16:20]; esk = cols[:, 20:24]; den = cols[:, 24:28]; rinv = cols[:, 28:32]
            sk4 = sink4[:, j, 2 * qd:2 * qd + 2, :].rearrange("p a two -> p two a")
            op('dve', 'tensor_reduce', [qk('s')], [qk('rmax')], out=rmax, in_=s_[:, :, W0:256], axis=AX.X, op=ALU.max)
            op('dve', 'tensor_tensor', [qk('rmax'), 'sinkb'], [qk('m')], out=mcol.rearrange("p (two a) -> p two a", a=2),
               in0=rmax.rearrange("p (two a) -> p two a", a=2), in1=sk4, op=ALU.max)
            op('dve', 'tensor_scalar', [qk('m')], [qk('negm')], out=negm, in0=mcol, scalar1=-1.0, scalar2=None, op0=ALU.mult)
            for sl_ in range(4):
                op('act', 'activation', [qk('s'), qk('negm')], [qk('p', sl_), qk('rsum', sl_)], out=p_[:, sl_, W0:256], in_=s_[:, sl_, W0:256],
                   func=AF.Exp, bias=negm[:, sl_:sl_ + 1], scale=1.0, accum_out=rsum[:, sl_:sl_ + 1])
            op('dve', 'tensor_tensor', [qk('negm'), 'sinkb'], [qk('tmp4')], out=tmp4.rearrange("p (two a) -> p two a", a=2),
               in0=negm.rearrange("p (two a) -> p two a", a=2), in1=sk4, op=ALU.add)
            op('act', 'activation', [qk('tmp4')], [qk('esk')], out=esk, in_=tmp4, func=AF.Exp)
            op('dve', 'tensor_tensor', [qk('esk')] + [qk('rsum', i) for i in range(4)], [qk('den')], out=den, in0=rsum, in1=esk, op=ALU.add)
            op('dve', 'reciprocal', [qk('den')], [qk('rinv')], out=rinv, in_=den)
            yield
            tb = 4 + qi_
            ptp = pb[tb][:].bitcast(BF16)
            halves = [1] if first else [0, 1]
            for sl_ in range(4):
                for hf in halves:
                    op('pe', 'transpose', [qk('p', sl_), 'identb'], PQ(tb), out=ptp[:, (sl_ * 2 + hf) * 128:(sl_ * 2 + hf + 1) * 128],
                       in_=p_[:, sl_, hf * 128:(hf + 1) * 128], identity=identb[:])
            if first:
                for sl_ in range(4):
                    op('act', 'activation', PQ(tb), [qk('pT')], out=pT_[:, (sl_ * 2 + 1) * 128:(sl_ * 2 + 2) * 128],
                       in_=ptp[:, (sl_ * 2 + 1) * 128:(sl_ * 2 + 2) * 128], func=AF.Copy)
            else:
                op('act', 'activation', PQ(tb), [qk('pT')], out=pT_, in_=ptp, func=AF.Copy)
            yield
            pvk = PQ(6 + qi_)
            pv = pb[6 + qi_][:, 0:256]
            for sl_ in range(4):
                for hf in halves:
                    op('pe', 'matmul', [qk('pT'), ('vbuf', j)], pvk, out=pv[:, sl_ * 64:(sl_ + 1) * 64],
                       lhsT=pT_[:, (sl_ * 2 + hf) * 128:(sl_ * 2 + hf + 1) * 128], rhs=vbuf[:, j, n + hf, kvh * 64:(kvh + 1) * 64],
                       start=(hf == halves[0]), stop=(hf == 1))
            o4 = otok[n].rearrange("p (a two d) -> p two a d", two=2, d=64)[:, :, 2 * qd:2 * qd + 2, :]
            rinv_b = bass.AP(uni, rinv.offset, [[uni.shape[1], 128], [2, 2], [1, 2], [0, 64]])
            op('dve', 'tensor_tensor', pvk + [qk('rinv')], [uk('otok', n, qd)], out=o4, in0=pv.rearrange("p (two a d) -> p two a d", two=2, d=64),
               in1=rinv_b, op=ALU.mult)
            yield

        for n in range(4):
            gl = [quad(n, qd) for qd in range(4)]
            run_interleaved(gl[0:2])
            run_interleaved(gl[2:4])
            otp = pb[4][:].bitcast(BF16)
            for cc in range(8):
                op('pe', 'transpose', [uk('otok', n, cc // 2), 'identb'], PQ(4), out=otp[:, cc * 128:(cc + 1) * 128],
                   in_=otok[n][:, cc * 128:(cc + 1) * 128], identity=identb[:])
            op('act', 'activation', PQ(4), [('mo', cc) for cc in range(8)], out=mo[:].rearrange("p (c t) -> p c t", t=512)[:, :, n * 128:(n + 1) * 128],
               in_=otp.rearrange("p (c t) -> p c t", t=128), func=AF.Copy)
        op('pool', 'tensor_copy', [('kbuf', j)], [('kbuf', j)], out=kbuf[:, j, :, 0:128], in_=kbuf[:, j, :, 512:640])
        op('pool', 'tensor_copy', [('vbuf', j)], [('vbuf', j)], out=vbuf[:, j, 0, :], in_=vbuf[:, j, 4, :])
        for u in range(2):
            wv, wk = ring_next('swout')
            w3 = wv.rearrange("p (k c) -> p k c", c=512)
            for dcl in range(4):
                dc = u * 4 + dcl
                ps, pk = big()
                for kk in range(8):
                    op('pe', 'matmul', [wk, ('mo', kk)], pk, out=ps[:], lhsT=w3[:, kk, dcl * 128:(dcl + 1) * 128], rhs=moc(kk), start=(kk == 0), stop=(kk == 7))
                op('dve', 'scalar_tensor_tensor', pk + [('xT', dc), 'bo'], [('xT', dc)], out=xTc(dc), in0=ps[:], scalar=bo[:, j, dc:dc + 1],
                   op0=ALU.add, in1=xTc(dc), op1=ALU.add)
            ring_release()

    phase_barrier()
    for t in range(NT):
        load_x_tile(t)
        for L in range(n_layers):
            if dn and L % 2 == 0:
                dn_layer(L, t)
            if swa and L % 2 == 1:
                swa_layer(L, t)
            mlp(L)
        store_out_tile(t)
    dma('pool', stS_o.ap(), Sf[:].rearrange("p a b -> p (a b)"), [('Sf', i) for i in range(16)], ['so0'], 'so0')
    dma('pool', stH_o.ap().rearrange("p (a b) -> p a b", b=4), halo[:], [('halo', i) for i in range(48)], ['so1'], 'so1')
    for jj in range(2):
        dma('pool', stK_o.ap().rearrange("p (j k c) -> p j k c", j=2, k=2)[:, jj, :, :], kbuf[:, jj, :, 0:128], [('kbuf', jj)], ['so2%d' % jj], 'so2%d' % jj)
        dma('pool', stV_o.ap().rearrange("p (j c) -> p j c", j=2)[:, jj, :], vbuf[:, jj, 0, :], [('vbuf', jj)], ['so3%d' % jj], 'so3%d' % jj)
    S_.add('pool', lambda e: None, [('out', 0), ('out', 1), 'so0', 'so1', 'so20', 'so21', 'so30', 'so31'], [], reg=False)
    assert ring['used'] == total_units, (ring['used'], total_units)
    S_.emit(nc, st)
    st.close()
    return nc, S_


_CACHE = {}
SL = 8192


def kernel(**inputs):
    import ml_dtypes
    x = np.asarray(inputs['x'], np.float32)
    B, S, _ = x.shape
    sl = min(SL, S)
    if sl not in _CACHE:
        _CACHE[sl] = build_program(sl)
    nc, _ = _CACHE[sl]
    shared = {}
    for k, v in inputs.items():
        if k == 'x':
            continue
        a = np.ascontiguousarray(np.asarray(v, np.float32))
        if k == 'norm_final':
            a = a.reshape(1, D)
        shared[k] = a
    shared.update(host_constants())
    ncore = B
    state = [dict(st_S_in=np.zeros((128, 2048), np.float32), st_halo_in=np.zeros((128, 192), np.float32),
                  st_k_in=np.zeros((128, 512), ml_dtypes.bfloat16), st_vv_in=np.zeros((128, 256), ml_dtypes.bfloat16))
             for _ in range(ncore)]
    out = np.empty((B, S, D), np.float32)
    for li in range(S // sl):
        in_maps = []
        for c in range(ncore):
            m = dict(shared)
            m['x'] = np.ascontiguousarray(x[c, li * sl:(li + 1) * sl])
            m['first_mask'] = np.full((128, 128), NEG if li == 0 else 0.0, np.float32)
            m.update(state[c])
            in_maps.append(m)
        res = run_bass_kernel_spmd(nc, in_maps, core_ids=list(range(ncore)))
        for c in range(ncore):
            r = res.results[c]
            out[c, li * sl:(li + 1) * sl] = np.asarray(r['out'], np.float32)
            state[c] = dict(st_S_in=np.asarray(r['st_S_out']), st_halo_in=np.asarray(r['st_halo_out']),
                            st_k_in=np.asarray(r['st_k_out']), st_vv_in=np.asarray(r['st_vv_out']))
    return out
```

```python
import math
import numpy as np
from contextlib import ExitStack
import concourse.bass as bass
import concourse.mybir as mybir
from concourse.bass_utils import run_bass_kernel_spmd

F32 = mybir.dt.float32
F32R = mybir.dt.float32r
BF16 = mybir.dt.bfloat16
AF = mybir.ActivationFunctionType
ALU = mybir.AluOpType
AX = mybir.AxisListType

D = 1024
T = 512
NSLOT = 4
EPS = 1e-6
NEG = -1.0e30
SEM_LIM = 16000


class Sched:
    def __init__(self):
        self.ops = []

    def add(self, eng, fn, reads=(), writes=(), dma=None, reg=True):
        def isps(k):
            return isinstance(k, tuple) and len(k) > 1 and k[0] in ('pq', 'pb')
        banks = []
        for k in tuple(reads) + tuple(writes):
            if isps(k) and ('pb', k[1]) not in banks:
                banks.append(('pb', k[1]))
        reads = tuple(k for k in reads if not isps(k))
        writes = tuple(k for k in writes if not isps(k)) + tuple(banks)
        self.ops.append(dict(eng=eng, fn=fn, r=tuple(reads), w=tuple(writes), dma=dma, reg=reg))

    def barrier(self, keys):
        keys = tuple(keys)
        for eng in ('pe', 'act', 'dve', 'pool', 'sp'):
            self.add(eng, lambda e: None, reads=(), writes=keys, reg=False)

    def emit(self, nc, stack):
        ops = self.ops
        last_w = {}
        readers = {}
        deps = []
        for i, o in enumerate(ops):
            d = set()
            for k in o['r']:
                if k in last_w:
                    d.add(last_w[k])
            for k in o['w']:
                if k in last_w:
                    d.add(last_w[k])
                d.update(readers.get(k, ()))
            d.discard(i)
            dd = []
            rset = set(o['r'])
            for j in d:
                p = ops[j]
                if o['reg'] and p['dma'] is None and o['dma'] is None and p['eng'] == o['eng']:
                    if o['eng'] == 'pe':
                        continue
                dd.append(j)
            deps.append(dd)
            if o['reg']:
                for k in o['r']:
                    readers.setdefault(k, []).append(i)
                for k in o['w']:
                    last_w[k] = i
                    readers[k] = []
        signaled = set()
        for dd in deps:
            signaled.update(dd)
        sems = {}
        counts = {}
        ncnt = {}
        token = [None] * len(ops)
        for i, o in enumerate(ops):
            if o['dma'] is not None:
                nm = 'd_' + o['dma']
                counts[nm] = counts.get(nm, 0) + 16
                token[i] = (nm, counts[nm])
            elif i in signaled:
                base = 'e_' + o['eng']
                ncnt[base] = ncnt.get(base, 0) + 1
                gen = (ncnt[base] - 1) // SEM_LIM
                nm = '%s_%d' % (base, gen)
                counts[nm] = (ncnt[base] - 1) % SEM_LIM + 1
                token[i] = (nm, counts[nm])
        self.counts = counts
        per_eng = {}
        for i, o in enumerate(ops):
            per_eng.setdefault(o['eng'], []).append(i)
        for nm in counts:
            sems[nm] = stack.enter_context(nc.semaphore(nm))
        block = stack.enter_context(nc.Block())
        binder = dict(pe=block.tensor, act=block.scalar, dve=block.vector,
                      pool=block.gpsimd, sp=block.sync)
        nw = [0]

        def make(idxs):
            def body(e):
                seen = {}
                for i in idxs:
                    o = ops[i]
                    need = {}
                    for j in deps[i]:
                        nm, v = token[j]
                        if seen.get(nm, 0) >= v:
                            continue
                        if need.get(nm, 0) < v:
                            need[nm] = v
                    for nm, v in need.items():
                        e.wait_ge(sems[nm], v)
                        seen[nm] = v
                        nw[0] += 1
                    inst = o['fn'](e)
                    if token[i] is not None:
                        nm, v = token[i]
                        inst.then_inc(sems[nm], 16 if o['dma'] is not None else 1)
            return body

        for engname, idxs in per_eng.items():
            binder[engname](make(idxs))
        self.nwaits = nw[0]


def run_interleaved(gens):
    gens = list(gens)
    while gens:
        for g in list(gens):
            try:
                next(g)
            except StopIteration:
                gens.remove(g)


def t5_bucket(n):
    if n < 16:
        return n
    v = 16 + int(np.float32(np.log(np.float32(n) / np.float32(16.0))) / np.float32(math.log(8.0)) * np.float32(16.0))
    return min(v, 31)


def host_constants():
    c = {}
    c['c_ident'] = np.eye(128, dtype=np.float32)
    ii = np.arange(128)
    c['c_U'] = (ii[:, None] <= ii[None, :]).astype(np.float32)
    c['c_neglo'] = np.where(ii[:, None] > ii[None, :], 0.0, NEG).astype(np.float32)
    c['c_negup'] = np.where(ii[:, None] <= ii[None, :], 0.0, NEG).astype(np.float32)
    c['c_J'] = np.eye(128, dtype=np.float32)[::-1].copy()
    mk = np.zeros((4, 128, 128), np.float32)
    mk[0] = (ii[:, None] // 16 == ii[None, :] // 16)
    for l, s_ in enumerate((16, 32, 64)):
        mk[l + 1] = (ii[:, None] // (2 * s_) == ii[None, :] // (2 * s_)) & (ii[:, None] // s_ != ii[None, :] // s_)
    c['c_masks'] = np.ascontiguousarray(mk.transpose(1, 0, 2)).reshape(128, 512)
    oh = np.zeros((32, 384), np.float32)
    ng = np.full((16, 384), NEG, np.float32)
    for m in range(384):
        dist = 255 - m
        if 0 <= dist <= 127:
            oh[t5_bucket(dist), m] = 1.0
            ng[:, m] = 0.0
    c['c_onehot'] = oh
    c['c_negrow'] = ng
    return c


def build_program(S, n_layers=4, dn=True, swa=True):
    NT = S // T
    nc = bass.Bass("TRN2", target_bir_lowering=False)

    def din(name, shape):
        return nc.dram_tensor(name, list(shape), F32, kind="ExternalInput")

    x_d = din("x", [S, D])
    nmix_d = din("norm_mix", [4, D]); nmlp_d = din("norm_mlp", [4, D]); nfin_d = din("norm_final", [1, D])
    dnin_d = din("dn_w_in", [2, D, 4112]); dncw_d = din("dn_conv_w", [2, 4, 3072])
    alog_d = din("dn_a_log", [2, 8]); dtb_d = din("dn_dt_bias", [2, 8]); dnnw_d = din("dn_norm_w", [2, 128])
    dnout_d = din("dn_w_out", [2, D, D])
    swqkv_d = din("swa_w_qkv", [2, D, 1280]); swb_d = din("swa_b_qkv", [2, 1280]); sink_d = din("swa_sinks", [2, 16])
    swout_d = din("swa_w_out", [2, D, D]); swbo_d = din("swa_b_out", [2, D]); relb_d = din("rel_bias", [32, 16])
    up_d = din("mlp_w_up", [4, D, 4096]); down_d = din("mlp_w_down", [4, 4096, D])
    cid_d = din("c_ident", [128, 128]); cU_d = din("c_U", [128, 128]); cnl_d = din("c_neglo", [128, 128])
    cnu_d = din("c_negup", [128, 128]); cJ_d = din("c_J", [128, 128]); coh_d = din("c_onehot", [32, 384])
    cng_d = din("c_negrow", [16, 384])
    cmk_d = din("c_masks", [128, 512])
    out_d = nc.dram_tensor("out", [S, D], F32, kind="ExternalOutput")
    stS_i = din("st_S_in", [128, 2048]); stH_i = din("st_halo_in", [128, 192]); fmask_d = din("first_mask", [128, 128])
    stK_i = nc.dram_tensor("st_k_in", [128, 512], BF16, kind="ExternalInput")
    stV_i = nc.dram_tensor("st_vv_in", [128, 256], BF16, kind="ExternalInput")
    stS_o = nc.dram_tensor("st_S_out", [128, 2048], F32, kind="ExternalOutput")
    stH_o = nc.dram_tensor("st_halo_out", [128, 192], F32, kind="ExternalOutput")
    stK_o = nc.dram_tensor("st_k_out", [128, 512], BF16, kind="ExternalOutput")
    stV_o = nc.dram_tensor("st_vv_out", [128, 256], BF16, kind="ExternalOutput")

    units = []
    for L in range(n_layers):
        j = L // 2
        if dn and L % 2 == 0:
            for hd in range(8):
                units.append(('dnin', L, hd))
            for u in range(2):
                units.append(('dnout', L, u))
        if swa and L % 2 == 1:
            for u in range(2):
                units.append(('swq', L, u))
            units.append(('swkv', L, 0))
            for u in range(2):
                units.append(('swout', L, u))
        units.append(('up', L, 0))
        for g in range(8):
            if g < 7:
                units.append(('up', L, g + 1))
            units.append(('down', L, g))
    NU = len(units)
    wsc_d = nc.dram_tensor("wsc", [NU, 128, 4096], BF16, kind="Internal")
    bm_d = nc.dram_tensor("bmscr", [16, 384], F32, kind="Internal")

    S_ = Sched()
    st = ExitStack()

    def sb(name, shape, dt=F32):
        return st.enter_context(nc.sbuf_tensor(name, list(shape), dt))

    identf = sb("identf", [128, 128]); identb = sb("identb", [128, 128], BF16)
    onesf = sb("onesf", [128, 128]); negonesf = sb("negonesf", [128, 128]); onesb = sb("onesb", [128, 128], BF16)
    fmask = sb("fmask", [128, 128]); cmask = sb("cmask", [128, 512]); Umask = sb("Umask", [128, 128]); neglo = sb("neglo", [128, 128]); negup = sb("negup", [128, 128])
    nmix = sb("nmix", [128, 32]); nmlp = sb("nmlp", [128, 32]); nfin = sb("nfin", [128, 8])
    cwt = sb("cwt", [128, 2, 96])
    nexpA = sb("nexpA", [128, 2, 8]); dtb = sb("dtb", [128, 2, 8]); dnnw = sb("dnnw", [128, 2])
    wba = sb("wba", [128, 2, 8, 16], BF16)
    bq8 = sb("bq8", [128, 2, 8]); bk2 = sb("bk2", [128, 2, 2]); bvb = sb("bvb", [128, 2, 128])
    sinkb = sb("sinkb", [128, 2, 16]); bo = sb("bo", [128, 2, 8])
    xT = sb("xT", [128, 4096]); hh = sb("hh", [128, 4096], BF16); mo = sb("mo", [128, 4096], BF16)
    ag = sb("ag", [128, 2, 2048], BF16)
    rlb = sb("rlb", [128, 2, 512])
    wr = sb("wr", [128, NSLOT, 4096], BF16)
    xs = sb("xs", [128, 2, 1024])
    Sf = sb("Sf", [128, 16, 128]); Sb = sb("Sb", [128, 16, 128], BF16)
    halo = sb("halo", [128, 48, 4])
    kbuf = sb("kbuf", [128, 2, 2, 640], BF16)
    vbuf = sb("vbuf", [128, 2, 5, 128], BF16)
    sqb = sb("sqb", [128, 2, 512], BF16); rt = sb("rt", [128, 512]); rstd = sb("rstd", [128, 512])
    rpool = sb("rpool", [128, 4, 12, 128], F32R)
    uni = sb("uni", [128, 16512])
    pb = [st.enter_context(nc.psum_tensor("pb%d" % b, [128, 512], F32)) for b in range(8)]

    def xTc(c):
        return xT[:, c * 512:(c + 1) * 512]

    def hc(c):
        return hh[:, c * 512:(c + 1) * 512]

    def moc(c):
        return mo[:, c * 512:(c + 1) * 512]

    def PQ(b):
        return [('pq', b, q) for q in range(4)]

    dcount = [0]

    def dma(eng, out, in_, reads, writes, key):
        writes = list(writes)
        if key.startswith('c'):
            writes.append('chain_' + key)
        S_.add(eng, lambda e: e.dma_start(out=out, in_=in_), reads=reads, writes=writes, dma=key)

    def DAP(t, off, dims):
        return bass.AP(t, off, [list(d) for d in dims])

    dma('pool', identf[:], cid_d.ap(), [], ['identf'], 'c0')
    dma('pool', Umask[:], cU_d.ap(), [], ['Umask'], 'c1')
    dma('pool', neglo[:], cnl_d.ap(), [], ['neglo'], 'c2')
    dma('pool', negup[:], cnu_d.ap(), [], ['negup'], 'c3')
    dma('pool', cmask[:], cmk_d.ap(), [], ['cmask'], 'c3')
    S_.add('pool', lambda e: e.tensor_copy(out=identb[:], in_=identf[:]), ['identf'], ['identb'])
    S_.add('pool', lambda e: e.memset(onesf[:], 1.0), [], ['onesf'])
    S_.add('pool', lambda e: e.memset(negonesf[:], -1.0), [], ['negonesf'])
    S_.add('pool', lambda e: e.memset(onesb[:], 1.0), [], ['onesb'])
    dma('pool', fmask[:], fmask_d.ap(), [], ['fmask'], 'c3')
    dma('pool', Sf[:].rearrange("p a b -> p (a b)"), stS_i.ap(), [], [('Sf', i) for i in range(16)], 'c11')
    S_.add('act', lambda e: e.activation(out=Sb[:].rearrange("p a b -> p (a b)"), in_=Sf[:].rearrange("p a b -> p (a b)"), func=AF.Copy),
           [('Sf', i) for i in range(16)], [('Sb', i) for i in range(16)])
    S_.add('pool', lambda e: e.memset(halo[:], 0.0), [], [('halo', i) for i in range(48)])
    dma('pool', halo[:, :, 0:3], stH_i.ap().rearrange("p (a b) -> p a b", b=4)[:, :, 0:3], [], [('halo', i) for i in range(48)], 'c11')
    S_.add('pool', lambda e: e.memset(kbuf[:], 0.0), [], [('kbuf', jj) for jj in range(2)])
    S_.add('pool', lambda e: e.memset(vbuf[:], 0.0), [], [('vbuf', jj) for jj in range(2)])
    for jj in range(2):
        dma('pool', kbuf[:, jj, :, 0:128], stK_i.ap().rearrange("p (j k c) -> p j k c", j=2, k=2)[:, jj, :, :], [], [('kbuf', jj)], 'c11')
        dma('pool', vbuf[:, jj, 0, :], stV_i.ap().rearrange("p (j c) -> p j c", j=2)[:, jj, :], [], [('vbuf', jj)], 'c11')
    tmpA = uni[:, 0:128]
    tmpB = uni[:, 128:256]

    def load_cols(dst_ap, src_t, nrows, key):
        dma('pool', tmpA[0:nrows, :], DAP(src_t, 0, [[128, nrows], [1, 128]]), [], [('u', 'tmpA')], 'c4')
        S_.add('pe', lambda e: e.transpose(out=pb[0][:, 0:nrows], in_=tmpA[0:nrows, :], identity=identf[0:nrows, 0:nrows]),
               [('u', 'tmpA'), 'identf'], PQ(0))
        S_.add('act', lambda e: e.activation(out=dst_ap, in_=pb[0][:, 0:nrows], func=AF.Copy), PQ(0), [key])

    load_cols(nmix[:], nmix_d, 32, 'nmix')
    load_cols(nmlp[:], nmlp_d, 32, 'nmlp')
    load_cols(nfin[:], nfin_d, 8, 'nfin')
    for jj in range(2):
        dma('pool', tmpA[0:96, :], DAP(dncw_d, jj * 4 * 3072, [[128, 96], [1, 128]]), [], [('u', 'tmpA')], 'c4')
        S_.add('pe', lambda e: e.transpose(out=pb[0][:, 0:96], in_=tmpA[0:96, :], identity=identf[0:96, 0:96]),
               [('u', 'tmpA'), 'identf'], PQ(0))
        S_.add('act', lambda e, jj=jj: e.activation(out=cwt[:, jj, :], in_=pb[0][:, 0:96], func=AF.Copy), PQ(0), ['cwt'])
    load_cols(dnnw[:], dnnw_d, 2, 'dnnw')
    load_cols(bo[:].rearrange("p a b -> p (a b)"), swbo_d, 16, 'bo')
    dma('pool', tmpA[0:20, :], DAP(swb_d, 0, [[128, 20], [1, 128]]), [], [('u', 'tmpA')], 'c4')
    S_.add('pe', lambda e: e.transpose(out=pb[0][:, 0:20], in_=tmpA[0:20, :], identity=identf[0:20, 0:20]),
           [('u', 'tmpA'), 'identf'], PQ(0))
    for jj in range(2):
        S_.add('act', lambda e, jj=jj: e.activation(out=bq8[:, jj, :], in_=pb[0][:, jj * 10:jj * 10 + 8], func=AF.Copy, scale=0.125),
               PQ(0), ['bq8'])
    for jj in range(2):
        for kv in range(2):
            for half in range(2):
                dma('pool', bk2[half * 64:(half + 1) * 64, jj, kv:kv + 1],
                    DAP(swb_d, jj * 1280 + 1024 + kv * 64, [[1, 64], [1, 1]]), [], ['bk2'], 'c5')
        dma('pool', bvb[:, jj, :], DAP(swb_d, jj * 1280 + 1152, [[0, 128], [1, 128]]), [], ['bvb'], 'c5')
        dma('pool', sinkb[:, jj, :], DAP(sink_d, jj * 16, [[0, 128], [1, 16]]), [], ['sinkb'], 'c5')
        dma('pool', dtb[:, jj, :], DAP(dtb_d, jj * 8, [[0, 128], [1, 8]]), [], ['dtb'], 'c5')
        dma('pool', nexpA[:, jj, :], DAP(alog_d, jj * 8, [[0, 128], [1, 8]]), [], ['nexpA0'], 'c5')
    S_.add('act', lambda e: e.activation(out=nexpA[:], in_=nexpA[:], func=AF.Exp), ['nexpA0'], ['nexpA1'])
    S_.add('dve', lambda e: e.tensor_scalar(out=nexpA[:], in0=nexpA[:], scalar1=-1.0, scalar2=None, op0=ALU.mult),
           ['nexpA1'], ['nexpA'])
    for jj in range(2):
        dma('pool', tmpB[:, 0:128].rearrange("p (k c) -> p k c", c=16),
            DAP(dnin_d, jj * D * 4112 + 4096, [[4112, 128], [128 * 4112, 8], [1, 16]]), [], [('u', 'tmpB')], 'c6')
        S_.add('dve', lambda e, jj=jj: e.tensor_copy(out=wba[:, jj, :, :], in_=tmpB[:, 0:128].rearrange("p (k c) -> p k c", c=16)),
               [('u', 'tmpB')], ['wba'])

    cast_rr = [0]

    def prepass():
        hu = 0
        for uid, (kind, L, idx) in enumerate(units):
            j = L // 2
            for hf in range(2):
                s = hu % 2
                hu += 1
                stg = xT[:, s * 2048:(s + 1) * 2048]
                bst = hh[:, s * 2048:(s + 1) * 2048]
                skeys = [('xT', 4 * s + i) for i in range(4)]
                bkeys = [('h', 4 * s + i) for i in range(4)]
                if kind == 'dnin':
                    for f2 in range(2):
                        f = 2 * hf + f2
                        dma('sp', stg[:, f2 * 1024:(f2 + 1) * 1024].rearrange("p (k c) -> p k c", c=128),
                            DAP(dnin_d, j * D * 4112 + f * 1024 + idx * 128, [[4112, 128], [128 * 4112, 8], [1, 128]]),
                            [], skeys, 'pp%d' % s)
                elif kind in ('dnout', 'swq', 'swout', 'up'):
                    src, rs, base = {'dnout': (dnout_d, D, j * D * D), 'swq': (swqkv_d, 1280, j * D * 1280),
                                     'swout': (swout_d, D, j * D * D), 'up': (up_d, 4096, L * D * 4096)}[kind]
                    dma('sp', stg.rearrange("p (k c) -> p k c", c=512),
                        DAP(src, base + (4 * hf) * 128 * rs + idx * 512, [[rs, 128], [128 * rs, 4], [1, 512]]),
                        [], skeys, 'pp%d' % s)
                elif kind == 'down':
                    dma('sp', stg.rearrange("p (k c) -> p k c", c=1024),
                        DAP(down_d, L * 4096 * D + ((idx * 4 + 2 * hf) * 128) * D, [[D, 128], [128 * D, 2], [1, D]]),
                        [], skeys, 'pp%d' % s)
                elif kind == 'swkv':
                    s3 = stg.rearrange("p (k c) -> p k c", c=512)
                    for (c0, w, col) in ((0, 64, 1024), (64, 64, 1024), (128, 64, 1088), (192, 64, 1088), (256, 128, 1152)):
                        dma('sp', s3[:, :, c0:c0 + w],
                            DAP(swqkv_d, j * D * 1280 + (4 * hf) * 128 * 1280 + col, [[1280, 128], [128 * 1280, 4], [1, w]]),
                            [], skeys, 'pp%d' % s)
                    S_.add('pool', lambda e, s3=s3: e.memset(s3[:, :, 384:512], 0.0), [], skeys)
                eng = ('act', 'dve', 'pool')[cast_rr[0] % 3]
                cast_rr[0] += 1
                if eng == 'act':
                    S_.add('act', lambda e, bst=bst, stg=stg: e.activation(out=bst, in_=stg, func=AF.Copy), skeys, bkeys)
                else:
                    S_.add(eng, lambda e, bst=bst, stg=stg: e.tensor_copy(out=bst, in_=stg), skeys, bkeys)
                dma('sp', DAP(wsc_d, uid * 128 * 4096 + hf * 2048, [[4096, 128], [1, 2048]]), bst,
                    bkeys, [('wsc', uid)], 'ps%d' % s)

    prepass()

    total_units = NU * NT
    ring = dict(loaded=0, used=0)

    def ring_load():
        gu = ring['loaded']
        if gu >= total_units:
            return
        slot = gu % NSLOT
        uid = gu % NU
        dma('sp', wr[:, slot, :], DAP(wsc_d, uid * 128 * 4096, [[4096, 128], [1, 4096]]),
            [('wsc', uid)], [('wr', slot)], 'w%d' % slot)
        ring['loaded'] += 1

    def ring_next(expect):
        gu = ring['used']
        assert units[gu % NU][0] == expect, (units[gu % NU], expect)
        ring['used'] += 1
        slot = gu % NSLOT
        return wr[:, slot, :], ('wr', slot)

    def ring_release():
        ring_load()

    for _ in range(NSLOT):
        ring_load()

    rot = dict(big=0, q=0)

    def big(banks=(0, 1)):
        b = banks[rot['big'] % len(banks)]
        rot['big'] += 1
        return pb[b], PQ(b)

    def rms_stats():
        ps, pk = big()
        for c in range(8):
            sq = sqb[:, c % 2, :]
            S_.add('act', lambda e, c=c, sq=sq: e.activation(out=sq, in_=xTc(c), func=AF.Square), [('xT', c)], [('sqb', c % 2)])
            S_.add('pe', lambda e, c=c, sq=sq, ps=ps: e.matmul(ps[:], lhsT=onesb[:], rhs=sq, start=(c == 0), stop=(c == 7)),
                   [('sqb', c % 2), 'onesb'], pk)
        S_.add('act', lambda e, ps=ps: e.activation(out=rt[:], in_=ps[:], func=AF.Ln, bias=EPS, scale=1.0 / D), pk, ['rt'])
        S_.add('act', lambda e: e.activation(out=rstd[:], in_=rt[:], func=AF.Exp, scale=-0.5), ['rt'], ['rstd'])

    def rms_norm_to_h(wtile, wkey, col0):
        rms_stats()
        for c in range(8):
            S_.add('dve', lambda e, c=c: e.scalar_tensor_tensor(out=hc(c), in0=xTc(c), scalar=wtile[:, col0 + c:col0 + c + 1],
                                                              op0=ALU.mult, in1=rstd[:], op1=ALU.mult),
                   [('xT', c), wkey, 'rstd'], [('h', c)])

    def load_x_tile(t):
        for s4 in range(4):
            sl = s4 % 2
            r0 = t * T + s4 * 128
            dma('pool', xs[:, sl, :], DAP(x_d, r0 * D, [[D, 128], [1, D]]), [], [('xs', sl)], 'xs%d' % sl)
            for hf in range(2):
                ps, pk = big()
                for cl in range(4):
                    c = hf * 4 + cl
                    S_.add('pe', lambda e, ps=ps, cl=cl, c=c, sl=sl: e.transpose(out=ps[:, cl * 128:(cl + 1) * 128],
                                                                               in_=xs[:, sl, c * 128:(c + 1) * 128], identity=identf[:]),
                           [('xs', sl), 'identf'], pk)
                dst = xT[:, hf * 2048:(hf + 1) * 2048].rearrange("p (c t) -> p c t", t=512)[:, :, s4 * 128:(s4 + 1) * 128]
                S_.add('act', lambda e, ps=ps, dst=dst: e.activation(out=dst, in_=ps[:].rearrange("p (c t) -> p c t", t=128), func=AF.Copy),
                       pk, [('xT', hf * 4 + i) for i in range(4)])

    def store_out_tile(t):
        rms_stats()
        for c in range(8):
            S_.add('dve', lambda e, c=c: e.scalar_tensor_tensor(out=xTc(c), in0=xTc(c), scalar=nfin[:, c:c + 1],
                                                              op0=ALU.mult, in1=rstd[:], op1=ALU.mult),
                   [('xT', c), 'nfin', 'rstd'], [('xT', c)])
        for s4 in range(4):
            sl = s4 % 2
            for hf in range(2):
                ps, pk = big()
                for cl in range(4):
                    c = hf * 4 + cl
                    S_.add('pe', lambda e, ps=ps, cl=cl, c=c, s4=s4: e.transpose(out=ps[:, cl * 128:(cl + 1) * 128],
                                                                               in_=xTc(c)[:, s4 * 128:(s4 + 1) * 128], identity=identf[:]),
                           [('xT', c), 'identf'], pk)
                S_.add('act', lambda e, ps=ps, sl=sl, hf=hf: e.activation(out=xs[:, sl, hf * 512:(hf + 1) * 512], in_=ps[:], func=AF.Copy),
                       pk, [('xs', sl)])
            r0 = t * T + s4 * 128
            dma('pool', DAP(out_d, r0 * D, [[D, 128], [1, D]]), xs[:, sl, :], [('xs', sl)], [('out', sl)], 'os%d' % sl)

    def mlp(L):
        rms_norm_to_h(nmlp, 'nmlp', L * 8)

        def do_up(g):
            wu, wuk = ring_next('up')
            assert units[(ring['used'] - 1) % NU][2] == g
            wu3 = wu.rearrange("p (k c) -> p k c", c=512)
            a = ag[:, g % 2, :]
            for fl in range(4):
                ps, pk = big((0, 1, 2, 3))
                for k in range(8):
                    S_.add('pe', lambda e, ps=ps, wu3=wu3, k=k, fl=fl: e.matmul(ps[:], lhsT=wu3[:, k, fl * 128:(fl + 1) * 128], rhs=hc(k),
                                                                              start=(k == 0), stop=(k == 7)),
                           [wuk, ('h', k)], pk)
                asl = a[:, fl * 512:(fl + 1) * 512]
                akey = ('ag', g % 2, fl)
                rl = rlb[:, fl % 2, :]
                S_.add('act', lambda e, ps=ps, rl=rl: e.activation(out=rl, in_=ps[:], func=AF.Relu), pk, [('rlb', fl % 2)])
                S_.add('pool', lambda e, asl=asl, rl=rl: e.tensor_tensor(out=asl, in0=rl, in1=rl, op=ALU.mult), [('rlb', fl % 2)], [akey])
            ring_release()

        def do_down(g):
            wd, wdk = ring_next('down')
            assert units[(ring['used'] - 1) % NU][2] == g
            wd3 = wd.rearrange("p (f c) -> p f c", c=1024)
            a = ag[:, g % 2, :]
            for dc in range(8):
                ps, pk = big((4, 5, 6, 7))
                for fl in range(4):
                    S_.add('pe', lambda e, ps=ps, wd3=wd3, fl=fl, dc=dc, a=a: e.matmul(ps[:], lhsT=wd3[:, fl, dc * 128:(dc + 1) * 128],
                                                                                   rhs=a[:, fl * 512:(fl + 1) * 512],
                                                                                   start=(fl == 0), stop=(fl == 3)),
                           [wdk, ('ag', g % 2, fl)], pk)
                S_.add('dve', lambda e, ps=ps, dc=dc: e.tensor_tensor(out=xTc(dc), in0=ps[:], in1=xTc(dc), op=ALU.add),
                       pk + [('xT', dc)], [('xT', dc)])
            ring_release()

        do_up(0)
        for g in range(8):
            if g < 7:
                do_up(g + 1)
            do_down(g)

    ukeys = set([('u', 'tmpA'), ('u', 'tmpB')])

    def uk(*k):
        key = ('u',) + k
        ukeys.add(key)
        return key

    def uf(off, n):
        return uni[:, off:off + n]

    def ubf(off, n):
        return uni[:, off:off + n // 2].bitcast(BF16)

    def op(eng, method, reads, writes, **kw):
        S_.add(eng, lambda e: getattr(e, method)(**kw), reads, writes)

    def pq(b, q):
        return pb[b][:, q * 128:(q + 1) * 128], [('pq', b, q)]

    def phase_barrier():
        S_.barrier(sorted(ukeys, key=str))

    bmfull_d = nc.dram_tensor("bmfull", [128, 4096], F32, kind="Internal")
    if swa:
        Jm = uf(8192, 128); oh = uf(8320, 384); ngr = uf(8704, 384); relb = uf(9088, 16); rvec = uf(9104, 384)
        BMrev = uf(4096, 4096); BMsb = uf(0, 4096)
        dma('pool', Jm, cJ_d.ap(), [], [uk('Jm')], 'c7')
        dma('pool', oh[0:32, :], coh_d.ap(), [], [uk('oh')], 'c7')
        dma('pool', ngr[0:16, :], cng_d.ap(), [], [uk('ngr')], 'c7')
        dma('pool', relb[0:32, :], relb_d.ap(), [], [uk('relb')], 'c7')
        op('pe', 'matmul', [uk('relb'), uk('oh')], PQ(0), out=pb[0][0:16, 0:384], lhsT=relb[0:32, :], rhs=oh[0:32, :], start=True, stop=True)
        op('dve', 'tensor_tensor', PQ(0) + [uk('ngr')], [uk('rvec')], out=rvec[0:16, :], in0=pb[0][0:16, 0:384], in1=ngr[0:16, :], op=ALU.add)
        dma('pool', bm_d.ap(), rvec[0:16, :], [uk('rvec')], ['bm_d'], 'c8')
        for hh2 in range(2):
            dma('sp', BMrev[:, hh2 * 2048:(hh2 + 1) * 2048].rearrange("p (h k) -> p h k", k=256),
                DAP(bm_d, hh2 * 8 * 384, [[1, 128], [384, 8], [1, 256]]), ['bm_d'], [uk('BMrev', hh2)], 'c9')
        for g8 in range(8):
            ps, pk = big()
            op('pe', 'matmul', [uk('Jm'), uk('BMrev', g8 // 4)], pk, out=ps[:], lhsT=Jm, rhs=BMrev[:, g8 * 512:(g8 + 1) * 512], start=True, stop=True)
            op('act', 'activation', pk, [uk('BMsb', g8)], out=BMsb[:, g8 * 512:(g8 + 1) * 512], in_=ps[:], func=AF.Copy)
        dma('sp', bmfull_d.ap(), BMsb, [uk('BMsb', g8) for g8 in range(8)], ['bmfull'], 'c10')

    pre3 = uf(0, 1548).rearrange("p (f n) -> p f n", n=516)
    cv3 = uf(1548, 1536).rearrange("p (f n) -> p f n", n=512)
    sl3 = uf(3084, 1024).rearrange("p (f n) -> p f n", n=512)
    sg = uf(4108, 512); rt2 = uf(4620, 512); rs2 = uf(7692, 512)
    sq2 = ubf(5132, 512)
    vbb = [ubf(5388 + i * 256, 512) for i in range(2)]
    zsb = [ubf(5900 + i * 256, 512) for i in range(3)]
    qnb = [ubf(6668 + i * 256, 512) for i in range(2)]
    knb = [ubf(7180 + i * 256, 512) for i in range(2)]
    LNAMES = ['GU', 't1', 't2', 'egb']
    RNAMES = ['Q0', 'P0', 'Q1', 'P1', 'Q2', 'P2', 'R0', 'R1', 'T0', 'T1', 'kbg', 'bv']
    lset = [{n: uf(8204 + c * 512 + i * 128, 128) for i, n in enumerate(LNAMES)} for c in range(4)]
    for c in range(4):
        for i, n in enumerate(RNAMES):
            lset[c][n] = rpool[:, c, i, :].bitcast(F32)

    def mk_hout(base):
        d = {'u': uf(base, 128)}
        for i, n in enumerate(['attnT', 'kdec', 'qdec', 'wT']):
            d[n] = ubf(base + 128 + i * 64, 128)
        d['cols'] = uf(base + 384, 8)
        return d

    hout = [[mk_hout(10252 + (p_ * 4 + c) * 400) for c in range(4)] for p_ in range(2)]
    vnewb = [ubf(13452 + i * 64, 128) for i in range(2)]
    junk = ubf(13580, 128)
    onb = [ubf(13644 + i * 64, 128) for i in range(2)]
    scol = uf(13772, 8)
    beta3 = uf(13780, 32).rearrange("p (s h) -> p s h", h=8)
    g3 = uf(13812, 32).rearrange("p (s h) -> p s h", h=8)
    tg3 = uf(13844, 32).rearrange("p (s h) -> p s h", h=8)
    ee3 = uf(13876, 32).rearrange("p (s h) -> p s h", h=8)

    def dn_layer(L, t):
        j = L // 2
        phase_barrier()
        rms_norm_to_h(nmix, 'nmix', L * 8)
        psq, pk = pq(2, 0)
        for s4 in range(4):
            for k in range(8):
                op('pe', 'matmul', [('h', k), 'wba'], pk, out=psq[:, s4 * 16:(s4 + 1) * 16], lhsT=hc(k)[:, s4 * 128:(s4 + 1) * 128],
                   rhs=wba[:, j, k, :], start=(k == 0), stop=(k == 7))
        pba3 = psq[:, 0:64].rearrange("p (s c) -> p s c", c=16)
        op('act', 'activation', pk, [uk('ee')], out=ee3, in_=pba3[:, :, 0:8], func=AF.Exp, scale=-1.0)
        op('act', 'activation', [uk('ee')], [uk('ee')], out=ee3, in_=ee3, func=AF.Ln, bias=1.0)
        op('act', 'activation', [uk('ee')], [uk('beta')], out=beta3, in_=ee3, func=AF.Exp, scale=-1.0)
        op('dve', 'tensor_tensor', pk + ['dtb'], [uk('tg')], out=tg3, in0=pba3[:, :, 8:16],
           in1=bass.AP(dtb, j * 8, [[16, 128], [0, 4], [1, 8]]), op=ALU.add)
        op('act', 'activation', [uk('tg')], [uk('tg')], out=tg3, in_=tg3, func=AF.Exp)
        op('act', 'activation', [uk('tg')], [uk('tg')], out=tg3, in_=tg3, func=AF.Ln, bias=1.0)
        op('dve', 'tensor_tensor', [uk('tg'), 'nexpA'], [uk('g')], out=g3, in0=tg3,
           in1=bass.AP(nexpA, j * 8, [[16, 128], [0, 4], [1, 8]]), op=ALU.mult)

        def projconv(hd):
            par = hd % 2
            wv, wk = ring_next('dnin')
            w4 = wv.rearrange("p (f k c) -> p f k c", f=4, k=8)
            for fi in range(3):
                ps, pk = big()
                for k in range(8):
                    op('pe', 'matmul', [wk, ('h', k)], pk, out=ps[:], lhsT=w4[:, fi, k, :], rhs=hc(k), start=(k == 0), stop=(k == 7))
                hidx = (j * 8 + hd) * 3 + fi
                op('pool', 'tensor_copy', [('halo', hidx)], [uk('pre', fi)], out=pre3[:, fi, 0:3], in_=halo[:, hidx, 0:3])
                op('act', 'activation', pk, [uk('pre', fi)], out=pre3[:, fi, 3:515], in_=ps[:], func=AF.Copy)
                op('pool', 'tensor_copy', [uk('pre', fi)], [('halo', hidx)], out=halo[:, hidx, 0:3], in_=pre3[:, fi, 512:515])
                yield
                ch = fi * 8 + hd
                for tap in range(4):
                    wcol = cwt[:, j, tap * 24 + ch:tap * 24 + ch + 1]
                    if fi == 2:
                        if tap == 0:
                            op('pool', 'tensor_scalar', [uk('pre', 2), 'cwt'], [uk('cv', 2)], out=cv3[:, 2, :], in0=pre3[:, 2, 0:512],
                               scalar1=wcol, scalar2=0.0, op0=ALU.mult, op1=ALU.add)
                        else:
                            op('pool', 'tensor_scalar', [uk('pre', 2), 'cwt'], [uk('rt2')], out=rt2, in0=pre3[:, 2, tap:tap + 512],
                               scalar1=wcol, scalar2=0.0, op0=ALU.mult, op1=ALU.add)
                            op('pool', 'tensor_tensor', [uk('rt2'), uk('cv', 2)], [uk('cv', 2)], out=cv3[:, 2, :], in0=cv3[:, 2, :], in1=rt2, op=ALU.add)
                        continue
                    if tap == 0:
                        op('dve', 'tensor_scalar', [uk('pre', fi), 'cwt'], [uk('cv', fi)], out=cv3[:, fi, :], in0=pre3[:, fi, 0:512],
                           scalar1=wcol, scalar2=None, op0=ALU.mult)
                    else:
                        op('dve', 'scalar_tensor_tensor', [uk('pre', fi), 'cwt', uk('cv', fi)], [uk('cv', fi)], out=cv3[:, fi, :],
                           in0=pre3[:, fi, tap:tap + 512], scalar=wcol, op0=ALU.mult, in1=cv3[:, fi, :], op1=ALU.add)
                yield
                op('act', 'activation', [uk('cv', fi)], [uk('sg')], out=sg, in_=cv3[:, fi, :], func=AF.Exp, scale=-1.0)
                op('act', 'activation', [uk('sg')], [uk('sg')], out=sg, in_=sg, func=AF.Ln, bias=1.0)
                op('act', 'activation', [uk('sg')], [uk('sg')], out=sg, in_=sg, func=AF.Exp, scale=-1.0)
                if fi < 2:
                    op('pool', 'tensor_tensor', [uk('cv', fi), uk('sg')], [uk('sl', fi)], out=sl3[:, fi, :], in0=cv3[:, fi, :], in1=sg, op=ALU.mult)
                else:
                    op('pool', 'tensor_tensor', [uk('cv', 2), uk('sg')], [uk('vb', par)], out=vbb[par], in0=cv3[:, 2, :], in1=sg, op=ALU.mult)
                yield
            ps, pk = big()
            for k in range(8):
                op('pe', 'matmul', [wk, ('h', k)], pk, out=ps[:], lhsT=w4[:, 3, k, :], rhs=hc(k), start=(k == 0), stop=(k == 7))
            ring_release()
            op('act', 'activation', pk, [uk('cv', 0)], out=cv3[:, 0, :], in_=ps[:], func=AF.Copy)
            op('act', 'activation', pk, [uk('sg')], out=sg, in_=ps[:], func=AF.Exp, scale=-1.0)
            op('act', 'activation', [uk('sg')], [uk('sg')], out=sg, in_=sg, func=AF.Ln, bias=1.0)
            op('act', 'activation', [uk('sg')], [uk('sg')], out=sg, in_=sg, func=AF.Exp, scale=-1.0)
            op('pool', 'tensor_tensor', [uk('cv', 0), uk('sg')], [uk('zs', hd % 3)], out=zsb[hd % 3], in0=cv3[:, 0, :], in1=sg, op=ALU.mult)
            yield
            for fi in range(2):
                op('act', 'activation', [uk('sl', fi)], [uk('sq2')], out=sq2, in_=sl3[:, fi, :], func=AF.Square)
                ps, pk = big()
                op('pe', 'matmul', [uk('sq2'), 'onesb'], pk, out=ps[:], lhsT=onesb[:], rhs=sq2, start=True, stop=True)
                op('act', 'activation', pk, [uk('rt2')], out=rt2, in_=ps[:], func=AF.Ln, bias=EPS)
                bias = -0.5 * math.log(128.0) if fi == 0 else 0.0
                op('act', 'activation', [uk('rt2')], [uk('rs2')], out=rs2, in_=rt2, func=AF.Exp, scale=-0.5, bias=bias)
                dst = qnb[par] if fi == 0 else knb[par]
                op('pool', 'tensor_tensor', [uk('sl', fi), uk('rs2')], [uk('qn' if fi == 0 else 'kn', par)], out=dst, in0=sl3[:, fi, :], in1=rs2, op=ALU.mult)
                yield

        def local(hd, c):
            par = hd % 2
            cs = slice(c * 128, (c + 1) * 128)
            ls = lset[c]
            ho = hout[par][c]
            lk = lambda n: uk('l', c, n)
            hk = lambda n: uk('ho', par, c, n)
            bcol = beta3[:, c, hd:hd + 1]
            gcol = g3[:, c, hd:hd + 1]
            qn_, kn_, vb_ = qnb[par], knb[par], vbb[par]
            lb = 2 + c
            tpf, tk = pq(lb, 0)
            tp = tpf.bitcast(BF16)
            op('pe', 'transpose', [uk('kn', par), 'identb'], tk, out=tp[:, 0:128], in_=kn_[:, cs], identity=identb[:])
            op('pe', 'transpose', [uk('vb', par), 'identb'], tk, out=tp[:, 128:256], in_=vb_[:, cs], identity=identb[:])
            op('pool', 'tensor_scalar', ['Umask', uk('g')], [lk('GU')], out=ls['GU'], in0=Umask[:], scalar1=gcol, scalar2=None, op0=ALU.mult)
            Ep, ek = pq(lb, 1)
            op('pe', 'matmul', [lk('GU'), 'onesf'], ek, out=Ep, lhsT=ls['GU'], rhs=onesf[:], start=True, stop=False)
            op('pe', 'matmul', [lk('GU'), 'negonesf'], ek, out=Ep, lhsT=negonesf[:], rhs=ls['GU'], start=False, stop=True)
            Bp, bk_ = pq(lb, 2)
            op('pe', 'matmul', [lk('GU'), 'onesf'], bk_, out=Bp, lhsT=onesf[:], rhs=ls['GU'], start=True, stop=True)
            cols = ho['cols']
            op('act', 'activation', bk_, [hk('gl')], out=cols[:, 0:1], in_=Bp[:, 127:128], func=AF.Copy)
            op('act', 'activation', bk_, [hk('egl')], out=cols[:, 1:2], in_=Bp[:, 127:128], func=AF.Exp)
            op('act', 'activation', bk_, [lk('egb')], out=ls['egb'], in_=Bp, func=AF.Exp)
            op('act', 'activation', ek, [hk('edl')], out=cols[:, 2:3], in_=Ep[:, 127:128], func=AF.Exp, scale=-1.0)
            op('act', 'activation', ek + [hk('gl')], [hk('eg')], out=cols[:, 3:4], in_=Ep[:, 127:128], func=AF.Exp, bias=cols[:, 0:1], scale=1.0)
            op('dve', 'tensor_tensor', ek + ['neglo'], [lk('t1')], out=ls['t1'], in0=Ep, in1=neglo[:], op=ALU.add)
            op('dve', 'scalar_tensor_tensor', ek + ['negup'], [lk('t2')], out=ls['t2'], in0=Ep, scalar=-1.0, op0=ALU.mult, in1=negup[:], op1=ALU.add)
            op('act', 'activation', [lk('t1')], [lk('t1')], out=ls['t1'], in_=ls['t1'], func=AF.Exp)
            op('act', 'activation', [lk('t2')], [lk('t2')], out=ls['t2'], in_=ls['t2'], func=AF.Exp)
            op('dve', 'tensor_scalar', tk + [uk('beta'), hk('eg')], [lk('kbg')], out=ls['kbg'].bitcast(F32R), in0=tp[:, 0:128],
               scalar1=bcol, scalar2=cols[:, 3:4], op0=ALU.mult, op1=ALU.mult)
            op('dve', 'tensor_scalar', tk + [hk('edl')], [hk('kdec')], out=ho['kdec'], in0=tp[:, 0:128], scalar1=cols[:, 2:3], scalar2=None, op0=ALU.mult)
            op('dve', 'tensor_scalar', tk + [uk('beta')], [lk('bv')], out=ls['bv'].bitcast(F32R), in0=tp[:, 128:256], scalar1=bcol, scalar2=None, op0=ALU.mult)
            yield
            KKp, kkk = pq(lb, 0)
            op('pe', 'matmul', [uk('kn', par)], kkk, out=KKp, lhsT=kn_[:, cs], rhs=kn_[:, cs], start=True, stop=True)
            KQp, kqk = pq(lb, 1)
            op('pe', 'matmul', [uk('kn', par), uk('qn', par)], kqk, out=KQp, lhsT=kn_[:, cs], rhs=qn_[:, cs], start=True, stop=True)
            op('dve', 'scalar_tensor_tensor', kkk + [uk('beta'), lk('t1')], [lk('Q0')], out=ls['Q0'].bitcast(F32R), in0=KKp, scalar=bcol,
               op0=ALU.mult, in1=ls['t1'], op1=ALU.mult)
            op('dve', 'tensor_tensor', kqk + [lk('t2')], [hk('attnT')], out=ho['attnT'], in0=KQp, in1=ls['t2'], op=ALU.mult)
            op('pool', 'tensor_tensor', [uk('qn', par), lk('egb')], [hk('qdec')], out=ho['qdec'], in0=qn_[:, cs], in1=ls['egb'], op=ALU.mult)
            yield
            Btp, btk = pq(lb, 2)
            op('pe', 'transpose', [lk('Q0'), 'identf'], btk, out=Btp, in_=ls['Q0'], identity=identf[:])
            op('act', 'activation', btk, [lk('P0')], out=ls['P0'].bitcast(F32R), in_=Btp, func=AF.Copy)
            NM = dict(Qa='Q1', Pa='P1', Qb='Q2', Pb='P2', Ra='R0', Rb='R1', Ta='T0', Tb='T1')
            tl = lambda n: ls[NM[n]]
            tkk = lambda n: lk(NM[n])
            r32 = lambda n: tl(n).bitcast(F32R)
            M = lambda l: cmask[:, l * 128:(l + 1) * 128]
            op('dve', 'tensor_tensor', [lk('Q0'), 'cmask'], [tkk('Qa')], out=r32('Qa'), in0=ls['Q0'], in1=M(0), op=ALU.mult)
            op('dve', 'tensor_tensor', [lk('P0'), 'cmask'], [tkk('Pa')], out=r32('Pa'), in0=ls['P0'], in1=M(0), op=ALU.mult)
            op('dve', 'tensor_tensor', [tkk('Qa'), 'identf'], [tkk('Ta')], out=r32('Ta'), in0=identf[:], in1=tl('Qa'), op=ALU.subtract)
            op('dve', 'tensor_tensor', [tkk('Pa'), 'identf'], [tkk('Ra')], out=r32('Ra'), in0=identf[:], in1=tl('Pa'), op=ALU.subtract)
            yield
            Qc, Pc, Qn_, Pn_, Rc, Rn_, Tc, Tn_ = 'Qa', 'Pa', 'Qb', 'Pb', 'Ra', 'Rb', 'Ta', 'Tb'
            for lev in range(3):
                Pps, ppk = pq(lb, 0)
                op('pe', 'matmul', [tkk(Qc), tkk(Pc)], ppk, out=Pps, lhsT=r32(Qc), rhs=r32(Pc), start=True, stop=True)
                Qps, qpk = pq(lb, 1)
                op('pe', 'matmul', [tkk(Qc), tkk(Pc)], qpk, out=Qps, lhsT=r32(Pc), rhs=r32(Qc), start=True, stop=True)
                op('act', 'activation', qpk, [tkk(Qn_)], out=r32(Qn_), in_=Qps, func=AF.Copy)
                op('act', 'activation', ppk, [tkk(Pn_)], out=r32(Pn_), in_=Pps, func=AF.Copy)
                yield
                Rps, rpk = pq(lb, 2)
                op('pe', 'matmul', [tkk(Qn_), tkk(Rc)], rpk, out=Rps, lhsT=r32(Qn_), rhs=r32(Rc), start=True, stop=True)
                Tps, tpk = pq(lb, 3)
                op('pe', 'matmul', [tkk(Pn_), tkk(Tc)], tpk, out=Tps, lhsT=r32(Pn_), rhs=r32(Tc), start=True, stop=True)
                op('dve', 'tensor_tensor', rpk + [tkk(Rc)], [tkk(Rn_)], out=r32(Rn_), in0=Rps, in1=tl(Rc), op=ALU.add)
                op('dve', 'tensor_tensor', tpk + [tkk(Tc)], [tkk(Tn_)], out=r32(Tn_), in0=Tps, in1=tl(Tc), op=ALU.add)
                yield
                Qc, Qn_ = Qn_, Qc
                Pc, Pn_ = Pn_, Pc
                Rc, Rn_ = Rn_, Rc
                Tc, Tn_ = Tn_, Tc
            Xs, X2s = Qn_, Pn_
            for lev in range(1, 4):
                last = (lev == 3)
                Xp, xk = pq(lb, 0)
                op('pe', 'matmul', [lk('Q0'), tkk(Rc)], xk, out=Xp, lhsT=ls['Q0'].bitcast(F32R), rhs=r32(Rc), start=True, stop=True)
                op('dve', 'tensor_tensor', xk + ['cmask'], [tkk(Xs)], out=r32(Xs), in0=Xp, in1=M(lev), op=ALU.mult)
                if not last:
                    X2p, x2k = pq(lb, 1)
                    op('pe', 'matmul', [lk('P0'), tkk(Tc)], x2k, out=X2p, lhsT=ls['P0'].bitcast(F32R), rhs=r32(Tc), start=True, stop=True)
                    op('dve', 'tensor_tensor', x2k + ['cmask'], [tkk(X2s)], out=r32(X2s), in0=X2p, in1=M(lev), op=ALU.mult)
                yield
                Yrp, yrk = pq(lb, 2)
                op('pe', 'matmul', [tkk(Tc), tkk(Xs)], yrk, out=Yrp, lhsT=r32(Tc), rhs=r32(Xs), start=True, stop=True)
                if not last:
                    Ytp, ytk = pq(lb, 3)
                    op('pe', 'matmul', [tkk(Rc), tkk(X2s)], ytk, out=Ytp, lhsT=r32(Rc), rhs=r32(X2s), start=True, stop=True)
                op('dve', 'tensor_tensor', yrk + [tkk(Rc)], [tkk(Rn_)], out=r32(Rn_), in0=tl(Rc), in1=Yrp, op=ALU.subtract)
                if not last:
                    op('dve', 'tensor_tensor', ytk + [tkk(Tc)], [tkk(Tn_)], out=r32(Tn_), in0=tl(Tc), in1=Ytp, op=ALU.subtract)
                yield
                Rc, Rn_ = Rn_, Rc
                Tc, Tn_ = Tn_, Tc
            Rf = NM[Rc]
            ups, upk = pq(lb, 0)
            op('pe', 'matmul', [lk(Rf), lk('bv')], upk, out=ups, lhsT=ls[Rf].bitcast(F32R), rhs=ls['bv'].bitcast(F32R), start=True, stop=True)
            wps, wpk = pq(lb, 1)
            op('pe', 'matmul', [lk(Rf), lk('kbg')], wpk, out=wps, lhsT=ls['kbg'].bitcast(F32R), rhs=ls[Rf].bitcast(F32R), start=True, stop=True)
            op('act', 'activation', upk, [hk('u')], out=ho['u'], in_=ups, func=AF.Copy)
            op('act', 'activation', wpk, [hk('wT')], out=ho['wT'], in_=wps, func=AF.Copy)
            yield

        def seq(hd):
            par = hd % 2
            si = j * 8 + hd
            for c in range(4):
                cs = slice(c * 128, (c + 1) * 128)
                ho = hout[par][c]
                hk = lambda n, c=c: uk('ho', par, c, n)
                cols = ho['cols']
                vi = c % 2
                wsp, wsk = pq(6, 0)
                op('pe', 'matmul', [hk('wT'), ('Sb', si)], wsk, out=wsp, lhsT=ho['wT'], rhs=Sb[:, si, :], start=True, stop=True)
                op('dve', 'tensor_tensor', wsk + [hk('u')], [uk('vnew', vi)], out=vnewb[vi], in0=ho['u'], in1=wsp, op=ALU.subtract)
                yield
                ops_, opk = pq(7, 0)
                op('pe', 'matmul', [hk('qdec'), ('Sb', si)], opk, out=ops_, lhsT=ho['qdec'], rhs=Sb[:, si, :], start=True, stop=False)
                op('pe', 'matmul', [hk('attnT'), uk('vnew', vi)], opk, out=ops_, lhsT=ho['attnT'], rhs=vnewb[vi], start=False, stop=True)
                sup, suk = pq(6, 1)
                op('pe', 'matmul', [hk('kdec'), uk('vnew', vi)], suk, out=sup, lhsT=ho['kdec'], rhs=vnewb[vi], start=True, stop=True)
                op('dve', 'scalar_tensor_tensor', suk + [('Sf', si), hk('egl')], [('Sf', si)], out=Sf[:, si, :], in0=Sf[:, si, :], scalar=cols[:, 1:2],
                   op0=ALU.mult, in1=sup, op1=ALU.add)
                op('act', 'activation', [('Sf', si)], [('Sb', si)], out=Sb[:, si, :], in_=Sf[:, si, :], func=AF.Copy)
                sc = scol[:, vi * 4:vi * 4 + 4]
                op('act', 'activation', opk, [uk('junk'), uk('ssq', vi)], out=junk, in_=ops_, func=AF.Square, accum_out=sc[:, 0:1])
                op('act', 'activation', [uk('ssq', vi)], [uk('sln', vi)], out=sc[:, 1:2], in_=sc[:, 0:1], func=AF.Ln, bias=EPS, scale=1.0 / 128)
                op('act', 'activation', [uk('sln', vi)], [uk('rso', vi)], out=sc[:, 2:3], in_=sc[:, 1:2], func=AF.Exp, scale=-0.5)
                op('act', 'activation', opk + [uk('rso', vi)], [uk('on', vi)], out=onb[vi], in_=ops_, func=AF.Copy, scale=sc[:, 2:3])
                yield
                otf, otk = pq(7, 1)
                otp = otf.bitcast(BF16)
                op('pe', 'transpose', [uk('on', vi), 'identb'], otk, out=otp[:, 0:128], in_=onb[vi], identity=identb[:])
                op('dve', 'scalar_tensor_tensor', otk + ['dnnw', uk('zs', hd % 3)], [('mo', hd)], out=moc(hd)[:, cs], in0=otp[:, 0:128],
                   scalar=dnnw[:, j:j + 1], op0=ALU.mult, in1=zsb[hd % 3][:, cs], op1=ALU.mult)
                yield

        run_interleaved([projconv(0)])
        for hd in range(8):
            gens = [local(hd, c) for c in range(4)]
            if hd > 0:
                gens.append(seq(hd - 1))
            if hd < 7:
                gens.append(projconv(hd + 1))
            run_interleaved(gens)
        run_interleaved([seq(7)])
        for u in range(2):
            wv, wk = ring_next('dnout')
            w3 = wv.rearrange("p (k c) -> p k c", c=512)
            for dcl in range(4):
                dc = u * 4 + dcl
                ps, pk = big()
                for kk in range(8):
                    op('pe', 'matmul', [wk, ('mo', kk)], pk, out=ps[:], lhsT=w3[:, kk, dcl * 128:(dcl + 1) * 128], rhs=moc(kk), start=(kk == 0), stop=(kk == 7))
                op('dve', 'tensor_tensor', pk + [('xT', dc)], [('xT', dc)], out=xTc(dc), in0=ps[:], in1=xTc(dc), op=ALU.add)
            ring_release()

    BM = uf(0, 4096)
    BM4 = BM.rearrange("p (a two k) -> p a two k", two=2, k=256)
    qT3 = ubf(8192, 4096).rearrange("p (c t) -> p c t", t=512)
    NQS = 2

    def mk_qs(base):
        return dict(s=uf(base, 1024).rearrange("p (s k) -> p s k", k=256),
                    p=ubf(base + 1024, 1024).rearrange("p (s k) -> p s k", k=256),
                    pT=ubf(base + 1536, 1024),
                    cols=uf(base + 2048, 32))

    qsets = [mk_qs(10240 + i * 2080) for i in range(NQS)]
    otok = [ubf(14400 + n * 512, 1024) for n in range(4)]
    sink4 = sinkb[:].rearrange("p j (a two) -> p j a two", two=2)

    def swa_layer(L, t):
        j = L // 2
        phase_barrier()
        dma('pool', BM, bmfull_d.ap(), ['bmfull'], [uk('BM')], 'bm')
        rms_norm_to_h(nmix, 'nmix', L * 8)
        for u in range(2):
            wv, wk = ring_next('swq')
            w3 = wv.rearrange("p (k c) -> p k c", c=512)
            for ccl in range(4):
                cc = u * 4 + ccl
                ps, pk = big()
                for k in range(8):
                    op('pe', 'matmul', [wk, ('h', k)], pk, out=ps[:], lhsT=w3[:, k, ccl * 128:(ccl + 1) * 128], rhs=hc(k), start=(k == 0), stop=(k == 7))
                op('act', 'activation', pk + ['bq8'], [uk('qT', cc)], out=qT3[:, cc, :], in_=ps[:], func=AF.Identity, bias=bq8[:, j, cc:cc + 1], scale=0.125)
            ring_release()
        wv, wk = ring_next('swkv')
        w3 = wv.rearrange("p (k c) -> p k c", c=512)
        for kv in range(2):
            ps, pk = big()
            for k in range(8):
                op('pe', 'matmul', [wk, ('h', k)], pk, out=ps[:], lhsT=w3[:, k, kv * 128:(kv + 1) * 128], rhs=hc(k), start=(k == 0), stop=(k == 7))
            op('act', 'activation', pk + ['bk2'], [('kbuf', j)], out=kbuf[:, j, kv, 128:640], in_=ps[:], func=AF.Identity, bias=bk2[:, j, kv:kv + 1], scale=1.0)
        for s4 in range(4):
            psb, pk = big()
            psq = psb[:, 0:128]
            for k in range(8):
                op('pe', 'matmul', [wk, ('h', k)], pk, out=psq, lhsT=hc(k)[:, s4 * 128:(s4 + 1) * 128], rhs=w3[:, k, 256:384], start=(k == 0), stop=(k == 7))
            op('dve', 'tensor_tensor', pk + ['bvb'], [('vbuf', j)], out=vbuf[:, j, 1 + s4, :], in0=psq, in1=bvb[:, j, :], op=ALU.add)
        ring_release()

        qcount = [0]

        def quad(n, qd):
            qs = qsets[qcount[0] % NQS]
            qi_ = qcount[0] % NQS
            qcount[0] += 1
            first = False
            seq_start_blk = (t == 0 and n == 0)
            W0 = 128 if first else 0
            KW = 256 - W0
            kvh = qd // 2
            qk = lambda *nm: uk('qs', qi_, *nm)
            s_, p_, pT_, cols = qs['s'], qs['p'], qs['pT'], qs['cols']
            banks = [(0, 1), (2, 3)][qi_]
            for two in range(2):
                b = banks[two]
                for a in range(2):
                    cc = 2 * qd + a
                    op('pe', 'matmul', [uk('qT', cc), ('kbuf', j)], PQ(b), out=pb[b][:, a * 256 + W0:(a + 1) * 256],
                       lhsT=qT3[two * 64:(two + 1) * 64, cc, n * 128:(n + 1) * 128],
                       rhs=kbuf[two * 64:(two + 1) * 64, j, kvh, n * 128 + W0:n * 128 + 256], start=True, stop=True)
                op('dve', 'tensor_tensor', PQ(b) + [uk('BM')], [qk('s')], out=s_[:, 2 * two:2 * two + 2, W0:256],
                   in0=pb[b][:].rearrange("p (a k) -> p a k", k=256)[:, :, W0:256], in1=BM4[:, 2 * qd:2 * qd + 2, two, W0:256], op=ALU.add)
                if seq_start_blk:
                    op('dve', 'tensor_tensor', [qk('s'), 'fmask'], [qk('s')], out=s_[:, 2 * two:2 * two + 2, 0:128], in0=s_[:, 2 * two:2 * two + 2, 0:128],
                       in1=bass.AP(fmask, 0, [[128, 128], [0, 2], [1, 128]]), op=ALU.add)
            yield
            rmax = cols[:, 0:4]; mcol = cols[:, 4:8]; negm = cols[:, 8:12]; rsum = cols[:, 12:16]
            tmp4 = cols[:, 16:20]; esk = cols[:, 20:24]; den = cols[:, 24:28]; rinv = cols[:, 28:32]
            sk4 = sink4[:, j, 2 * qd:2 * qd + 2, :].rearrange("p a two -> p two a")
            op('dve', 'tensor_reduce', [qk('s')], [qk('rmax')], out=rmax, in_=s_[:, :, W0:256], axis=AX.X, op=ALU.max)
            op('dve', 'tensor_tensor', [qk('rmax'), 'sinkb'], [qk('m')], out=mcol.rearrange("p (two a) -> p two a", a=2),
               in0=rmax.rearrange("p (two a) -> p two a", a=2), in1=sk4, op=ALU.max)
            op('dve', 'tensor_scalar', [qk('m')], [qk('negm')], out=negm, in0=mcol, scalar1=-1.0, scalar2=None, op0=ALU.mult)
            for sl_ in range(4):
                op('act', 'activation', [qk('s'), qk('negm')], [qk('p', sl_), qk('rsum', sl_)], out=p_[:, sl_, W0:256], in_=s_[:, sl_, W0:256],
                   func=AF.Exp, bias=negm[:, sl_:sl_ + 1], scale=1.0, accum_out=rsum[:, sl_:sl_ + 1])
            op('dve', 'tensor_tensor', [qk('negm'), 'sinkb'], [qk('tmp4')], out=tmp4.rearrange("p (two a) -> p two a", a=2),
               in0=negm.rearrange("p (two a) -> p two a", a=2), in1=sk4, op=ALU.add)
            op('act', 'activation', [qk('tmp4')], [qk('esk')], out=esk, in_=tmp4, func=AF.Exp)
            op('dve', 'tensor_tensor', [qk('esk')] + [qk('rsum', i) for i in range(4)], [qk('den')], out=den, in0=rsum, in1=esk, op=ALU.add)
            op('dve', 'reciprocal', [qk('den')], [qk('rinv')], out=rinv, in_=den)
            yield
            tb = 4 + qi_
            ptp = pb[tb][:].bitcast(BF16)
            halves = [1] if first else [0, 1]
            for sl_ in range(4):
                for hf in halves:
                    op('pe', 'transpose', [qk('p', sl_), 'identb'], PQ(tb), out=ptp[:, (sl_ * 2 + hf) * 128:(sl_ * 2 + hf + 1) * 128],
                       in_=p_[:, sl_, hf * 128:(hf + 1) * 128], identity=identb[:])
            if first:
                for sl_ in range(4):
                    op('act', 'activation', PQ(tb), [qk('pT')], out=pT_[:, (sl_ * 2 + 1) * 128:(sl_ * 2 + 2) * 128],
                       in_=ptp[:, (sl_ * 2 + 1) * 128:(sl_ * 2 + 2) * 128], func=AF.Copy)
            else:
                op('act', 'activation', PQ(tb), [qk('pT')], out=pT_, in_=ptp, func=AF.Copy)
            yield
            pvk = PQ(6 + qi_)
            pv = pb[6 + qi_][:, 0:256]
            for sl_ in range(4):
                for hf in halves:
                    op('pe', 'matmul', [qk('pT'), ('vbuf', j)], pvk, out=pv[:, sl_ * 64:(sl_ + 1) * 64],
                       lhsT=pT_[:, (sl_ * 2 + hf) * 128:(sl_ * 2 + hf + 1) * 128], rhs=vbuf[:, j, n + hf, kvh * 64:(kvh + 1) * 64],
                       start=(hf == halves[0]), stop=(hf == 1))
            o4 = otok[n].rearrange("p (a two d) -> p two a d", two=2, d=64)[:, :, 2 * qd:2 * qd + 2, :]
            rinv_b = bass.AP(uni, rinv.offset, [[uni.shape[1], 128], [2, 2], [1, 2], [0, 64]])
            op('dve', 'tensor_tensor', pvk + [qk('rinv')], [uk('otok', n, qd)], out=o4, in0=pv.rearrange("p (two a d) -> p two a d", two=2, d=64),
               in1=rinv_b, op=ALU.mult)
            yield

        for n in range(4):
            gl = [quad(n, qd) for qd in range(4)]
            run_interleaved(gl[0:2])
            run_interleaved(gl[2:4])
            otp = pb[4][:].bitcast(BF16)
            for cc in range(8):
                op('pe', 'transpose', [uk('otok', n, cc // 2), 'identb'], PQ(4), out=otp[:, cc * 128:(cc + 1) * 128],
                   in_=otok[n][:, cc * 128:(cc + 1) * 128], identity=identb[:])
            op('act', 'activation', PQ(4), [('mo', cc) for cc in range(8)], out=mo[:].rearrange("p (c t) -> p c t", t=512)[:, :, n * 128:(n + 1) * 128],
               in_=otp.rearrange("p (c t) -> p c t", t=128), func=AF.Copy)
        op('pool', 'tensor_copy', [('kbuf', j)], [('kbuf', j)], out=kbuf[:, j, :, 0:128], in_=kbuf[:, j, :, 512:640])
        op('pool', 'tensor_copy', [('vbuf', j)], [('vbuf', j)], out=vbuf[:, j, 0, :], in_=vbuf[:, j, 4, :])
        for u in range(2):
            wv, wk = ring_next('swout')
            w3 = wv.rearrange("p (k c) -> p k c", c=512)
            for dcl in range(4):
                dc = u * 4 + dcl
                ps, pk = big()
                for kk in range(8):
                    op('pe', 'matmul', [wk, ('mo', kk)], pk, out=ps[:], lhsT=w3[:, kk, dcl * 128:(dcl + 1) * 128], rhs=moc(kk), start=(kk == 0), stop=(kk == 7))
                op('dve', 'scalar_tensor_tensor', pk + [('xT', dc), 'bo'], [('xT', dc)], out=xTc(dc), in0=ps[:], scalar=bo[:, j, dc:dc + 1],
                   op0=ALU.add, in1=xTc(dc), op1=ALU.add)
            ring_release()

    phase_barrier()
    for t in range(NT):
        load_x_tile(t)
        for L in range(n_layers):
            if dn and L % 2 == 0:
                dn_layer(L, t)
            if swa and L % 2 == 1:
                swa_layer(L, t)
            mlp(L)
        store_out_tile(t)
    dma('pool', stS_o.ap(), Sf[:].rearrange("p a b -> p (a b)"), [('Sf', i) for i in range(16)], ['so0'], 'so0')
    dma('pool', stH_o.ap().rearrange("p (a b) -> p a b", b=4), halo[:], [('halo', i) for i in range(48)], ['so1'], 'so1')
    for jj in range(2):
        dma('pool', stK_o.ap().rearrange("p (j k c) -> p j k c", j=2, k=2)[:, jj, :, :], kbuf[:, jj, :, 0:128], [('kbuf', jj)], ['so2%d' % jj], 'so2%d' % jj)
        dma('pool', stV_o.ap().rearrange("p (j c) -> p j c", j=2)[:, jj, :], vbuf[:, jj, 0, :], [('vbuf', jj)], ['so3%d' % jj], 'so3%d' % jj)
    S_.add('pool', lambda e: None, [('out', 0), ('out', 1), 'so0', 'so1', 'so20', 'so21', 'so30', 'so31'], [], reg=False)
    assert ring['used'] == total_units, (ring['used'], total_units)
    S_.emit(nc, st)
    st.close()
    return nc, S_


_CACHE = {}
SL = 8192


def kernel(**inputs):
    import ml_dtypes
    x = np.asarray(inputs['x'], np.float32)
    B, S, _ = x.shape
    sl = min(SL, S)
    if sl not in _CACHE:
        _CACHE[sl] = build_program(sl)
    nc, _ = _CACHE[sl]
    shared = {}
    for k, v in inputs.items():
        if k == 'x':
            continue
        a = np.ascontiguousarray(np.asarray(v, np.float32))
        if k == 'norm_final':
            a = a.reshape(1, D)
        shared[k] = a
    shared.update(host_constants())
    ncore = B
    state = [dict(st_S_in=np.zeros((128, 2048), np.float32), st_halo_in=np.zeros((128, 192), np.float32),
                  st_k_in=np.zeros((128, 512), ml_dtypes.bfloat16), st_vv_in=np.zeros((128, 256), ml_dtypes.bfloat16))
             for _ in range(ncore)]
    out = np.empty((B, S, D), np.float32)
    for li in range(S // sl):
        in_maps = []
        for c in range(ncore):
            m = dict(shared)
            m['x'] = np.ascontiguousarray(x[c, li * sl:(li + 1) * sl])
            m['first_mask'] = np.full((128, 128), NEG if li == 0 else 0.0, np.float32)
            m.update(state[c])
            in_maps.append(m)
        res = run_bass_kernel_spmd(nc, in_maps, core_ids=list(range(ncore)))
        for c in range(ncore):
            r = res.results[c]
            out[c, li * sl:(li + 1) * sl] = np.asarray(r['out'], np.float32)
            state[c] = dict(st_S_in=np.asarray(r['st_S_out']), st_halo_in=np.asarray(r['st_halo_out']),
                            st_k_in=np.asarray(r['st_k_out']), st_vv_in=np.asarray(r['st_vv_out']))
    return out
```
